# Optimizing a Trainium2 kernel written in Bass

```python
import jax, jax.numpy as jnp
from jax import lax
import numpy as np

D_MODEL = 1024
BATCH = 2
SEQ = 16384
DEPTH = 4

D_PLE = 256
D_FF = 4 * D_MODEL
HEAD_DIM = 64
D_A = 3 * D_MODEL // 8
D_B = 3 * D_MODEL // 8
D_C = D_MODEL - D_A - D_B
N_HEADS_A = D_A // HEAD_DIM
N_HEADS_B = D_B // HEAD_DIM
POOL_WINDOWS = (2, 4, 8, 16)
N_POOL_GROUPS = len(POOL_WINDOWS)
D_POOL_GROUP = D_C // N_POOL_GROUPS
CHUNK = 128
CONV_WIDTH = 3
D_IN = 2 * D_A + 3 * D_B + D_C
SPLITS = (D_A, 2 * D_A, 2 * D_A + D_B, 2 * D_A + 2 * D_B, 2 * D_A + 3 * D_B)
RMS_EPS = 1e-6
LN_EPS = 1e-5

kernel_name = "hybrid_sgu_conv_pool_trunk"


def rms_norm(x, g):
    xf = x.astype(jnp.float32)
    y = xf * lax.rsqrt(jnp.mean(xf * xf, axis=-1, keepdims=True) + RMS_EPS)
    return (y * g.astype(jnp.float32)).astype(x.dtype)


def spatial_gating(u, v, w_s, b_s, ln_g, ln_b):
    bsz, t, _ = u.shape
    n = t // CHUNK
    u = jax.nn.gelu(u, approximate=False)
    v = jax.nn.gelu(v, approximate=False)
    vf = v.reshape(bsz, n, CHUNK, N_HEADS_A, HEAD_DIM).astype(jnp.float32)
    mu = jnp.mean(vf, axis=-1, keepdims=True)
    var = jnp.mean(jnp.square(vf - mu), axis=-1, keepdims=True)
    vn = ((vf - mu) * lax.rsqrt(var + LN_EPS)
          * ln_g.reshape(N_HEADS_A, HEAD_DIM).astype(jnp.float32)
          + ln_b.reshape(N_HEADS_A, HEAD_DIM).astype(jnp.float32)).astype(u.dtype)
    mask = jnp.tril(jnp.ones((CHUNK, CHUNK), dtype=bool))
    w = jnp.where(mask[None], w_s, jnp.zeros((), w_s.dtype)).astype(u.dtype)
    mixed = jnp.einsum('hts,bnshd->bnthd', w, vn) + b_s.T.astype(u.dtype)[:, :, None]
    return u * mixed.reshape(bsz, t, D_A)


def short_conv(z, gate_b, gate_c, conv_w):
    h = gate_c * z
    y = lax.conv_general_dilated(
        h, conv_w.astype(h.dtype)[:, None, :], window_strides=(1,),
        padding=[(CONV_WIDTH - 1, 0)],
        dimension_numbers=('NWC', 'WIO', 'NWC'), feature_group_count=D_B)
    return gate_b * y


def multiscale_pool(z, w_pool, pool_scale):
    bsz, t, _ = z.shape
    zf = z.astype(jnp.float32)
    cs = jnp.cumsum(zf, axis=1)
    pos_count = jnp.arange(1, t + 1, dtype=jnp.float32)
    outs = []
    for g, win in enumerate(POOL_WINDOWS):
        sl = slice(g * D_POOL_GROUP, (g + 1) * D_POOL_GROUP)
        c = cs[..., sl]
        lag = jnp.pad(c, ((0, 0), (win, 0), (0, 0)))[:, :t]
        mean = (c - lag) / jnp.minimum(pos_count, float(win))[None, :, None]
        outs.append(mean - zf[..., sl])
    pooled = jnp.stack(outs, axis=2).astype(z.dtype)
    y = jnp.einsum('btgc,gcd->btgd', pooled, w_pool)
    return y.reshape(bsz, t, D_C) * pool_scale


def setup_inputs(seed: int = 0) -> dict:
    key = jax.random.key(seed)
    ks = jax.random.split(key, 20)
    f32 = jnp.float32
    nrm = lambda k, shape, scale: jax.random.normal(k, shape, f32) * scale
    return {
        "x": nrm(ks[0], (BATCH, SEQ, D_MODEL), 1.0),
        "p": nrm(ks[1], (DEPTH, BATCH, SEQ, D_PLE), 1.0),
        "norm_mix_g": 1.0 + nrm(ks[2], (DEPTH, D_MODEL), 0.05),
        "w_in": nrm(ks[3], (DEPTH, D_MODEL, D_IN), D_MODEL ** -0.5),
        "sgu_w": nrm(ks[4], (DEPTH, N_HEADS_A, CHUNK, CHUNK), CHUNK ** -0.5),
        "sgu_b": 1.0 + nrm(ks[5], (DEPTH, N_HEADS_A, CHUNK), 0.1),
        "sgu_ln_g": 1.0 + nrm(ks[6], (DEPTH, D_A), 0.05),
        "sgu_ln_b": nrm(ks[7], (DEPTH, D_A), 0.02),
        "conv_w": nrm(ks[8], (DEPTH, CONV_WIDTH, D_B), CONV_WIDTH ** -0.5),
        "pool_w": nrm(ks[9], (DEPTH, N_POOL_GROUPS, D_POOL_GROUP, D_POOL_GROUP), D_POOL_GROUP ** -0.5),
        "pool_scale": 1.0 + nrm(ks[10], (DEPTH, D_C), 0.1),
        "w_out": nrm(ks[11], (DEPTH, D_MODEL, D_MODEL), D_MODEL ** -0.5),
        "norm_ff_g": 1.0 + nrm(ks[12], (DEPTH, D_MODEL), 0.05),
        "w_ff1": nrm(ks[13], (DEPTH, D_MODEL, D_FF), D_MODEL ** -0.5),
        "w_ff2": nrm(ks[14], (DEPTH, D_FF, D_MODEL), D_FF ** -0.5),
        "norm_ple_g": 1.0 + nrm(ks[15], (DEPTH, D_MODEL), 0.05),
        "w_ple_gate": nrm(ks[16], (DEPTH, D_MODEL, D_MODEL), D_MODEL ** -0.5),
        "w_ple_proj": nrm(ks[17], (DEPTH, D_PLE, D_MODEL), D_PLE ** -0.5),
        "final_g": 1.0 + nrm(ks[18], (D_MODEL,), 0.05),
    }


def reference(x, p, norm_mix_g, w_in, sgu_w, sgu_b, sgu_ln_g, sgu_ln_b, conv_w,
              pool_w, pool_scale, w_out, norm_ff_g, w_ff1, w_ff2, norm_ple_g,
              w_ple_gate, w_ple_proj, final_g):
    for i in range(DEPTH):
        h = rms_norm(x, norm_mix_g[i])
        proj = h @ w_in[i]
        u_a, v_a, z_b, g_b, g_c, z_c = jnp.split(proj, SPLITS, axis=-1)
        y_a = spatial_gating(u_a, v_a, sgu_w[i], sgu_b[i], sgu_ln_g[i], sgu_ln_b[i])
        y_b = short_conv(z_b, g_b, g_c, conv_w[i])
        y_c = multiscale_pool(z_c, pool_w[i], pool_scale[i])
        x = x + jnp.concatenate([y_a, y_b, y_c], axis=-1) @ w_out[i]
        h = rms_norm(x, norm_ff_g[i])
        x = x + jnp.square(jax.nn.relu(h @ w_ff1[i])) @ w_ff2[i]
        gate = jax.nn.sigmoid(rms_norm(x, norm_ple_g[i]) @ w_ple_gate[i])
        x = x + (p[i] @ w_ple_proj[i]) * gate
    return rms_norm(x, final_g)
```

```python
import numpy as np
import concourse.bass as bass
import concourse.mybir as mybir
from concourse.bass_utils import run_bass_kernel_spmd

F32 = mybir.dt.float32
BF16 = mybir.dt.bfloat16
AF = mybir.ActivationFunctionType
ALU = mybir.AluOpType
AX = mybir.AxisListType

D = 1024
KC = 8
DFF = 4096
DPLE = 256
NHEAD = 6
RMS_EPS = 1e-6
LN_EPS = 1e-5
HALO = 2
NCORES = 8

CFG = dict(L=4, SEGC=32, TILES=None, NBUF_W=4)

PIECES = []


def _mk_pieces():
    off = 0

    def add(name, ln):
        nonlocal off
        PIECES.append((name, off, ln))
        off += ln
    add("U", 3072)
    add("V", 3072)
    add("POOL", 2048)
    for c in range(3):
        add("CONV%d" % c, 3072)
    for i in range(2):
        add("OUT%d" % i, 4096)
    for i in range(8):
        add("FF1_%d" % i, 4096)
    for i in range(8):
        add("FF2_%d" % i, 4096)
    add("PLE", 2048)
    for i in range(2):
        add("GATE%d" % i, 4096)
    return off


E_W = _mk_pieces()
NPIECE = len(PIECES)
WSLOT = 4096
PLE_IDX = [i for i, pc in enumerate(PIECES) if pc[0] == "PLE"][0]
RING = [i for i in range(NPIECE) if i != PLE_IDX]
NRING = len(RING)

C_GMIX, C_GFF, C_GPLE, C_GFIN, C_CW, C_PSC, C_IW, C_CORR, C_LNG, C_LNB = 0, 8, 16, 24, 32, 41, 43, 45, 77, 461
S_SMALL = 845
S_BIG = 1792


def _chunk_block(W, col0, ncol=128):
    k = W.shape[0] // 128
    blk = W[:, col0:col0 + ncol].reshape(k, 128, ncol)
    return np.ascontiguousarray(blk.transpose(1, 0, 2)).reshape(128, k * ncol)


def _pack_weights(inp, L):
    wpk = np.zeros((L, 128, E_W), np.float32)
    for l in range(L):
        w_in = np.asarray(inp["w_in"][l])
        w_out = np.asarray(inp["w_out"][l])
        w1 = np.asarray(inp["w_ff1"][l])
        w2 = np.asarray(inp["w_ff2"][l])
        wg = np.asarray(inp["w_ple_gate"][l])
        wp = np.asarray(inp["w_ple_proj"][l])
        parts = []
        parts += [_chunk_block(w_in, c * 128) for c in range(3)]
        parts += [_chunk_block(w_in, 384, 384)]
        parts += [_chunk_block(w_in, 1920), _chunk_block(w_in, 2048)]
        for c in range(3):
            parts += [_chunk_block(w_in, 768 + c * 128), _chunk_block(w_in, 1536 + c * 128),
                      _chunk_block(w_in, 1152 + c * 128)]
        parts += [_chunk_block(w_out, m * 128) for m in range(8)]
        parts += [_chunk_block(w1, j * 128) for j in range(32)]
        parts += [_chunk_block(w2, m * 128) for m in range(8)]
        parts += [_chunk_block(wp, m * 128) for m in range(8)]
        parts += [_chunk_block(wg, m * 128) for m in range(8)]
        row = np.concatenate(parts, axis=1)
        assert row.shape == (128, E_W), row.shape
        wpk[l] = row
    return wpk


def _pack_small(inp, L, first_seg):
    spk = np.zeros((L, 128, S_SMALL), np.float32)
    for l in range(L):
        s = spk[l]
        s[:, C_GMIX:C_GMIX + 8] = np.asarray(inp["norm_mix_g"][l]).reshape(8, 128).T
        s[:, C_GFF:C_GFF + 8] = np.asarray(inp["norm_ff_g"][l]).reshape(8, 128).T
        s[:, C_GPLE:C_GPLE + 8] = np.asarray(inp["norm_ple_g"][l]).reshape(8, 128).T
        s[:, C_GFIN:C_GFIN + 8] = np.asarray(inp["final_g"]).reshape(8, 128).T
        cw = np.asarray(inp["conv_w"][l])
        for c in range(3):
            for k in range(3):
                s[:, C_CW + c * 3 + k] = cw[k, c * 128:(c + 1) * 128]
        s[:, C_PSC:C_PSC + 2] = np.asarray(inp["pool_scale"][l]).reshape(2, 128).T
        wins = np.array([[2, 4], [8, 16]], np.float32)
        for cc in range(2):
            for hf in range(2):
                win = wins[cc, hf]
                s[hf * 64:(hf + 1) * 64, C_IW + cc] = 1.0 / win
                for i in range(16):
                    v = win / min(i + 1.0, win) if first_seg else 1.0
                    s[hf * 64:(hf + 1) * 64, C_CORR + cc * 16 + i] = v
        s[:, C_LNG:C_LNG + 384] = np.asarray(inp["sgu_ln_g"][l])[None, :]
        s[:, C_LNB:C_LNB + 384] = np.asarray(inp["sgu_ln_b"][l])[None, :]
    return spk


def _pack_big(inp, L):
    bpk = np.zeros((L, 128, S_BIG), np.float32)
    for l in range(L):
        w = np.asarray(inp["sgu_w"][l])
        bpk[l, :, 0:768] = np.ascontiguousarray(w.transpose(2, 0, 1)).reshape(128, 768)
        bs = np.asarray(inp["sgu_b"][l]).reshape(768)
        bpk[l, 0, 768:1536] = bs
        bpk[l, 32, 768:1536] = bs
        wpool = np.asarray(inp["pool_w"][l])
        for cc in range(2):
            blk = np.zeros((128, 128), np.float32)
            blk[0:64, 0:64] = wpool[2 * cc]
            blk[64:128, 64:128] = wpool[2 * cc + 1]
            bpk[l, :, 1536 + cc * 128:1536 + (cc + 1) * 128] = blk
    return bpk


class Buf:
    __slots__ = ("name", "w", "r")

    def __init__(self, name):
        self.name = name
        self.w = None
        self.r = {}


class Eng:
    def __init__(self, nc, name, h, self_sync):
        self.name = name
        self.h = h
        self.sem = nc.alloc_semaphore("s_" + name)
        self.cnt = 0
        self.seen = {}
        self.self_sync = self_sync
        self.pend_r = []
        self.pend_w = []


class Slot:
    def __init__(self, nc, name):
        self.name = name
        self.sem = nc.alloc_semaphore("d_" + name)
        self.cnt = 0


class TR:
    def __init__(self, nc):
        self.nc = nc
        self.pe = Eng(nc, "pe", nc.tensor, False)
        self.act = Eng(nc, "act", nc.scalar, True)
        self.dve = Eng(nc, "dve", nc.vector, True)
        self.pool = Eng(nc, "pool", nc.gpsimd, True)
        self.sp = Eng(nc, "sp", nc.sync, False)

    @staticmethod
    def _deps(reads, writes):
        need = {}
        for b in reads:
            if b.w is not None:
                o, c = b.w
                if need.get(o, 0) < c:
                    need[o] = c
        for b in writes:
            if b.w is not None:
                o, c = b.w
                if need.get(o, 0) < c:
                    need[o] = c
            for o, c in b.r.items():
                if need.get(o, 0) < c:
                    need[o] = c
        return need

    @staticmethod
    def _wait(eng, need):
        for o, c in need.items():
            if o is eng and not eng.self_sync:
                continue
            if eng.seen.get(o, 0) >= c:
                continue
            eng.h.wait_ge(o.sem, c)
            eng.seen[o] = c

    def op(self, eng, fn, reads=(), writes=()):
        self._wait(eng, self._deps(reads, writes))
        ins = fn()
        eng.cnt += 1
        ins.then_inc(eng.sem, 1)
        for b in reads:
            b.r[eng] = eng.cnt
        for b in writes:
            b.w = (eng, eng.cnt)
            b.r = {}
        return ins

    def mm(self, fn, reads=(), writes=(), inc=False):
        eng = self.pe
        self._wait(eng, self._deps(reads, writes))
        ins = fn()
        eng.pend_r.extend(reads)
        eng.pend_w.extend(writes)
        if inc:
            eng.cnt += 1
            ins.then_inc(eng.sem, 1)
            for b in eng.pend_r:
                b.r[eng] = eng.cnt
            for b in eng.pend_w:
                b.w = (eng, eng.cnt)
                b.r = {}
            eng.pend_r = []
            eng.pend_w = []
        return ins

    def dma(self, q, slot, fn, reads=(), writes=()):
        self._wait(q, self._deps(reads, writes))
        ins = fn()
        slot.cnt += 16
        ins.then_inc(slot.sem, 16)
        for b in reads:
            b.r[slot] = slot.cnt
        for b in writes:
            b.w = (slot, slot.cnt)
            b.r = {}
        return ins


def build_nc(L, NCH, tiles, nbuf_w=3):
    assert sum(tiles) == NCH and max(tiles) <= 4 and len(set(tiles[i] for i in range(0, len(tiles) - len(tiles) % 2))) <= 1
    NTOK = NCH * 128
    NMAIN = (NCH - HALO) * 128
    nc = bass.Bass("TRN2", target_bir_lowering=False)
    tr = TR(nc)
    PE, ACT, DVE, POOL, SP = tr.pe, tr.act, tr.dve, tr.pool, tr.sp

    xT = nc.dram_tensor("xT", [KC, 128, NTOK], F32, kind="ExternalInput").ap()
    pT = nc.dram_tensor("pT", [L, 2, 128, NTOK], F32, kind="ExternalInput").ap()
    wpk = nc.dram_tensor("wpk", [L, 128, E_W], F32, kind="ExternalInput").ap()
    spk = nc.dram_tensor("spk", [L, 128, S_SMALL], F32, kind="ExternalInput").ap()
    bpk = nc.dram_tensor("bpk", [L, 128, S_BIG], F32, kind="ExternalInput").ap()
    outT = nc.dram_tensor("outT", [KC, 128, NMAIN], F32, kind="ExternalOutput").ap()
    wbf = nc.dram_tensor("wbf", [L, 128, E_W], BF16, kind="Internal").ap()

    def sb(name, shape, dt):
        return nc.alloc_sbuf_tensor(name, shape, dt).ap()

    NXB = 4
    TM = max(tiles) * 128
    NHS = 2
    x_sb = [sb("x%d" % i, [128, KC, TM], F32) for i in range(NXB)]
    x_buf = [[Buf("x%d_%d" % (i, k)) for k in range(KC)] for i in range(NXB)]

    class Half:
        pass

    halves = []
    for hs in range(NHS):
        H = Half()
        H.hs = hs
        H.h_sb = sb("h%d" % hs, [128, KC, TM], BF16)
        H.h_buf = [Buf("h%d_%d" % (hs, k)) for k in range(KC)]
        H.hid_sb = sb("hid%d" % hs, [128, 32, TM], BF16)
        H.hid_buf = [Buf("hid%d_%d" % (hs, k)) for k in range(32)]
        H.ycat_sb = H.hid_sb[:, 0:KC, :]
        H.ycat_buf = H.hid_buf[0:KC]
        H.u_sb = sb("u%d" % hs, [128, 3, TM], F32)
        H.u_buf = [Buf("u%d_%d" % (hs, k)) for k in range(3)]
        H.p_sb = sb("pbf%d" % hs, [128, 2, TM], BF16)
        H.p_buf = Buf("pbf%d" % hs)
        H.vn_sb = [sb("vn%d_%d" % (hs, i), [128, 384], BF16) for i in range(max(tiles))]
        H.vn_buf = [Buf("vn%d_%d" % (hs, i)) for i in range(max(tiles))]
        H.sq_sb = [sb("sq%d_%d" % (hs, i), [128, TM], BF16) for i in range(KC)]
        H.sq_buf = [Buf("sq%d_%d" % (hs, i)) for i in range(KC)]
        H.pl_sb = [sb("pl%d_%d" % (hs, i), [128, TM], BF16) for i in range(2)]
        H.pl_buf = [Buf("pl%d_%d" % (hs, i)) for i in range(2)]
        H.sl_p = Slot(nc, "p%d" % hs)
        halves.append(H)
    cur = Half()
    wple_sb = sb("wple", [128, 2048], BF16)
    wple_buf = Buf("wple")
    w_sb = [sb("w%d" % i, [128, WSLOT], BF16) for i in range(nbuf_w)]
    w_buf = [Buf("w%d" % i) for i in range(nbuf_w)]
    small_sb = sb("small", [128, L, S_SMALL], F32)
    small_buf = Buf("small")
    wt_sb = sb("wt", [128, L, 768], BF16)
    bshl_sb = sb("bshl", [128, L, 768], BF16)
    wpb_sb = sb("wpb", [128, L, 256], BF16)
    wt_buf = Buf("wt")
    bshl_buf = Buf("bshl")
    wpb_buf = Buf("wpb")
    stage_sb = x_sb[NXB - 1].rearrange("p k t -> p (k t)")[:, 0:S_BIG]
    assert KC * TM >= S_BIG
    stage_buf = Buf("stage")
    ones_sb = sb("ones", [128, 128], BF16)
    sel_sb = sb("sel", [128, 64], BF16)
    mhalf_sb = sb("mhalf", [128, 8], F32)
    ms_sb = [sb("ms%d" % i, [128, TM], F32) for i in range(2)]
    ms_buf = [Buf("ms%d" % i) for i in range(2)]
    v_sb = [sb("v%d" % i, [128, 384], F32) for i in range(4)]
    v_buf = [Buf("v%d" % i) for i in range(4)]
    vsq_sbs = [sb("vsq%d" % i, [128, 384], F32) for i in range(4)]
    vsq_bufs = [Buf("vsq%d" % i) for i in range(4)]
    st_sb = [sb("st%d" % i, [128, 5, NHEAD], F32) for i in range(4)]
    st_buf = [[Buf("st%d_%d" % (i, j)) for j in range(5)] for i in range(4)]
    gcs_sb = [sb("gcs%d" % i, [128, TM], F32) for i in range(4)]
    gcs_buf = [Buf("gcs%d" % i) for i in range(4)]
    zbs_sb = [sb("zbs%d" % i, [128, TM], F32) for i in range(4)]
    zbs_buf = [Buf("zbs%d" % i) for i in range(4)]
    mxs_sb = [sb("mxs%d" % i, [128, 384], F32) for i in range(2)]
    mxs_buf = [Buf("mxs%d" % i) for i in range(2)]
    stbf_sb = sb("stbf", [128, 768], BF16)
    stbf_buf = Buf("stbf")
    gbs_sb = [sb("gbs%d" % i, [128, TM], F32) for i in range(4)]
    gbs_buf = [Buf("gbs%d" % i) for i in range(4)]
    hb_sb = [sb("hb%d" % i, [128, TM + 2], F32) for i in range(2)]
    hb_buf = [Buf("hb%d" % i) for i in range(2)]
    a0_sb = [sb("a0_%d" % i, [128, TM], F32) for i in range(2)]
    a0_buf = [Buf("a0_%d" % i) for i in range(2)]
    ccar_sb = sb("ccar", [128, L, 3, 2], F32)
    ccar_buf = [[Buf("ccar%d_%d" % (l, c)) for c in range(3)] for l in range(L)]
    zc_sb = [sb("zc%d" % i, [128, TM + 16], F32) for i in range(2 * NHS)]
    zc_buf = [Buf("zc%d" % i) for i in range(2 * NHS)]
    S_sb = [sb("S%d" % i, [128, TM + 16], F32) for i in range(4)]
    S_buf = [Buf("S%d" % i) for i in range(4)]
    zcar_sb = sb("zcar", [128, L, 2, 16], F32)
    zcar_buf = [[Buf("zcar%d_%d" % (l, c)) for c in range(2)] for l in range(L)]
    rl_sb = [sb("rl%d" % i, [128, TM], F32) for i in range(3)]
    rl_buf = [Buf("rl%d" % i) for i in range(3)]
    th_sb, th_buf = rl_sb, rl_buf
    t1_sb, t1_buf = gcs_sb, gcs_buf

    ps_sb = [nc.alloc_psum_tensor("ps%d" % i, [128, 512], F32).ap() for i in range(8)]
    ps_buf = [Buf("ps%d" % i) for i in range(8)]
    ps_next = [0]

    def ps_alloc():
        i = ps_next[0]
        ps_next[0] = (i + 1) % 8
        assert not PE.pend_w, "ps_alloc inside an open accumulation group"
        assert ps_buf[i].w is None or ps_buf[i].r, "psum bank %d still live" % i
        return ps_sb[i], ps_buf[i]

    sl_small = Slot(nc, "small")
    sl_stage = Slot(nc, "stage")
    sl_wple = Slot(nc, "wple")
    sl_wplewb = Slot(nc, "wplewb")
    sl_x = [Slot(nc, "x%d" % i) for i in range(NXB)]
    sl_o = [Slot(nc, "o%d" % i) for i in range(NXB)]
    sl_w = [Slot(nc, "w%d" % i) for i in range(nbuf_w)]
    sl_wb = [Slot(nc, "wb%d" % i) for i in range(nbuf_w)]
    wbf_buf = [[Buf("wbf%d_%d" % (l, j)) for j in range(NPIECE)] for l in range(L)]

    cbufs = [Buf("c_ones"), Buf("c_sel"), Buf("c_mhalf")]
    tr.op(POOL, lambda: nc.gpsimd.memset(ones_sb, 1.0 / 1024.0), writes=[cbufs[0]])
    tr.op(POOL, lambda: nc.gpsimd.memset(sel_sb, 0.0), writes=[cbufs[1]])
    tr.op(POOL, lambda: nc.gpsimd.memset(sel_sb[0:1, :], 1.0), writes=[cbufs[1]])
    tr.op(POOL, lambda: nc.gpsimd.memset(sel_sb[32:33, :], 1.0), writes=[cbufs[1]])
    tr.op(POOL, lambda: nc.gpsimd.memset(mhalf_sb, -0.5), writes=[cbufs[2]])
    allc = []
    for l in range(L):
        allc += ccar_buf[l] + zcar_buf[l]
    tr.op(POOL, lambda: nc.gpsimd.memset(ccar_sb, 0.0), writes=[b for l in range(L) for b in ccar_buf[l]])
    tr.op(POOL, lambda: nc.gpsimd.memset(zcar_sb, 0.0), writes=[b for l in range(L) for b in zcar_buf[l]])
    tr.dma(SP, sl_small, lambda: nc.sync.dma_start(out=small_sb, in_=spk.rearrange("l p s -> p l s")),
           writes=[small_buf])
    for l in range(L):
        tr.dma(SP, sl_stage, lambda l=l: nc.sync.dma_start(out=stage_sb, in_=bpk[l]), writes=[stage_buf])
        tr.op(POOL, lambda l=l: nc.gpsimd.affine_select(
            out=wt_sb[:, l, :].rearrange("p (h t) -> p h t", h=NHEAD),
            in_=stage_sb[:, 0:768].rearrange("p (h t) -> p h t", h=NHEAD),
            pattern=[[0, NHEAD], [1, 128]], compare_op=ALU.is_ge, fill=0.0, base=0,
            channel_multiplier=-1), reads=[stage_buf], writes=[wt_buf])
        tr.op(DVE, lambda: nc.vector.tensor_copy(out=stbf_sb, in_=stage_sb[:, 768:1536]),
              reads=[stage_buf], writes=[stbf_buf])
        tr.op(DVE, lambda l=l: nc.vector.tensor_copy(out=bshl_sb[0:32, l, :], in_=stbf_sb[0:32, :]),
              reads=[stbf_buf], writes=[bshl_buf])
        tr.op(DVE, lambda l=l: nc.vector.tensor_tensor(out=bshl_sb[32:64, l, :], in0=stage_sb[32:64, 768:1536],
                                                       in1=stbf_sb[32:64, :], op=ALU.subtract),
              reads=[stage_buf, stbf_buf], writes=[bshl_buf])
        tr.op(DVE, lambda l=l: nc.vector.tensor_copy(out=bshl_sb[64:128, l, :], in_=stbf_sb[64:128, :]),
              reads=[stbf_buf], writes=[bshl_buf])
        tr.op(DVE, lambda l=l: nc.vector.tensor_copy(out=wpb_sb[:, l, :], in_=stage_sb[:, 1536:1792]),
              reads=[stage_buf], writes=[wpb_buf])

    for b_ in x_buf[NXB - 1]:
        b_.w = stage_buf.w
        b_.r = dict(stage_buf.r)

    npairs = (len(tiles) + NHS - 1) // NHS
    seq = [(pi, l, j) for pi in range(npairs) for l in range(L) for j in RING]
    issued = [0]

    def issue_load(gi):
        pi, l, j = seq[gi]
        s = gi % nbuf_w
        _, off, ln = PIECES[j]
        if pi == 0:
            tr.dma(POOL, sl_w[s], lambda: nc.gpsimd.dma_start(
                out=w_sb[s][:, 0:ln].rearrange("p (a b) -> p a b", b=1024),
                in_=wpk[l, :, off:off + ln].rearrange("p (a b) -> p a b", b=1024)),
                writes=[w_buf[s]])
            if npairs > 1:
                tr.dma(SP, sl_wb[s], lambda: nc.sync.dma_start(out=wbf[l, :, off:off + ln], in_=w_sb[s][:, 0:ln]),
                       reads=[w_buf[s]], writes=[wbf_buf[l][j]])
        else:
            tr.dma(SP, sl_w[s], lambda: nc.sync.dma_start(out=w_sb[s][:, 0:ln], in_=wbf[l, :, off:off + ln]),
                   reads=[wbf_buf[l][j]], writes=[w_buf[s]])

    def get_piece(pi, l, rj):
        gi = (pi * L + l) * NRING + rj
        while issued[0] < len(seq) and issued[0] <= gi + nbuf_w - 1:
            issue_load(issued[0])
            issued[0] += 1
        s = gi % nbuf_w
        return w_sb[s], w_buf[s]

    def load_ple(pi, l):
        _, off, ln = PIECES[PLE_IDX]
        if pi == 0:
            tr.dma(POOL, sl_wple, lambda: nc.gpsimd.dma_start(
                out=wple_sb.rearrange("p (a b) -> p a b", b=1024),
                in_=wpk[l, :, off:off + ln].rearrange("p (a b) -> p a b", b=1024)), writes=[wple_buf])
            if npairs > 1:
                tr.dma(SP, sl_wplewb, lambda: nc.sync.dma_start(out=wbf[l, :, off:off + ln], in_=wple_sb),
                       reads=[wple_buf], writes=[wbf_buf[l][PLE_IDX]])
        else:
            tr.dma(SP, sl_wple, lambda: nc.sync.dma_start(out=wple_sb, in_=wbf[l, :, off:off + ln]),
                   reads=[wbf_buf[l][PLE_IDX]], writes=[wple_buf])


    def norm_sq(xi, T, k):
        tr.op(ACT, lambda: nc.scalar.activation(out=cur.sq_sb[k][:, :T], in_=x_sb[xi][:, k, :T], func=AF.Square),
              reads=[x_buf[xi][k]], writes=[cur.sq_buf[k]])

    def norm(xi, T, l, gcol, skip_sq=False):
        xs, xb = x_sb[xi], x_buf[xi]
        pst, psb = ps_alloc()
        if not skip_sq:
            for k in range(KC):
                norm_sq(xi, T, k)
        for k in range(KC):
            tr.mm(lambda k=k: nc.tensor.matmul(pst[:, :T], lhsT=ones_sb, rhs=cur.sq_sb[k][:, :T],
                                               start=(k == 0), stop=(k == KC - 1)),
                  reads=[cur.sq_buf[k], cbufs[0]], writes=[psb], inc=True)
        r = norm.par
        norm.par ^= 1
        tr.op(ACT, lambda: nc.scalar.activation(out=ms_sb[r][:, :T], in_=pst[:, :T], func=AF.Sqrt,
                                                bias=eps_sb[:, 0:1], scale=1.0),
              reads=[psb, cbufs[2]], writes=[ms_buf[r]])
        tr.op(DVE, lambda: nc.vector.reciprocal(out=ms_sb[r][:, :T], in_=ms_sb[r][:, :T]),
              reads=[], writes=[ms_buf[r]])
        return r

    norm.par = 0

    def norm_apply_h(xi, T, l, gcol, r):
        xs, xb = x_sb[xi], x_buf[xi]
        for k in range(KC):
            tr.op(DVE, lambda k=k: nc.vector.scalar_tensor_tensor(
                out=cur.h_sb[:, k, :T], in0=xs[:, k, :T], scalar=small_sb[:, l, gcol + k:gcol + k + 1],
                in1=ms_sb[r][:, :T], op0=ALU.mult, op1=ALU.mult),
                reads=[xb[k], ms_buf[r], small_buf], writes=[cur.h_buf[k]])

    def group(out_ap, out_buf, pairs, reads_each):
        n = len(pairs)
        for i, (lt, rh) in enumerate(pairs):
            tr.mm(lambda lt=lt, rh=rh, i=i: nc.tensor.matmul(out_ap, lhsT=lt, rhs=rh, start=(i == 0), stop=(i == n - 1)),
                  reads=reads_each[i], writes=[out_buf], inc=(i == n - 1))

    eps_sb = sb("eps", [128, 2], F32)
    tr.op(POOL, lambda: nc.gpsimd.memset(eps_sb[:, 0:1], RMS_EPS), writes=[cbufs[2]])
    tr.op(POOL, lambda: nc.gpsimd.memset(eps_sb[:, 1:2], LN_EPS), writes=[cbufs[2]])

    lnpar = [0]
    cvpar = [0]

    def ln_chain(psv, psvb, l, c):
        r = lnpar[0]
        lnpar[0] = (r + 1) % 4
        vs, vb = v_sb[r], v_buf[r]
        vsq_sb, vsq_buf = vsq_sbs[r], vsq_bufs[r]
        st, stb = st_sb[r], st_buf[r]
        tr.op(ACT, lambda: nc.scalar.activation(out=vs, in_=psv[:, 0:384], func=AF.Gelu), reads=[psvb], writes=[vb])
        v3 = vs.rearrange("p (h d) -> p h d", h=NHEAD)
        tr.op(DVE, lambda: nc.vector.tensor_reduce(out=st[:, 0, :], in_=v3, axis=AX.X, op=ALU.add),
              reads=[vb], writes=[stb[0]])
        tr.op(ACT, lambda: nc.scalar.activation(out=vsq_sb, in_=vs, func=AF.Square), reads=[vb], writes=[vsq_buf])
        tr.op(DVE, lambda: nc.vector.tensor_reduce(out=st[:, 1, :], in_=vsq_sb.rearrange("p (h d) -> p h d", h=NHEAD),
                                                   axis=AX.X, op=ALU.add), reads=[vsq_buf], writes=[stb[1]])
        tr.op(DVE, lambda: nc.vector.tensor_scalar(out=st[:, 2, :], in0=st[:, 0, :], scalar1=1.0 / 64.0, scalar2=None,
                                                   op0=ALU.mult), reads=[stb[0]], writes=[stb[2]])
        tr.op(DVE, lambda: nc.vector.tensor_tensor(out=st[:, 3, :], in0=st[:, 2, :], in1=st[:, 2, :], op=ALU.mult),
              reads=[stb[2]], writes=[stb[3]])
        tr.op(DVE, lambda: nc.vector.tensor_scalar(out=st[:, 4, :], in0=st[:, 1, :], scalar1=1.0 / 64.0, scalar2=LN_EPS,
                                                   op0=ALU.mult, op1=ALU.add), reads=[stb[1]], writes=[stb[4]])
        tr.op(DVE, lambda: nc.vector.tensor_tensor(out=st[:, 4, :], in0=st[:, 4, :], in1=st[:, 3, :], op=ALU.subtract),
              reads=[stb[3]], writes=[stb[4]])
        tr.op(POOL, lambda: nc.gpsimd.tensor_tensor(out=st[:, 4, :], in0=st[:, 4, :], in1=mhalf_sb[:, 0:NHEAD], op=ALU.pow),
              reads=[cbufs[2]], writes=[stb[4]])
        mean_bc = st[:, 2, :].unsqueeze(2).broadcast_to([128, NHEAD, 64])
        rstd_bc = st[:, 4, :].unsqueeze(2).broadcast_to([128, NHEAD, 64])
        vsq3 = vsq_sb.rearrange("p (h d) -> p h d", h=NHEAD)
        lng3 = small_sb[:, l, C_LNG:C_LNG + 384].rearrange("p (h d) -> p h d", h=NHEAD)
        tr.op(DVE, lambda: nc.vector.tensor_tensor(out=v3, in0=v3, in1=mean_bc, op=ALU.subtract),
              reads=[stb[2]], writes=[vb])
        tr.op(DVE, lambda: nc.vector.tensor_tensor(out=vsq3, in0=lng3, in1=rstd_bc, op=ALU.mult),
              reads=[stb[4], small_buf], writes=[vsq_buf])
        tr.op(DVE, lambda: nc.vector.tensor_tensor(out=vs, in0=vs, in1=vsq_sb, op=ALU.mult),
              reads=[vsq_buf], writes=[vb])
        tr.op(DVE, lambda: nc.vector.tensor_tensor(out=cur.vn_sb[c], in0=vs, in1=small_sb[:, l, C_LNB:C_LNB + 384], op=ALU.add),
              reads=[vb, small_buf], writes=[cur.vn_buf[c]])
        return c

    def mix_chunk(c, r, l):
        pm, pmb = ps_alloc()
        for j in range(3):
            for e in range(2):
                hd = 2 * j + e
                o = pm[e * 64:(e + 1) * 64, j * 128:(j + 1) * 128]
                tr.mm(lambda o=o, hd=hd: nc.tensor.matmul(o, lhsT=cur.vn_sb[r][:, hd * 64:(hd + 1) * 64],
                                                          rhs=wt_sb[:, l, hd * 128:(hd + 1) * 128],
                                                          start=True, stop=False, skip_group_check=True),
                      reads=[cur.vn_buf[r], wt_buf], writes=[pmb])
                last = (j == 2 and e == 1)
                tr.mm(lambda o=o, hd=hd: nc.tensor.matmul(o, lhsT=sel_sb, rhs=bshl_sb[:, l, hd * 128:(hd + 1) * 128],
                                                          start=False, stop=True, skip_group_check=True),
                      reads=[bshl_buf, cbufs[1]], writes=[pmb], inc=last)
        mr = mixpar[0]
        mixpar[0] ^= 1
        tr.op(ACT, lambda: nc.scalar.activation(out=mxs_sb[mr], in_=pm[:, 0:384], func=AF.Copy), reads=[pmb], writes=[mxs_buf[mr]])
        tr.op(DVE, lambda: nc.vector.tensor_tensor(
            out=cur.ycat_sb[:, 0:3, c * 128:(c + 1) * 128],
            in0=mxs_sb[mr].rearrange("p (j t) -> p j t", j=3),
            in1=cur.u_sb[:, 0:3, c * 128:(c + 1) * 128], op=ALU.mult),
            reads=[mxs_buf[mr]] + cur.u_buf, writes=cur.ycat_buf[0:3])

    mixpar = [0]

    def load_x(ti, t0):
        xi_ = ti % NXB
        T_ = tiles[ti] * 128
        tr.dma(SP, sl_x[xi_], lambda: nc.sync.dma_start(
            out=x_sb[xi_][:, :, :T_], in_=xT[:, :, t0:t0 + T_].rearrange("k p t -> p k t")), writes=x_buf[xi_])

    def tile_gen(ti, pi, tok0):
        nch = tiles[ti]
        T = nch * 128
        xi = ti % NXB
        xs, xb = x_sb[xi], x_buf[xi]
        r = norm(xi, T, 0, C_GMIX)
        norm_apply_h(xi, T, 0, C_GMIX, r)
        for l in range(L):
            yield
            if cur.hs == 0:
                load_ple(pi, l)
            tr.dma(POOL, cur.sl_p, lambda: nc.gpsimd.dma_start(
                out=cur.p_sb[:, :, :T], in_=pT[l, :, :, tok0:tok0 + T].rearrange("k p t -> p k t")), writes=[cur.p_buf])
            wp, wb = get_piece(pi, l, 0)
            w4 = wp[:, 0:3072].rearrange("p (m k j) -> p m k j", m=3, k=KC)
            for m in range(3):
                pu, pub = ps_alloc()
                group(pu[:, :T], pub, [(w4[:, m, k, :], cur.h_sb[:, k, :T]) for k in range(KC)],
                      [[wb, cur.h_buf[k]] for k in range(KC)])
                tr.op(ACT, lambda m=m, pu=pu: nc.scalar.activation(out=cur.u_sb[:, m, :T], in_=pu[:, :T], func=AF.Gelu),
                      reads=[pub], writes=[cur.u_buf[m]])
            yield
            wp, wb = get_piece(pi, l, 1)
            wv = wp[:, 0:3072].rearrange("p (k j) -> p k j", k=KC)
            vn_idx = []
            pend_mix = []
            for c in range(nch):
                pv, pvb = ps_alloc()
                group(pv[:, 0:384], pvb, [(cur.h_sb[:, k, c * 128:(c + 1) * 128], wv[:, k, :]) for k in range(KC)],
                      [[wb, cur.h_buf[k]] for k in range(KC)])
                pend_mix.append((c, pv, pvb))
            ln_done = {}

            def do_ln(c):
                cc_, pv_, pvb_ = pend_mix[c]
                ln_done[c] = ln_chain(pv_, pvb_, l, c)

            do_ln(0)
            if nch > 1:
                do_ln(1)
            yield
            wp, wb = get_piece(pi, l, 2)
            w4 = wp[:, 0:2048].rearrange("p (m k j) -> p m k j", m=2, k=KC)
            has_fix = (tok0 <= HALO * 128 < tok0 + T)
            q0 = 16 + HALO * 128 - tok0
            for cc in range(2):
                pz, pzb = ps_alloc()
                group(pz[:, :T], pzb, [(w4[:, cc, k, :], cur.h_sb[:, k, :T]) for k in range(KC)],
                      [[wb, cur.h_buf[k]] for k in range(KC)])
                zs, zb = zc_sb[cur.hs * 2 + cc], zc_buf[cur.hs * 2 + cc]
                tr.op(ACT, lambda: nc.scalar.activation(out=zs[:, 16:16 + T], in_=pz[:, :T], func=AF.Copy),
                      reads=[pzb], writes=[zb])
                tr.op(DVE, lambda cc=cc: nc.vector.tensor_copy(out=zs[:, 0:16], in_=zcar_sb[:, l, cc, :]),
                      reads=[zcar_buf[l][cc]], writes=[zb])
                tr.op(DVE, lambda cc=cc: nc.vector.tensor_copy(out=zcar_sb[:, l, cc, :], in_=zs[:, T:T + 16]),
                      reads=[zb], writes=[zcar_buf[l][cc]])
                W_ = 16 + T
                tr.op(POOL, lambda: nc.gpsimd.tensor_tensor(out=S_sb[0][:, 1:W_], in0=zs[:, 1:W_], in1=zs[:, 0:W_ - 1], op=ALU.add),
                      reads=[zb], writes=[S_buf[0]])
                if cc == 0:
                    tr.op(POOL, lambda: nc.gpsimd.tensor_tensor(out=S_sb[1][64:128, 3:W_], in0=S_sb[0][64:128, 3:W_],
                                                                in1=S_sb[0][64:128, 1:W_ - 2], op=ALU.add),
                          reads=[S_buf[0]], writes=[S_buf[1]])
                    srcs = [(0, slice(0, 64)), (1, slice(64, 128))]
                else:
                    tr.op(POOL, lambda: nc.gpsimd.tensor_tensor(out=S_sb[1][:, 3:W_], in0=S_sb[0][:, 3:W_],
                                                                in1=S_sb[0][:, 1:W_ - 2], op=ALU.add),
                          reads=[S_buf[0]], writes=[S_buf[1]])
                    tr.op(POOL, lambda: nc.gpsimd.tensor_tensor(out=S_sb[2][:, 7:W_], in0=S_sb[1][:, 7:W_],
                                                                in1=S_sb[1][:, 3:W_ - 4], op=ALU.add),
                          reads=[S_buf[1]], writes=[S_buf[2]])
                    tr.op(POOL, lambda: nc.gpsimd.tensor_tensor(out=S_sb[3][64:128, 15:W_], in0=S_sb[2][64:128, 15:W_],
                                                                in1=S_sb[2][64:128, 7:W_ - 8], op=ALU.add),
                          reads=[S_buf[2]], writes=[S_buf[3]])
                    srcs = [(2, slice(0, 64)), (3, slice(64, 128))]
                for si, psl in srcs:
                    if has_fix:
                        tr.op(DVE, lambda si=si, psl=psl, cc=cc: nc.vector.tensor_tensor(
                            out=S_sb[si][psl, q0:q0 + 16], in0=S_sb[si][psl, q0:q0 + 16],
                            in1=small_sb[psl, l, C_CORR + cc * 16:C_CORR + (cc + 1) * 16], op=ALU.mult),
                            reads=[small_buf], writes=[S_buf[si]])
                    tr.op(DVE, lambda si=si, psl=psl, cc=cc: nc.vector.scalar_tensor_tensor(
                        out=cur.pl_sb[cc][psl, :T], in0=S_sb[si][psl, 16:16 + T], scalar=small_sb[psl, l, C_IW + cc:C_IW + cc + 1],
                        in1=zs[psl, 16:16 + T], op0=ALU.mult, op1=ALU.subtract),
                        reads=[S_buf[si], zb, small_buf], writes=[cur.pl_buf[cc]])
            for c in range(3):
                yield
                wp, wb = get_piece(pi, l, 3 + c)
                w4 = wp[:, 0:3072].rearrange("p (m k j) -> p m k j", m=3, k=KC)
                pss = []
                for m in range(3):
                    pz, pzb = ps_alloc()
                    group(pz[:, :T], pzb, [(w4[:, m, k, :], cur.h_sb[:, k, :T]) for k in range(KC)],
                          [[wb, cur.h_buf[k]] for k in range(KC)])
                    pss.append((pz, pzb))
                (pz, pzb), (pgc, pgcb), (pgb, pgbb) = pss
                rr = cvpar[0]
                cvpar[0] = (rr + 1) % 4
                cw = lambda k, c=c: small_sb[:, l, C_CW + c * 3 + k:C_CW + c * 3 + k + 1]
                tr.op(ACT, lambda: nc.scalar.activation(out=zbs_sb[rr][:, :T], in_=pz[:, :T], func=AF.Copy),
                      reads=[pzb], writes=[zbs_buf[rr]])
                tr.op(ACT, lambda: nc.scalar.activation(out=gcs_sb[rr][:, :T], in_=pgc[:, :T], func=AF.Copy),
                      reads=[pgcb], writes=[gcs_buf[rr]])
                tr.op(ACT, lambda: nc.scalar.activation(out=gbs_sb[rr][:, :T], in_=pgb[:, :T], func=AF.Copy),
                      reads=[pgbb], writes=[gbs_buf[rr]])
                tr.op(DVE, lambda: nc.vector.tensor_tensor(out=hb_sb[rr % 2][:, 2:2 + T], in0=zbs_sb[rr][:, :T], in1=gcs_sb[rr][:, :T],
                                                           op=ALU.mult), reads=[zbs_buf[rr], gcs_buf[rr]], writes=[hb_buf[rr % 2]])
                tr.op(DVE, lambda c=c: nc.vector.tensor_copy(out=hb_sb[rr % 2][:, 0:2], in_=ccar_sb[:, l, c, :]),
                      reads=[ccar_buf[l][c]], writes=[hb_buf[rr % 2]])
                tr.op(DVE, lambda: nc.vector.tensor_scalar(out=a0_sb[rr % 2][:, :T], in0=hb_sb[rr % 2][:, 2:2 + T], scalar1=cw(2), scalar2=None,
                                                           op0=ALU.mult), reads=[hb_buf[rr % 2], small_buf], writes=[a0_buf[rr % 2]])
                tr.op(DVE, lambda c=c: nc.vector.tensor_copy(out=ccar_sb[:, l, c, :], in_=hb_sb[rr % 2][:, T:T + 2]),
                      reads=[hb_buf[rr % 2]], writes=[ccar_buf[l][c]])
                tr.op(DVE, lambda: nc.vector.scalar_tensor_tensor(out=a0_sb[rr % 2][:, :T], in0=hb_sb[rr % 2][:, 1:1 + T], scalar=cw(1),
                                                                  in1=a0_sb[rr % 2][:, :T], op0=ALU.mult, op1=ALU.add),
                      reads=[hb_buf[rr % 2], small_buf], writes=[a0_buf[rr % 2]])
                tr.op(DVE, lambda: nc.vector.scalar_tensor_tensor(out=a0_sb[rr % 2][:, :T], in0=hb_sb[rr % 2][:, 0:T], scalar=cw(0),
                                                                  in1=a0_sb[rr % 2][:, :T], op0=ALU.mult, op1=ALU.add),
                      reads=[hb_buf[rr % 2], small_buf], writes=[a0_buf[rr % 2]])
                tr.op(DVE, lambda c=c: nc.vector.tensor_tensor(out=cur.ycat_sb[:, 3 + c, :T], in0=a0_sb[rr % 2][:, :T], in1=gbs_sb[rr][:, :T],
                                                               op=ALU.mult), reads=[a0_buf[rr % 2], gbs_buf[rr]], writes=[cur.ycat_buf[3 + c]])
                if c == 0:
                    for c2 in range(2, nch):
                        do_ln(c2)
                if c >= 1 and c - 1 < nch:
                    mix_chunk(c - 1, ln_done[c - 1], l)
                if c == 2:
                    for c2 in range(2, min(nch, 3)):
                        mix_chunk(c2, ln_done[c2], l)
            if nch > 3:
                mix_chunk(3, ln_done[3], l)
            korder = [3, 4, 5, 0, 1, 2, 6, 7]
            for i in range(2):
                yield
                if i == 0:
                    for cc in range(2):
                        pp, ppb = ps_alloc()
                        tr.mm(lambda cc=cc, pp=pp: nc.tensor.matmul(pp[:, :T], lhsT=wpb_sb[:, l, cc * 128:(cc + 1) * 128],
                                                                    rhs=cur.pl_sb[cc][:, :T], start=True, stop=True),
                              reads=[wpb_buf, cur.pl_buf[cc]], writes=[ppb], inc=True)
                        tr.op(ACT, lambda cc=cc, pp=pp: nc.scalar.activation(out=cur.ycat_sb[:, 6 + cc, :T], in_=pp[:, :T], func=AF.Copy,
                                                                             scale=small_sb[:, l, C_PSC + cc:C_PSC + cc + 1]),
                              reads=[ppb, small_buf], writes=[cur.ycat_buf[6 + cc]])
                wp, wb = get_piece(pi, l, 6 + i)
                w4 = wp.rearrange("p (m k j) -> p m k j", m=4, k=KC)
                for mi in range(4):
                    m = i * 4 + mi
                    po, pob = ps_alloc()
                    group(po[:, :T], pob, [(w4[:, mi, k, :], cur.ycat_sb[:, k, :T]) for k in korder],
                          [[wb, cur.ycat_buf[k]] for k in korder])
                    tr.op(DVE, lambda m=m, po=po: nc.vector.tensor_tensor(out=xs[:, m, :T], in0=xs[:, m, :T], in1=po[:, :T], op=ALU.add),
                          reads=[pob], writes=[xb[m]])
                    norm_sq(xi, T, m)
            r = norm(xi, T, l, C_GFF, skip_sq=True)
            norm_apply_h(xi, T, l, C_GFF, r)
            for i in range(8):
                yield
                wp, wb = get_piece(pi, l, 8 + i)
                w4 = wp.rearrange("p (m k j) -> p m k j", m=4, k=KC)
                for mi in range(4):
                    j = i * 4 + mi
                    pf, pfb = ps_alloc()
                    group(pf[:, :T], pfb, [(w4[:, mi, k, :], cur.h_sb[:, k, :T]) for k in range(KC)],
                          [[wb, cur.h_buf[k]] for k in range(KC)])
                    rr = j % 3
                    tr.op(ACT, lambda pf=pf, rr=rr: nc.scalar.activation(out=rl_sb[rr][:, :T], in_=pf[:, :T], func=AF.Relu),
                          reads=[pfb], writes=[rl_buf[rr]])
                    tr.op(POOL, lambda j=j, rr=rr: nc.gpsimd.tensor_tensor(out=cur.hid_sb[:, j, :T], in0=rl_sb[rr][:, :T], in1=rl_sb[rr][:, :T],
                                                                        op=ALU.mult), reads=[rl_buf[rr]], writes=[cur.hid_buf[j]])
            for m in range(8):
                yield
                wp, wb = get_piece(pi, l, 16 + m)
                w3 = wp.rearrange("p (k j) -> p k j", k=32)
                po, pob = ps_alloc()
                group(po[:, :T], pob, [(w3[:, k, :], cur.hid_sb[:, k, :T]) for k in range(32)],
                      [[wb, cur.hid_buf[k]] for k in range(32)])
                tr.op(DVE, lambda m=m, po=po: nc.vector.tensor_tensor(out=xs[:, m, :T], in0=xs[:, m, :T], in1=po[:, :T], op=ALU.add),
                      reads=[pob], writes=[xb[m]])
                norm_sq(xi, T, m)
            r = norm(xi, T, l, C_GPLE, skip_sq=True)
            norm_apply_h(xi, T, l, C_GPLE, r)
            wpl, wplb = wple_sb, wple_buf
            wpl4 = wpl[:, 0:2048].rearrange("p (m k j) -> p m k j", m=8, k=2)
            for i in range(2):
                yield
                wp, wb = get_piece(pi, l, 24 + i)
                w4 = wp.rearrange("p (m k j) -> p m k j", m=4, k=KC)
                for mi in range(4):
                    m = i * 4 + mi
                    pg, pgb_ = ps_alloc()
                    group(pg[:, :T], pgb_, [(w4[:, mi, k, :], cur.h_sb[:, k, :T]) for k in range(KC)],
                          [[wb, cur.h_buf[k]] for k in range(KC)])
                    pq, pqb = ps_alloc()
                    group(pq[:, :T], pqb, [(wpl4[:, m, k, :], cur.p_sb[:, k, :T]) for k in range(2)],
                          [[wplb, cur.p_buf] for k in range(2)])
                    rr = m % 2
                    tr.op(ACT, lambda pg=pg, rr=rr: nc.scalar.activation(out=th_sb[rr][:, :T], in_=pg[:, :T], func=AF.Tanh, scale=0.5),
                          reads=[pgb_], writes=[th_buf[rr]])
                    tr.op(DVE, lambda pq=pq, rr=rr: nc.vector.scalar_tensor_tensor(out=t1_sb[rr][:, :T], in0=th_sb[rr][:, :T], scalar=1.0,
                                                                                   in1=pq[:, :T], op0=ALU.add, op1=ALU.mult),
                          reads=[th_buf[rr], pqb], writes=[t1_buf[rr]])
                    tr.op(DVE, lambda m=m, rr=rr: nc.vector.scalar_tensor_tensor(out=xs[:, m, :T], in0=t1_sb[rr][:, :T], scalar=0.5,
                                                                                  in1=xs[:, m, :T], op0=ALU.mult, op1=ALU.add),
                          reads=[t1_buf[rr]], writes=[xb[m]])
                    norm_sq(xi, T, m)
            if l + 1 < L:
                r = norm(xi, T, l + 1, C_GMIX, skip_sq=True)
                norm_apply_h(xi, T, l + 1, C_GMIX, r)
        r = norm(xi, T, L - 1, C_GFIN, skip_sq=True)
        for k in range(KC):
            tr.op(DVE, lambda k=k: nc.vector.scalar_tensor_tensor(
                out=xs[:, k, :T], in0=xs[:, k, :T], scalar=small_sb[:, L - 1, C_GFIN + k:C_GFIN + k + 1],
                in1=ms_sb[r][:, :T], op0=ALU.mult, op1=ALU.mult),
                reads=[ms_buf[r], small_buf], writes=[xb[k]])
        lo = max(tok0, HALO * 128)
        hi = tok0 + T
        if hi > lo:
            tr.dma(SP, sl_o[xi], lambda: nc.sync.dma_start(
                out=outT[:, :, lo - HALO * 128:hi - HALO * 128].rearrange("k p t -> p k t"),
                in_=xs[:, :, lo - tok0:hi - tok0]), reads=xb)

    tok_starts = [0]
    for n_ in tiles:
        tok_starts.append(tok_starts[-1] + n_ * 128)
    pairs = [list(range(i, min(i + NHS, len(tiles)))) for i in range(0, len(tiles), NHS)]
    for t_ in pairs[0]:
        load_x(t_, tok_starts[t_])
    for pi, pr in enumerate(pairs):
        if pi + 1 < len(pairs):
            for t_ in pairs[pi + 1]:
                load_x(t_, tok_starts[t_])
        gens = [(tile_gen(t_, pi, tok_starts[t_]), halves[k_]) for k_, t_ in enumerate(pr)]
        alive = list(gens)
        while alive:
            for g_ in list(alive):
                cur.__dict__.update(g_[1].__dict__)
                try:
                    next(g_[0])
                except StopIteration:
                    alive.remove(g_)
    for s in sl_o:
        if s.cnt:
            nc.sync.wait_ge(s.sem, s.cnt)
    for s in sl_wb + [sl_wplewb]:
        if s.cnt:
            nc.sync.wait_ge(s.sem, s.cnt)
    return nc


def _default_tiles(nch):
    n = -(-nch // 2)
    base = nch // n
    rem = nch - base * n
    return [base + 1] * rem + [base] * (n - rem)


def kernel(x, p, norm_mix_g, w_in, sgu_w, sgu_b, sgu_ln_g, sgu_ln_b, conv_w, pool_w, pool_scale, w_out,
           norm_ff_g, w_ff1, w_ff2, norm_ple_g, w_ple_gate, w_ple_proj, final_g, _cfg=None):
    cfg = dict(CFG)
    if _cfg:
        cfg.update(_cfg)
    inp = dict(norm_mix_g=norm_mix_g, w_in=w_in, sgu_w=sgu_w, sgu_b=sgu_b, sgu_ln_g=sgu_ln_g, sgu_ln_b=sgu_ln_b,
               conv_w=conv_w, pool_w=pool_w, pool_scale=pool_scale, w_out=w_out, norm_ff_g=norm_ff_g, w_ff1=w_ff1,
               w_ff2=w_ff2, norm_ple_g=norm_ple_g, w_ple_gate=w_ple_gate, w_ple_proj=w_ple_proj, final_g=final_g)
    x = np.asarray(x, np.float32)
    p = np.asarray(p, np.float32)
    L = cfg["L"]
    B, S, _ = x.shape
    nseg = NCORES // B
    SEGC = S // 128 // nseg
    NCH = SEGC + HALO
    tiles = cfg["TILES"] or _default_tiles(NCH)
    wpk = _pack_weights(inp, L)
    bpk = _pack_big(inp, L)
    spk_first = _pack_small(inp, L, True)
    spk_rest = _pack_small(inp, L, False)
    in_maps = []
    for c in range(NCORES):
        b, sg = divmod(c, nseg)
        t0 = sg * SEGC * 128
        xs = np.zeros((NCH * 128, D), np.float32)
        ps = np.zeros((L, NCH * 128, DPLE), np.float32)
        if sg == 0:
            xs[HALO * 128:] = x[b, t0:t0 + SEGC * 128]
            ps[:, HALO * 128:] = p[:L, b, t0:t0 + SEGC * 128]
        else:
            xs[:] = x[b, t0 - HALO * 128:t0 + SEGC * 128]
            ps[:] = p[:L, b, t0 - HALO * 128:t0 + SEGC * 128]
        xTc = np.ascontiguousarray(xs.T).reshape(KC, 128, NCH * 128)
        pTc = np.ascontiguousarray(ps.transpose(0, 2, 1)).reshape(L, 2, 128, NCH * 128)
        in_maps.append({"xT": xTc, "pT": pTc, "wpk": wpk, "spk": spk_first if sg == 0 else spk_rest, "bpk": bpk})
    nc = build_nc(L, NCH, tiles, cfg["NBUF_W"])
    res = run_bass_kernel_spmd(nc, in_maps, core_ids=list(range(NCORES)))
    out = np.zeros((B, S, D), np.float32)
    for c in range(NCORES):
        b, sg = divmod(c, nseg)
        t0 = sg * SEGC * 128
        oT = np.asarray(res.results[c]["outT"]).reshape(D, SEGC * 128)
        out[b, t0:t0 + SEGC * 128] = oT.T
    return out
```

```python
import numpy as np
import concourse.bass as bass
import concourse.mybir as mybir
from concourse.bass_utils import run_bass_kernel_spmd

F32 = mybir.dt.float32
BF16 = mybir.dt.bfloat16
AF = mybir.ActivationFunctionType
ALU = mybir.AluOpType
AX = mybir.AxisListType

D = 1024
KC = 8
DFF = 4096
DPLE = 256
NHEAD = 6
RMS_EPS = 1e-6
LN_EPS = 1e-5
HALO = 2
NCORES = 8

CFG = dict(L=4, SEGC=32, TILES=None, NBUF_W=4)

PIECES = []


def _mk_pieces():
    off = 0

    def add(name, ln):
        nonlocal off
        PIECES.append((name, off, ln))
        off += ln
    add("U", 3072)
    add("V", 3072)
    add("POOL", 2048)
    for c in range(3):
        add("CONV%d" % c, 3072)
    for i in range(2):
        add("OUT%d" % i, 4096)
    for i in range(8):
        add("FF1_%d" % i, 4096)
    for i in range(8):
        add("FF2_%d" % i, 4096)
    add("PLE", 2048)
    for i in range(2):
        add("GATE%d" % i, 4096)
    return off


E_W = _mk_pieces()
NPIECE = len(PIECES)
WSLOT = 4096
PLE_IDX = [i for i, pc in enumerate(PIECES) if pc[0] == "PLE"][0]
RING = [i for i in range(NPIECE) if i != PLE_IDX]
NRING = len(RING)

C_GMIX, C_GFF, C_GPLE, C_GFIN, C_CW, C_PSC, C_IW, C_CORR, C_LNG, C_LNB = 0, 8, 16, 24, 32, 41, 43, 45, 77, 461
S_SMALL = 845
S_BIG = 1792


def _chunk_block(W, col0, ncol=128):
    k = W.shape[0] // 128
    blk = W[:, col0:col0 + ncol].reshape(k, 128, ncol)
    return np.ascontiguousarray(blk.transpose(1, 0, 2)).reshape(128, k * ncol)


def _pack_weights(inp, L):
    wpk = np.zeros((L, 128, E_W), np.float32)
    for l in range(L):
        w_in = np.asarray(inp["w_in"][l])
        w_out = np.asarray(inp["w_out"][l])
        w1 = np.asarray(inp["w_ff1"][l])
        w2 = np.asarray(inp["w_ff2"][l])
        wg = np.asarray(inp["w_ple_gate"][l])
        wp = np.asarray(inp["w_ple_proj"][l])
        parts = []
        parts += [_chunk_block(w_in, c * 128) for c in range(3)]
        parts += [_chunk_block(w_in, 384, 384)]
        parts += [_chunk_block(w_in, 1920), _chunk_block(w_in, 2048)]
        for c in range(3):
            parts += [_chunk_block(w_in, 768 + c * 128), _chunk_block(w_in, 1536 + c * 128),
                      _chunk_block(w_in, 1152 + c * 128)]
        parts += [_chunk_block(w_out, m * 128) for m in range(8)]
        parts += [_chunk_block(w1, j * 128) for j in range(32)]
        parts += [_chunk_block(w2, m * 128) for m in range(8)]
        parts += [_chunk_block(wp, m * 128) for m in range(8)]
        parts += [_chunk_block(wg, m * 128) for m in range(8)]
        row = np.concatenate(parts, axis=1)
        assert row.shape == (128, E_W), row.shape
        wpk[l] = row
    return wpk


def _pack_small(inp, L, first_seg):
    spk = np.zeros((L, 128, S_SMALL), np.float32)
    for l in range(L):
        s = spk[l]
        s[:, C_GMIX:C_GMIX + 8] = np.asarray(inp["norm_mix_g"][l]).reshape(8, 128).T
        s[:, C_GFF:C_GFF + 8] = np.asarray(inp["norm_ff_g"][l]).reshape(8, 128).T
        s[:, C_GPLE:C_GPLE + 8] = np.asarray(inp["norm_ple_g"][l]).reshape(8, 128).T
        s[:, C_GFIN:C_GFIN + 8] = np.asarray(inp["final_g"]).reshape(8, 128).T
        cw = np.asarray(inp["conv_w"][l])
        for c in range(3):
            for k in range(3):
                s[:, C_CW + c * 3 + k] = cw[k, c * 128:(c + 1) * 128]
        s[:, C_PSC:C_PSC + 2] = np.asarray(inp["pool_scale"][l]).reshape(2, 128).T
        wins = np.array([[2, 4], [8, 16]], np.float32)
        for cc in range(2):
            for hf in range(2):
                win = wins[cc, hf]
                s[hf * 64:(hf + 1) * 64, C_IW + cc] = 1.0 / win
                for i in range(16):
                    v = win / min(i + 1.0, win) if first_seg else 1.0
                    s[hf * 64:(hf + 1) * 64, C_CORR + cc * 16 + i] = v
        s[:, C_LNG:C_LNG + 384] = np.asarray(inp["sgu_ln_g"][l])[None, :]
        s[:, C_LNB:C_LNB + 384] = np.asarray(inp["sgu_ln_b"][l])[None, :]
    return spk


def _pack_big(inp, L):
    bpk = np.zeros((L, 128, S_BIG), np.float32)
    for l in range(L):
        w = np.asarray(inp["sgu_w"][l])
        bpk[l, :, 0:768] = np.ascontiguousarray(w.transpose(2, 0, 1)).reshape(128, 768)
        bs = np.asarray(inp["sgu_b"][l]).reshape(768)
        bpk[l, 0, 768:1536] = bs
        bpk[l, 32, 768:1536] = bs
        wpool = np.asarray(inp["pool_w"][l])
        for cc in range(2):
            blk = np.zeros((128, 128), np.float32)
            blk[0:64, 0:64] = wpool[2 * cc]
            blk[64:128, 64:128] = wpool[2 * cc + 1]
            bpk[l, :, 1536 + cc * 128:1536 + (cc + 1) * 128] = blk
    return bpk


class Buf:
    __slots__ = ("name", "w", "r")

    def __init__(self, name):
        self.name = name
        self.w = None
        self.r = {}


class Eng:
    def __init__(self, nc, name, h, self_sync):
        self.name = name
        self.h = h
        self.sem = nc.alloc_semaphore("s_" + name)
        self.cnt = 0
        self.seen = {}
        self.self_sync = self_sync
        self.pend_r = []
        self.pend_w = []


class Slot:
    def __init__(self, nc, name):
        self.name = name
        self.sem = nc.alloc_semaphore("d_" + name)
        self.cnt = 0


class TR:
    def __init__(self, nc):
        self.nc = nc
        self.pe = Eng(nc, "pe", nc.tensor, False)
        self.act = Eng(nc, "act", nc.scalar, True)
        self.dve = Eng(nc, "dve", nc.vector, True)
        self.pool = Eng(nc, "pool", nc.gpsimd, True)
        self.sp = Eng(nc, "sp", nc.sync, False)

    @staticmethod
    def _deps(reads, writes):
        need = {}
        for b in reads:
            if b.w is not None:
                o, c = b.w
                if need.get(o, 0) < c:
                    need[o] = c
        for b in writes:
            if b.w is not None:
                o, c = b.w
                if need.get(o, 0) < c:
                    need[o] = c
            for o, c in b.r.items():
                if need.get(o, 0) < c:
                    need[o] = c
        return need

    @staticmethod
    def _wait(eng, need):
        for o, c in need.items():
            if o is eng and not eng.self_sync:
                continue
            if eng.seen.get(o, 0) >= c:
                continue
            eng.h.wait_ge(o.sem, c)
            eng.seen[o] = c

    def op(self, eng, fn, reads=(), writes=()):
        self._wait(eng, self._deps(reads, writes))
        ins = fn()
        eng.cnt += 1
        ins.then_inc(eng.sem, 1)
        for b in reads:
            b.r[eng] = eng.cnt
        for b in writes:
            b.w = (eng, eng.cnt)
            b.r = {}
        return ins

    def mm(self, fn, reads=(), writes=(), inc=False):
        eng = self.pe
        self._wait(eng, self._deps(reads, writes))
        ins = fn()
        eng.pend_r.extend(reads)
        eng.pend_w.extend(writes)
        if inc:
            eng.cnt += 1
            ins.then_inc(eng.sem, 1)
            for b in eng.pend_r:
                b.r[eng] = eng.cnt
            for b in eng.pend_w:
                b.w = (eng, eng.cnt)
                b.r = {}
            eng.pend_r = []
            eng.pend_w = []
        return ins

    def dma(self, q, slot, fn, reads=(), writes=()):
        self._wait(q, self._deps(reads, writes))
        ins = fn()
        slot.cnt += 16
        ins.then_inc(slot.sem, 16)
        for b in reads:
            b.r[slot] = slot.cnt
        for b in writes:
            b.w = (slot, slot.cnt)
            b.r = {}
        return ins


def build_nc(L, NCH, tiles, nbuf_w=3):
    assert sum(tiles) == NCH and max(tiles) <= 4 and len(set(tiles[i] for i in range(0, len(tiles) - len(tiles) % 2))) <= 1
    NTOK = NCH * 128
    NMAIN = (NCH - HALO) * 128
    nc = bass.Bass("TRN2", target_bir_lowering=False)
    tr = TR(nc)
    PE, ACT, DVE, POOL, SP = tr.pe, tr.act, tr.dve, tr.pool, tr.sp

    xT = nc.dram_tensor("xT", [KC, 128, NTOK], F32, kind="ExternalInput").ap()
    pT = nc.dram_tensor("pT", [L, 2, 128, NTOK], F32, kind="ExternalInput").ap()
    wpk = nc.dram_tensor("wpk", [L, 128, E_W], F32, kind="ExternalInput").ap()
    spk = nc.dram_tensor("spk", [L, 128, S_SMALL], F32, kind="ExternalInput").ap()
    bpk = nc.dram_tensor("bpk", [L, 128, S_BIG], F32, kind="ExternalInput").ap()
    outT = nc.dram_tensor("outT", [KC, 128, NMAIN], F32, kind="ExternalOutput").ap()
    wbf = nc.dram_tensor("wbf", [L, 128, E_W], BF16, kind="Internal").ap()

    def sb(name, shape, dt):
        return nc.alloc_sbuf_tensor(name, shape, dt).ap()

    NXB = 4
    TM = max(tiles) * 128
    NHS = 2
    x_sb = [sb("x%d" % i, [128, KC, TM], F32) for i in range(NXB)]
    x_buf = [[Buf("x%d_%d" % (i, k)) for k in range(KC)] for i in range(NXB)]

    class Half:
        pass

    halves = []
    for hs in range(NHS):
        H = Half()
        H.hs = hs
        H.h_sb = sb("h%d" % hs, [128, KC, TM], BF16)
        H.h_buf = [Buf("h%d_%d" % (hs, k)) for k in range(KC)]
        H.hid_sb = sb("hid%d" % hs, [128, 32, TM], BF16)
        H.hid_buf = [Buf("hid%d_%d" % (hs, k)) for k in range(32)]
        H.ycat_sb = H.hid_sb[:, 0:KC, :]
        H.ycat_buf = H.hid_buf[0:KC]
        H.u_sb = sb("u%d" % hs, [128, 3, TM], F32)
        H.u_buf = [Buf("u%d_%d" % (hs, k)) for k in range(3)]
        H.p_sb = sb("pbf%d" % hs, [128, 2, TM], BF16)
        H.p_buf = Buf("pbf%d" % hs)
        H.vn_sb = [sb("vn%d_%d" % (hs, i), [128, 384], BF16) for i in range(max(tiles))]
        H.vn_buf = [Buf("vn%d_%d" % (hs, i)) for i in range(max(tiles))]
        H.sq_sb = [sb("sq%d_%d" % (hs, i), [128, TM], BF16) for i in range(KC)]
        H.sq_buf = [Buf("sq%d_%d" % (hs, i)) for i in range(KC)]
        H.pl_sb = [sb("pl%d_%d" % (hs, i), [128, TM], BF16) for i in range(2)]
        H.pl_buf = [Buf("pl%d_%d" % (hs, i)) for i in range(2)]
        H.sl_p = Slot(nc, "p%d" % hs)
        halves.append(H)
    cur = Half()
    wple_sb = sb("wple", [128, 2048], BF16)
    wple_buf = Buf("wple")
    w_sb = [sb("w%d" % i, [128, WSLOT], BF16) for i in range(nbuf_w)]
    w_buf = [Buf("w%d" % i) for i in range(nbuf_w)]
    small_sb = sb("small", [128, L, S_SMALL], F32)
    small_buf = Buf("small")
    wt_sb = sb("wt", [128, L, 768], BF16)
    bshl_sb = sb("bshl", [128, L, 768], BF16)
    wpb_sb = sb("wpb", [128, L, 256], BF16)
    wt_buf = Buf("wt")
    bshl_buf = Buf("bshl")
    wpb_buf = Buf("wpb")
    stage_sb = x_sb[NXB - 1].rearrange("p k t -> p (k t)")[:, 0:S_BIG]
    assert KC * TM >= S_BIG
    stage_buf = Buf("stage")
    ones_sb = sb("ones", [128, 128], BF16)
    sel_sb = sb("sel", [128, 64], BF16)
    mhalf_sb = sb("mhalf", [128, 8], F32)
    ms_sb = [sb("ms%d" % i, [128, TM], F32) for i in range(2)]
    ms_buf = [Buf("ms%d" % i) for i in range(2)]
    v_sb = [sb("v%d" % i, [128, 384], F32) for i in range(4)]
    v_buf = [Buf("v%d" % i) for i in range(4)]
    vsq_sbs = [sb("vsq%d" % i, [128, 384], F32) for i in range(4)]
    vsq_bufs = [Buf("vsq%d" % i) for i in range(4)]
    st_sb = [sb("st%d" % i, [128, 5, NHEAD], F32) for i in range(4)]
    st_buf = [[Buf("st%d_%d" % (i, j)) for j in range(5)] for i in range(4)]
    gcs_sb = [sb("gcs%d" % i, [128, TM], F32) for i in range(4)]
    gcs_buf = [Buf("gcs%d" % i) for i in range(4)]
    zbs_sb = [sb("zbs%d" % i, [128, TM], F32) for i in range(4)]
    zbs_buf = [Buf("zbs%d" % i) for i in range(4)]
    mxs_sb = [sb("mxs%d" % i, [128, 384], F32) for i in range(2)]
    mxs_buf = [Buf("mxs%d" % i) for i in range(2)]
    stbf_sb = sb("stbf", [128, 768], BF16)
    stbf_buf = Buf("stbf")
    gbs_sb = [sb("gbs%d" % i, [128, TM], F32) for i in range(4)]
    gbs_buf = [Buf("gbs%d" % i) for i in range(4)]
    hb_sb = [sb("hb%d" % i, [128, TM + 2], F32) for i in range(2)]
    hb_buf = [Buf("hb%d" % i) for i in range(2)]
    a0_sb = [sb("a0_%d" % i, [128, TM], F32) for i in range(2)]
    a0_buf = [Buf("a0_%d" % i) for i in range(2)]
    ccar_sb = sb("ccar", [128, L, 3, 2], F32)
    ccar_buf = [[Buf("ccar%d_%d" % (l, c)) for c in range(3)] for l in range(L)]
    zc_sb = [sb("zc%d" % i, [128, TM + 16], F32) for i in range(2 * NHS)]
    zc_buf = [Buf("zc%d" % i) for i in range(2 * NHS)]
    S_sb = [sb("S%d" % i, [128, TM + 16], F32) for i in range(4)]
    S_buf = [Buf("S%d" % i) for i in range(4)]
    zcar_sb = sb("zcar", [128, L, 2, 16], F32)
    zcar_buf = [[Buf("zcar%d_%d" % (l, c)) for c in range(2)] for l in range(L)]
    rl_sb = [sb("rl%d" % i, [128, TM], F32) for i in range(3)]
    rl_buf = [Buf("rl%d" % i) for i in range(3)]
    th_sb, th_buf = rl_sb, rl_buf
    t1_sb, t1_buf = gcs_sb, gcs_buf

    ps_sb = [nc.alloc_psum_tensor("ps%d" % i, [128, 512], F32).ap() for i in range(8)]
    ps_buf = [Buf("ps%d" % i) for i in range(8)]
    ps_next = [0]

    def ps_alloc():
        i = ps_next[0]
        ps_next[0] = (i + 1) % 8
        assert not PE.pend_w, "ps_alloc inside an open accumulation group"
        assert ps_buf[i].w is None or ps_buf[i].r, "psum bank %d still live" % i
        return ps_sb[i], ps_buf[i]

    sl_small = Slot(nc, "small")
    sl_stage = Slot(nc, "stage")
    sl_wple = Slot(nc, "wple")
    sl_wplewb = Slot(nc, "wplewb")
    sl_x = [Slot(nc, "x%d" % i) for i in range(NXB)]
    sl_o = [Slot(nc, "o%d" % i) for i in range(NXB)]
    sl_w = [Slot(nc, "w%d" % i) for i in range(nbuf_w)]
    sl_wb = [Slot(nc, "wb%d" % i) for i in range(nbuf_w)]
    wbf_buf = [[Buf("wbf%d_%d" % (l, j)) for j in range(NPIECE)] for l in range(L)]

    cbufs = [Buf("c_ones"), Buf("c_sel"), Buf("c_mhalf")]
    tr.op(POOL, lambda: nc.gpsimd.memset(ones_sb, 1.0 / 1024.0), writes=[cbufs[0]])
    tr.op(POOL, lambda: nc.gpsimd.memset(sel_sb, 0.0), writes=[cbufs[1]])
    tr.op(POOL, lambda: nc.gpsimd.memset(sel_sb[0:1, :], 1.0), writes=[cbufs[1]])
    tr.op(POOL, lambda: nc.gpsimd.memset(sel_sb[32:33, :], 1.0), writes=[cbufs[1]])
    tr.op(POOL, lambda: nc.gpsimd.memset(mhalf_sb, -0.5), writes=[cbufs[2]])
    allc = []
    for l in range(L):
        allc += ccar_buf[l] + zcar_buf[l]
    tr.op(POOL, lambda: nc.gpsimd.memset(ccar_sb, 0.0), writes=[b for l in range(L) for b in ccar_buf[l]])
    tr.op(POOL, lambda: nc.gpsimd.memset(zcar_sb, 0.0), writes=[b for l in range(L) for b in zcar_buf[l]])
    tr.dma(SP, sl_small, lambda: nc.sync.dma_start(out=small_sb, in_=spk.rearrange("l p s -> p l s")),
           writes=[small_buf])
    for l in range(L):
        tr.dma(SP, sl_stage, lambda l=l: nc.sync.dma_start(out=stage_sb, in_=bpk[l]), writes=[stage_buf])
        tr.op(POOL, lambda l=l: nc.gpsimd.affine_select(
            out=wt_sb[:, l, :].rearrange("p (h t) -> p h t", h=NHEAD),
            in_=stage_sb[:, 0:768].rearrange("p (h t) -> p h t", h=NHEAD),
            pattern=[[0, NHEAD], [1, 128]], compare_op=ALU.is_ge, fill=0.0, base=0,
            channel_multiplier=-1), reads=[stage_buf], writes=[wt_buf])
        tr.op(DVE, lambda: nc.vector.tensor_copy(out=stbf_sb, in_=stage_sb[:, 768:1536]),
              reads=[stage_buf], writes=[stbf_buf])
        tr.op(DVE, lambda l=l: nc.vector.tensor_copy(out=bshl_sb[0:32, l, :], in_=stbf_sb[0:32, :]),
              reads=[stbf_buf], writes=[bshl_buf])
        tr.op(DVE, lambda l=l: nc.vector.tensor_tensor(out=bshl_sb[32:64, l, :], in0=stage_sb[32:64, 768:1536],
                                                       in1=stbf_sb[32:64, :], op=ALU.subtract),
              reads=[stage_buf, stbf_buf], writes=[bshl_buf])
        tr.op(DVE, lambda l=l: nc.vector.tensor_copy(out=bshl_sb[64:128, l, :], in_=stbf_sb[64:128, :]),
              reads=[stbf_buf], writes=[bshl_buf])
        tr.op(DVE, lambda l=l: nc.vector.tensor_copy(out=wpb_sb[:, l, :], in_=stage_sb[:, 1536:1792]),
              reads=[stage_buf], writes=[wpb_buf])

    for b_ in x_buf[NXB - 1]:
        b_.w = stage_buf.w
        b_.r = dict(stage_buf.r)

    npairs = (len(tiles) + NHS - 1) // NHS
    seq = [(pi, l, j) for pi in range(npairs) for l in range(L) for j in RING]
    issued = [0]

    def issue_load(gi):
        pi, l, j = seq[gi]
        s = gi % nbuf_w
        _, off, ln = PIECES[j]
        if pi == 0:
            tr.dma(POOL, sl_w[s], lambda: nc.gpsimd.dma_start(
                out=w_sb[s][:, 0:ln].rearrange("p (a b) -> p a b", b=1024),
                in_=wpk[l, :, off:off + ln].rearrange("p (a b) -> p a b", b=1024)),
                writes=[w_buf[s]])
            if npairs > 1:
                tr.dma(SP, sl_wb[s], lambda: nc.sync.dma_start(out=wbf[l, :, off:off + ln], in_=w_sb[s][:, 0:ln]),
                       reads=[w_buf[s]], writes=[wbf_buf[l][j]])
        else:
            tr.dma(SP, sl_w[s], lambda: nc.sync.dma_start(out=w_sb[s][:, 0:ln], in_=wbf[l, :, off:off + ln]),
                   reads=[wbf_buf[l][j]], writes=[w_buf[s]])

    def get_piece(pi, l, rj):
        gi = (pi * L + l) * NRING + rj
        while issued[0] < len(seq) and issued[0] <= gi + nbuf_w - 1:
            issue_load(issued[0])
            issued[0] += 1
        s = gi % nbuf_w
        return w_sb[s], w_buf[s]

    def load_ple(pi, l):
        _, off, ln = PIECES[PLE_IDX]
        if pi == 0:
            tr.dma(POOL, sl_wple, lambda: nc.gpsimd.dma_start(
                out=wple_sb.rearrange("p (a b) -> p a b", b=1024),
                in_=wpk[l, :, off:off + ln].rearrange("p (a b) -> p a b", b=1024)), writes=[wple_buf])
            if npairs > 1:
                tr.dma(SP, sl_wplewb, lambda: nc.sync.dma_start(out=wbf[l, :, off:off + ln], in_=wple_sb),
                       reads=[wple_buf], writes=[wbf_buf[l][PLE_IDX]])
        else:
            tr.dma(SP, sl_wple, lambda: nc.sync.dma_start(out=wple_sb, in_=wbf[l, :, off:off + ln]),
                   reads=[wbf_buf[l][PLE_IDX]], writes=[wple_buf])


    def norm_sq(xi, T, k):
        tr.op(ACT, lambda: nc.scalar.activation(out=cur.sq_sb[k][:, :T], in_=x_sb[xi][:, k, :T], func=AF.Square),
              reads=[x_buf[xi][k]], writes=[cur.sq_buf[k]])

    def norm(xi, T, l, gcol, skip_sq=False):
        xs, xb = x_sb[xi], x_buf[xi]
        pst, psb = ps_alloc()
        if not skip_sq:
            for k in range(KC):
                norm_sq(xi, T, k)
        for k in range(KC):
            tr.mm(lambda k=k: nc.tensor.matmul(pst[:, :T], lhsT=ones_sb, rhs=cur.sq_sb[k][:, :T],
                                               start=(k == 0), stop=(k == KC - 1)),
                  reads=[cur.sq_buf[k], cbufs[0]], writes=[psb], inc=True)
        r = norm.par
        norm.par ^= 1
        tr.op(ACT, lambda: nc.scalar.activation(out=ms_sb[r][:, :T], in_=pst[:, :T], func=AF.Sqrt,
                                                bias=eps_sb[:, 0:1], scale=1.0),
              reads=[psb, cbufs[2]], writes=[ms_buf[r]])
        tr.op(DVE, lambda: nc.vector.reciprocal(out=ms_sb[r][:, :T], in_=ms_sb[r][:, :T]),
              reads=[], writes=[ms_buf[r]])
        return r

    norm.par = 0

    def norm_apply_h(xi, T, l, gcol, r):
        xs, xb = x_sb[xi], x_buf[xi]
        for k in range(KC):
            tr.op(DVE, lambda k=k: nc.vector.scalar_tensor_tensor(
                out=cur.h_sb[:, k, :T], in0=xs[:, k, :T], scalar=small_sb[:, l, gcol + k:gcol + k + 1],
                in1=ms_sb[r][:, :T], op0=ALU.mult, op1=ALU.mult),
                reads=[xb[k], ms_buf[r], small_buf], writes=[cur.h_buf[k]])

    def group(out_ap, out_buf, pairs, reads_each):
        n = len(pairs)
        for i, (lt, rh) in enumerate(pairs):
            tr.mm(lambda lt=lt, rh=rh, i=i: nc.tensor.matmul(out_ap, lhsT=lt, rhs=rh, start=(i == 0), stop=(i == n - 1)),
                  reads=reads_each[i], writes=[out_buf], inc=(i == n - 1))

    eps_sb = sb("eps", [128, 2], F32)
    tr.op(POOL, lambda: nc.gpsimd.memset(eps_sb[:, 0:1], RMS_EPS), writes=[cbufs[2]])
    tr.op(POOL, lambda: nc.gpsimd.memset(eps_sb[:, 1:2], LN_EPS), writes=[cbufs[2]])

    lnpar = [0]
    cvpar = [0]

    def ln_chain(psv, psvb, l, c):
        r = lnpar[0]
        lnpar[0] = (r + 1) % 4
        vs, vb = v_sb[r], v_buf[r]
        vsq_sb, vsq_buf = vsq_sbs[r], vsq_bufs[r]
        st, stb = st_sb[r], st_buf[r]
        tr.op(ACT, lambda: nc.scalar.activation(out=vs, in_=psv[:, 0:384], func=AF.Gelu), reads=[psvb], writes=[vb])
        v3 = vs.rearrange("p (h d) -> p h d", h=NHEAD)
        tr.op(DVE, lambda: nc.vector.tensor_reduce(out=st[:, 0, :], in_=v3, axis=AX.X, op=ALU.add),
              reads=[vb], writes=[stb[0]])
        tr.op(ACT, lambda: nc.scalar.activation(out=vsq_sb, in_=vs, func=AF.Square), reads=[vb], writes=[vsq_buf])
        tr.op(DVE, lambda: nc.vector.tensor_reduce(out=st[:, 1, :], in_=vsq_sb.rearrange("p (h d) -> p h d", h=NHEAD),
                                                   axis=AX.X, op=ALU.add), reads=[vsq_buf], writes=[stb[1]])
        tr.op(DVE, lambda: nc.vector.tensor_scalar(out=st[:, 2, :], in0=st[:, 0, :], scalar1=1.0 / 64.0, scalar2=None,
                                                   op0=ALU.mult), reads=[stb[0]], writes=[stb[2]])
        tr.op(DVE, lambda: nc.vector.tensor_tensor(out=st[:, 3, :], in0=st[:, 2, :], in1=st[:, 2, :], op=ALU.mult),
              reads=[stb[2]], writes=[stb[3]])
        tr.op(DVE, lambda: nc.vector.tensor_scalar(out=st[:, 4, :], in0=st[:, 1, :], scalar1=1.0 / 64.0, scalar2=LN_EPS,
                                                   op0=ALU.mult, op1=ALU.add), reads=[stb[1]], writes=[stb[4]])
        tr.op(DVE, lambda: nc.vector.tensor_tensor(out=st[:, 4, :], in0=st[:, 4, :], in1=st[:, 3, :], op=ALU.subtract),
              reads=[stb[3]], writes=[stb[4]])
        tr.op(POOL, lambda: nc.gpsimd.tensor_tensor(out=st[:, 4, :], in0=st[:, 4, :], in1=mhalf_sb[:, 0:NHEAD], op=ALU.pow),
              reads=[cbufs[2]], writes=[stb[4]])
        mean_bc = st[:, 2, :].unsqueeze(2).broadcast_to([128, NHEAD, 64])
        rstd_bc = st[:, 4, :].unsqueeze(2).broadcast_to([128, NHEAD, 64])
        vsq3 = vsq_sb.rearrange("p (h d) -> p h d", h=NHEAD)
        lng3 = small_sb[:, l, C_LNG:C_LNG + 384].rearrange("p (h d) -> p h d", h=NHEAD)
        tr.op(DVE, lambda: nc.vector.tensor_tensor(out=v3, in0=v3, in1=mean_bc, op=ALU.subtract),
              reads=[stb[2]], writes=[vb])
        tr.op(DVE, lambda: nc.vector.tensor_tensor(out=vsq3, in0=lng3, in1=rstd_bc, op=ALU.mult),
              reads=[stb[4], small_buf], writes=[vsq_buf])
        tr.op(DVE, lambda: nc.vector.tensor_tensor(out=vs, in0=vs, in1=vsq_sb, op=ALU.mult),
              reads=[vsq_buf], writes=[vb])
        tr.op(DVE, lambda: nc.vector.tensor_tensor(out=cur.vn_sb[c], in0=vs, in1=small_sb[:, l, C_LNB:C_LNB + 384], op=ALU.add),
              reads=[vb, small_buf], writes=[cur.vn_buf[c]])
        return c

    def mix_chunk(c, r, l):
        pm, pmb = ps_alloc()
        for j in range(3):
            for e in range(2):
                hd = 2 * j + e
                o = pm[e * 64:(e + 1) * 64, j * 128:(j + 1) * 128]
                tr.mm(lambda o=o, hd=hd: nc.tensor.matmul(o, lhsT=cur.vn_sb[r][:, hd * 64:(hd + 1) * 64],
                                                          rhs=wt_sb[:, l, hd * 128:(hd + 1) * 128],
                                                          start=True, stop=False, skip_group_check=True),
                      reads=[cur.vn_buf[r], wt_buf], writes=[pmb])
                last = (j == 2 and e == 1)
                tr.mm(lambda o=o, hd=hd: nc.tensor.matmul(o, lhsT=sel_sb, rhs=bshl_sb[:, l, hd * 128:(hd + 1) * 128],
                                                          start=False, stop=True, skip_group_check=True),
                      reads=[bshl_buf, cbufs[1]], writes=[pmb], inc=last)
        mr = mixpar[0]
        mixpar[0] ^= 1
        tr.op(ACT, lambda: nc.scalar.activation(out=mxs_sb[mr], in_=pm[:, 0:384], func=AF.Copy), reads=[pmb], writes=[mxs_buf[mr]])
        tr.op(DVE, lambda: nc.vector.tensor_tensor(
            out=cur.ycat_sb[:, 0:3, c * 128:(c + 1) * 128],
            in0=mxs_sb[mr].rearrange("p (j t) -> p j t", j=3),
            in1=cur.u_sb[:, 0:3, c * 128:(c + 1) * 128], op=ALU.mult),
            reads=[mxs_buf[mr]] + cur.u_buf, writes=cur.ycat_buf[0:3])

    mixpar = [0]

    def load_x(ti, t0):
        xi_ = ti % NXB
        T_ = tiles[ti] * 128
        tr.dma(SP, sl_x[xi_], lambda: nc.sync.dma_start(
            out=x_sb[xi_][:, :, :T_], in_=xT[:, :, t0:t0 + T_].rearrange("k p t -> p k t")), writes=x_buf[xi_])

    def tile_gen(ti, pi, tok0):
        nch = tiles[ti]
        T = nch * 128
        xi = ti % NXB
        xs, xb = x_sb[xi], x_buf[xi]
        r = norm(xi, T, 0, C_GMIX)
        norm_apply_h(xi, T, 0, C_GMIX, r)
        for l in range(L):
            yield
            if cur.hs == 0:
                load_ple(pi, l)
            tr.dma(POOL, cur.sl_p, lambda: nc.gpsimd.dma_start(
                out=cur.p_sb[:, :, :T], in_=pT[l, :, :, tok0:tok0 + T].rearrange("k p t -> p k t")), writes=[cur.p_buf])
            wp, wb = get_piece(pi, l, 0)
            w4 = wp[:, 0:3072].rearrange("p (m k j) -> p m k j", m=3, k=KC)
            for m in range(3):
                pu, pub = ps_alloc()
                group(pu[:, :T], pub, [(w4[:, m, k, :], cur.h_sb[:, k, :T]) for k in range(KC)],
                      [[wb, cur.h_buf[k]] for k in range(KC)])
                tr.op(ACT, lambda m=m, pu=pu: nc.scalar.activation(out=cur.u_sb[:, m, :T], in_=pu[:, :T], func=AF.Gelu),
                      reads=[pub], writes=[cur.u_buf[m]])
            yield
            wp, wb = get_piece(pi, l, 1)
            wv = wp[:, 0:3072].rearrange("p (k j) -> p k j", k=KC)
            vn_idx = []
            pend_mix = []
            for c in range(nch):
                pv, pvb = ps_alloc()
                group(pv[:, 0:384], pvb, [(cur.h_sb[:, k, c * 128:(c + 1) * 128], wv[:, k, :]) for k in range(KC)],
                      [[wb, cur.h_buf[k]] for k in range(KC)])
                pend_mix.append((c, pv, pvb))
            ln_done = {}

            def do_ln(c):
                cc_, pv_, pvb_ = pend_mix[c]
                ln_done[c] = ln_chain(pv_, pvb_, l, c)

            do_ln(0)
            if nch > 1:
                do_ln(1)
            yield
            wp, wb = get_piece(pi, l, 2)
            w4 = wp[:, 0:2048].rearrange("p (m k j) -> p m k j", m=2, k=KC)
            has_fix = (tok0 <= HALO * 128 < tok0 + T)
            q0 = 16 + HALO * 128 - tok0
            for cc in range(2):
                pz, pzb = ps_alloc()
                group(pz[:, :T], pzb, [(w4[:, cc, k, :], cur.h_sb[:, k, :T]) for k in range(KC)],
                      [[wb, cur.h_buf[k]] for k in range(KC)])
                zs, zb = zc_sb[cur.hs * 2 + cc], zc_buf[cur.hs * 2 + cc]
                tr.op(ACT, lambda: nc.scalar.activation(out=zs[:, 16:16 + T], in_=pz[:, :T], func=AF.Copy),
                      reads=[pzb], writes=[zb])
                tr.op(DVE, lambda cc=cc: nc.vector.tensor_copy(out=zs[:, 0:16], in_=zcar_sb[:, l, cc, :]),
                      reads=[zcar_buf[l][cc]], writes=[zb])
                tr.op(DVE, lambda cc=cc: nc.vector.tensor_copy(out=zcar_sb[:, l, cc, :], in_=zs[:, T:T + 16]),
                      reads=[zb], writes=[zcar_buf[l][cc]])
                W_ = 16 + T
                tr.op(POOL, lambda: nc.gpsimd.tensor_tensor(out=S_sb[0][:, 1:W_], in0=zs[:, 1:W_], in1=zs[:, 0:W_ - 1], op=ALU.add),
                      reads=[zb], writes=[S_buf[0]])
                if cc == 0:
                    tr.op(POOL, lambda: nc.gpsimd.tensor_tensor(out=S_sb[1][64:128, 3:W_], in0=S_sb[0][64:128, 3:W_],
                                                                in1=S_sb[0][64:128, 1:W_ - 2], op=ALU.add),
                          reads=[S_buf[0]], writes=[S_buf[1]])
                    srcs = [(0, slice(0, 64)), (1, slice(64, 128))]
                else:
                    tr.op(POOL, lambda: nc.gpsimd.tensor_tensor(out=S_sb[1][:, 3:W_], in0=S_sb[0][:, 3:W_],
                                                                in1=S_sb[0][:, 1:W_ - 2], op=ALU.add),
                          reads=[S_buf[0]], writes=[S_buf[1]])
                    tr.op(POOL, lambda: nc.gpsimd.tensor_tensor(out=S_sb[2][:, 7:W_], in0=S_sb[1][:, 7:W_],
                                                                in1=S_sb[1][:, 3:W_ - 4], op=ALU.add),
                          reads=[S_buf[1]], writes=[S_buf[2]])
                    tr.op(POOL, lambda: nc.gpsimd.tensor_tensor(out=S_sb[3][64:128, 15:W_], in0=S_sb[2][64:128, 15:W_],
                                                                in1=S_sb[2][64:128, 7:W_ - 8], op=ALU.add),
                          reads=[S_buf[2]], writes=[S_buf[3]])
                    srcs = [(2, slice(0, 64)), (3, slice(64, 128))]
                for si, psl in srcs:
                    if has_fix:
                        tr.op(DVE, lambda si=si, psl=psl, cc=cc: nc.vector.tensor_tensor(
                            out=S_sb[si][psl, q0:q0 + 16], in0=S_sb[si][psl, q0:q0 + 16],
                            in1=small_sb[psl, l, C_CORR + cc * 16:C_CORR + (cc + 1) * 16], op=ALU.mult),
                            reads=[small_buf], writes=[S_buf[si]])
                    tr.op(DVE, lambda si=si, psl=psl, cc=cc: nc.vector.scalar_tensor_tensor(
                        out=cur.pl_sb[cc][psl, :T], in0=S_sb[si][psl, 16:16 + T], scalar=small_sb[psl, l, C_IW + cc:C_IW + cc + 1],
                        in1=zs[psl, 16:16 + T], op0=ALU.mult, op1=ALU.subtract),
                        reads=[S_buf[si], zb, small_buf], writes=[cur.pl_buf[cc]])
            for c in range(3):
                yield
                wp, wb = get_piece(pi, l, 3 + c)
                w4 = wp[:, 0:3072].rearrange("p (m k j) -> p m k j", m=3, k=KC)
                pss = []
                for m in range(3):
                    pz, pzb = ps_alloc()
                    group(pz[:, :T], pzb, [(w4[:, m, k, :], cur.h_sb[:, k, :T]) for k in range(KC)],
                          [[wb, cur.h_buf[k]] for k in range(KC)])
                    pss.append((pz, pzb))
                (pz, pzb), (pgc, pgcb), (pgb, pgbb) = pss
                rr = cvpar[0]
                cvpar[0] = (rr + 1) % 4
                cw = lambda k, c=c: small_sb[:, l, C_CW + c * 3 + k:C_CW + c * 3 + k + 1]
                tr.op(ACT, lambda: nc.scalar.activation(out=zbs_sb[rr][:, :T], in_=pz[:, :T], func=AF.Copy),
                      reads=[pzb], writes=[zbs_buf[rr]])
                tr.op(ACT, lambda: nc.scalar.activation(out=gcs_sb[rr][:, :T], in_=pgc[:, :T], func=AF.Copy),
                      reads=[pgcb], writes=[gcs_buf[rr]])
                tr.op(ACT, lambda: nc.scalar.activation(out=gbs_sb[rr][:, :T], in_=pgb[:, :T], func=AF.Copy),
                      reads=[pgbb], writes=[gbs_buf[rr]])
                tr.op(DVE, lambda: nc.vector.tensor_tensor(out=hb_sb[rr % 2][:, 2:2 + T], in0=zbs_sb[rr][:, :T], in1=gcs_sb[rr][:, :T],
                                                           op=ALU.mult), reads=[zbs_buf[rr], gcs_buf[rr]], writes=[hb_buf[rr % 2]])
                tr.op(DVE, lambda c=c: nc.vector.tensor_copy(out=hb_sb[rr % 2][:, 0:2], in_=ccar_sb[:, l, c, :]),
                      reads=[ccar_buf[l][c]], writes=[hb_buf[rr % 2]])
                tr.op(DVE, lambda: nc.vector.tensor_scalar(out=a0_sb[rr % 2][:, :T], in0=hb_sb[rr % 2][:, 2:2 + T], scalar1=cw(2), scalar2=None,
                                                           op0=ALU.mult), reads=[hb_buf[rr % 2], small_buf], writes=[a0_buf[rr % 2]])
                tr.op(DVE, lambda c=c: nc.vector.tensor_copy(out=ccar_sb[:, l, c, :], in_=hb_sb[rr % 2][:, T:T + 2]),
                      reads=[hb_buf[rr % 2]], writes=[ccar_buf[l][c]])
                tr.op(DVE, lambda: nc.vector.scalar_tensor_tensor(out=a0_sb[rr % 2][:, :T], in0=hb_sb[rr % 2][:, 1:1 + T], scalar=cw(1),
                                                                  in1=a0_sb[rr % 2][:, :T], op0=ALU.mult, op1=ALU.add),
                      reads=[hb_buf[rr % 2], small_buf], writes=[a0_buf[rr % 2]])
                tr.op(DVE, lambda: nc.vector.scalar_tensor_tensor(out=a0_sb[rr % 2][:, :T], in0=hb_sb[rr % 2][:, 0:T], scalar=cw(0),
                                                                  in1=a0_sb[rr % 2][:, :T], op0=ALU.mult, op1=ALU.add),
                      reads=[hb_buf[rr % 2], small_buf], writes=[a0_buf[rr % 2]])
                tr.op(DVE, lambda c=c: nc.vector.tensor_tensor(out=cur.ycat_sb[:, 3 + c, :T], in0=a0_sb[rr % 2][:, :T], in1=gbs_sb[rr][:, :T],
                                                               op=ALU.mult), reads=[a0_buf[rr % 2], gbs_buf[rr]], writes=[cur.ycat_buf[3 + c]])
                if c == 0:
                    for c2 in range(2, nch):
                        do_ln(c2)
                if c >= 1 and c - 1 < nch:
                    mix_chunk(c - 1, ln_done[c - 1], l)
                if c == 2:
                    for c2 in range(2, min(nch, 3)):
                        mix_chunk(c2, ln_done[c2], l)
            if nch > 3:
                mix_chunk(3, ln_done[3], l)
            korder = [3, 4, 5, 0, 1, 2, 6, 7]
            for i in range(2):
                yield
                if i == 0:
                    for cc in range(2):
                        pp, ppb = ps_alloc()
                        tr.mm(lambda cc=cc, pp=pp: nc.tensor.matmul(pp[:, :T], lhsT=wpb_sb[:, l, cc * 128:(cc + 1) * 128],
                                                                    rhs=cur.pl_sb[cc][:, :T], start=True, stop=True),
                              reads=[wpb_buf, cur.pl_buf[cc]], writes=[ppb], inc=True)
                        tr.op(ACT, lambda cc=cc, pp=pp: nc.scalar.activation(out=cur.ycat_sb[:, 6 + cc, :T], in_=pp[:, :T], func=AF.Copy,
                                                                             scale=small_sb[:, l, C_PSC + cc:C_PSC + cc + 1]),
                              reads=[ppb, small_buf], writes=[cur.ycat_buf[6 + cc]])
                wp, wb = get_piece(pi, l, 6 + i)
                w4 = wp.rearrange("p (m k j) -> p m k j", m=4, k=KC)
                for mi in range(4):
                    m = i * 4 + mi
                    po, pob = ps_alloc()
                    group(po[:, :T], pob, [(w4[:, mi, k, :], cur.ycat_sb[:, k, :T]) for k in korder],
                          [[wb, cur.ycat_buf[k]] for k in korder])
                    tr.op(DVE, lambda m=m, po=po: nc.vector.tensor_tensor(out=xs[:, m, :T], in0=xs[:, m, :T], in1=po[:, :T], op=ALU.add),
                          reads=[pob], writes=[xb[m]])
                    norm_sq(xi, T, m)
            r = norm(xi, T, l, C_GFF, skip_sq=True)
            norm_apply_h(xi, T, l, C_GFF, r)
            for i in range(8):
                yield
                wp, wb = get_piece(pi, l, 8 + i)
                w4 = wp.rearrange("p (m k j) -> p m k j", m=4, k=KC)
                for mi in range(4):
                    j = i * 4 + mi
                    pf, pfb = ps_alloc()
                    group(pf[:, :T], pfb, [(w4[:, mi, k, :], cur.h_sb[:, k, :T]) for k in range(KC)],
                          [[wb, cur.h_buf[k]] for k in range(KC)])
                    rr = j % 3
                    tr.op(ACT, lambda pf=pf, rr=rr: nc.scalar.activation(out=rl_sb[rr][:, :T], in_=pf[:, :T], func=AF.Relu),
                          reads=[pfb], writes=[rl_buf[rr]])
                    tr.op(POOL, lambda j=j, rr=rr: nc.gpsimd.tensor_tensor(out=cur.hid_sb[:, j, :T], in0=rl_sb[rr][:, :T], in1=rl_sb[rr][:, :T],
                                                                        op=ALU.mult), reads=[rl_buf[rr]], writes=[cur.hid_buf[j]])
            for m in range(8):
                yield
                wp, wb = get_piece(pi, l, 16 + m)
                w3 = wp.rearrange("p (k j) -> p k j", k=32)
                po, pob = ps_alloc()
                group(po[:, :T], pob, [(w3[:, k, :], cur.hid_sb[:, k, :T]) for k in range(32)],
                      [[wb, cur.hid_buf[k]] for k in range(32)])
                tr.op(DVE, lambda m=m, po=po: nc.vector.tensor_tensor(out=xs[:, m, :T], in0=xs[:, m, :T], in1=po[:, :T], op=ALU.add),
                      reads=[pob], writes=[xb[m]])
                norm_sq(xi, T, m)
            r = norm(xi, T, l, C_GPLE, skip_sq=True)
            norm_apply_h(xi, T, l, C_GPLE, r)
            wpl, wplb = wple_sb, wple_buf
            wpl4 = wpl[:, 0:2048].rearrange("p (m k j) -> p m k j", m=8, k=2)
            for i in range(2):
                yield
                wp, wb = get_piece(pi, l, 24 + i)
                w4 = wp.rearrange("p (m k j) -> p m k j", m=4, k=KC)
                for mi in range(4):
                    m = i * 4 + mi
                    pg, pgb_ = ps_alloc()
                    group(pg[:, :T], pgb_, [(w4[:, mi, k, :], cur.h_sb[:, k, :T]) for k in range(KC)],
                          [[wb, cur.h_buf[k]] for k in range(KC)])
                    pq, pqb = ps_alloc()
                    group(pq[:, :T], pqb, [(wpl4[:, m, k, :], cur.p_sb[:, k, :T]) for k in range(2)],
                          [[wplb, cur.p_buf] for k in range(2)])
                    rr = m % 2
                    tr.op(ACT, lambda pg=pg, rr=rr: nc.scalar.activation(out=th_sb[rr][:, :T], in_=pg[:, :T], func=AF.Tanh, scale=0.5),
                          reads=[pgb_], writes=[th_buf[rr]])
                    tr.op(DVE, lambda pq=pq, rr=rr: nc.vector.scalar_tensor_tensor(out=t1_sb[rr][:, :T], in0=th_sb[rr][:, :T], scalar=1.0,
                                                                                   in1=pq[:, :T], op0=ALU.add, op1=ALU.mult),
                          reads=[th_buf[rr], pqb], writes=[t1_buf[rr]])
                    tr.op(DVE, lambda m=m, rr=rr: nc.vector.scalar_tensor_tensor(out=xs[:, m, :T], in0=t1_sb[rr][:, :T], scalar=0.5,
                                                                                  in1=xs[:, m, :T], op0=ALU.mult, op1=ALU.add),
                          reads=[t1_buf[rr]], writes=[xb[m]])
            if l + 1 < L:
                r = norm(xi, T, l + 1, C_GMIX)
                norm_apply_h(xi, T, l + 1, C_GMIX, r)
        r = norm(xi, T, L - 1, C_GFIN)
        for k in range(KC):
            tr.op(DVE, lambda k=k: nc.vector.scalar_tensor_tensor(
                out=xs[:, k, :T], in0=xs[:, k, :T], scalar=small_sb[:, L - 1, C_GFIN + k:C_GFIN + k + 1],
                in1=ms_sb[r][:, :T], op0=ALU.mult, op1=ALU.mult),
                reads=[ms_buf[r], small_buf], writes=[xb[k]])
        lo = max(tok0, HALO * 128)
        hi = tok0 + T
        if hi > lo:
            tr.dma(SP, sl_o[xi], lambda: nc.sync.dma_start(
                out=outT[:, :, lo - HALO * 128:hi - HALO * 128].rearrange("k p t -> p k t"),
                in_=xs[:, :, lo - tok0:hi - tok0]), reads=xb)

    tok_starts = [0]
    for n_ in tiles:
        tok_starts.append(tok_starts[-1] + n_ * 128)
    pairs = [list(range(i, min(i + NHS, len(tiles)))) for i in range(0, len(tiles), NHS)]
    for t_ in pairs[0]:
        load_x(t_, tok_starts[t_])
    for pi, pr in enumerate(pairs):
        if pi + 1 < len(pairs):
            for t_ in pairs[pi + 1]:
                load_x(t_, tok_starts[t_])
        gens = [(tile_gen(t_, pi, tok_starts[t_]), halves[k_]) for k_, t_ in enumerate(pr)]
        alive = list(gens)
        while alive:
            for g_ in list(alive):
                cur.__dict__.update(g_[1].__dict__)
                try:
                    next(g_[0])
                except StopIteration:
                    alive.remove(g_)
    for s in sl_o:
        if s.cnt:
            nc.sync.wait_ge(s.sem, s.cnt)
    for s in sl_wb + [sl_wplewb]:
        if s.cnt:
            nc.sync.wait_ge(s.sem, s.cnt)
    return nc


def _default_tiles(nch):
    n = -(-nch // 2)
    base = nch // n
    rem = nch - base * n
    return [base + 1] * rem + [base] * (n - rem)


def kernel(x, p, norm_mix_g, w_in, sgu_w, sgu_b, sgu_ln_g, sgu_ln_b, conv_w, pool_w, pool_scale, w_out,
           norm_ff_g, w_ff1, w_ff2, norm_ple_g, w_ple_gate, w_ple_proj, final_g, _cfg=None):
    cfg = dict(CFG)
    if _cfg:
        cfg.update(_cfg)
    inp = dict(norm_mix_g=norm_mix_g, w_in=w_in, sgu_w=sgu_w, sgu_b=sgu_b, sgu_ln_g=sgu_ln_g, sgu_ln_b=sgu_ln_b,
               conv_w=conv_w, pool_w=pool_w, pool_scale=pool_scale, w_out=w_out, norm_ff_g=norm_ff_g, w_ff1=w_ff1,
               w_ff2=w_ff2, norm_ple_g=norm_ple_g, w_ple_gate=w_ple_gate, w_ple_proj=w_ple_proj, final_g=final_g)
    x = np.asarray(x, np.float32)
    p = np.asarray(p, np.float32)
    L = cfg["L"]
    B, S, _ = x.shape
    nseg = NCORES // B
    SEGC = S // 128 // nseg
    NCH = SEGC + HALO
    tiles = cfg["TILES"] or _default_tiles(NCH)
    wpk = _pack_weights(inp, L)
    bpk = _pack_big(inp, L)
    spk_first = _pack_small(inp, L, True)
    spk_rest = _pack_small(inp, L, False)
    in_maps = []
    for c in range(NCORES):
        b, sg = divmod(c, nseg)
        t0 = sg * SEGC * 128
        xs = np.zeros((NCH * 128, D), np.float32)
        ps = np.zeros((L, NCH * 128, DPLE), np.float32)
        if sg == 0:
            xs[HALO * 128:] = x[b, t0:t0 + SEGC * 128]
            ps[:, HALO * 128:] = p[:L, b, t0:t0 + SEGC * 128]
        else:
            xs[:] = x[b, t0 - HALO * 128:t0 + SEGC * 128]
            ps[:] = p[:L, b, t0 - HALO * 128:t0 + SEGC * 128]
        xTc = np.ascontiguousarray(xs.T).reshape(KC, 128, NCH * 128)
        pTc = np.ascontiguousarray(ps.transpose(0, 2, 1)).reshape(L, 2, 128, NCH * 128)
        in_maps.append({"xT": xTc, "pT": pTc, "wpk": wpk, "spk": spk_first if sg == 0 else spk_rest, "bpk": bpk})
    nc = build_nc(L, NCH, tiles, cfg["NBUF_W"])
    res = run_bass_kernel_spmd(nc, in_maps, core_ids=list(range(NCORES)))
    out = np.zeros((B, S, D), np.float32)
    for c in range(NCORES):
        b, sg = divmod(c, nseg)
        t0 = sg * SEGC * 128
        oT = np.asarray(res.results[c]["outT"]).reshape(D, SEGC * 128)
        out[b, t0:t0 + SEGC * 128] = oT.T
    return out
```

```python
import numpy as np
import concourse.bass as bass
import concourse.mybir as mybir
from concourse.bass_utils import run_bass_kernel_spmd

F32 = mybir.dt.float32
BF16 = mybir.dt.bfloat16
AF = mybir.ActivationFunctionType
ALU = mybir.AluOpType
AX = mybir.AxisListType

D = 1024
KC = 8
DFF = 4096
DPLE = 256
NHEAD = 6
RMS_EPS = 1e-6
LN_EPS = 1e-5
HALO = 2
NCORES = 8

CFG = dict(L=4, SEGC=32, TILES=None, NBUF_W=4)

PIECES = []


def _mk_pieces():
    off = 0

    def add(name, ln):
        nonlocal off
        PIECES.append((name, off, ln))
        off += ln
    add("U", 3072)
    add("V", 3072)
    add("POOL", 2048)
    for c in range(3):
        add("CONV%d" % c, 3072)
    for i in range(2):
        add("OUT%d" % i, 4096)
    for i in range(8):
        add("FF1_%d" % i, 4096)
    for i in range(8):
        add("FF2_%d" % i, 4096)
    add("PLE", 2048)
    for i in range(2):
        add("GATE%d" % i, 4096)
    return off


E_W = _mk_pieces()
NPIECE = len(PIECES)
WSLOT = 4096
PLE_IDX = [i for i, pc in enumerate(PIECES) if pc[0] == "PLE"][0]
RING = [i for i in range(NPIECE) if i != PLE_IDX]
NRING = len(RING)

C_GMIX, C_GFF, C_GPLE, C_GFIN, C_CW, C_PSC, C_IW, C_CORR, C_LNG, C_LNB = 0, 8, 16, 24, 32, 41, 43, 45, 77, 461
S_SMALL = 845
S_BIG = 1792


def _chunk_block(W, col0, ncol=128):
    k = W.shape[0] // 128
    blk = W[:, col0:col0 + ncol].reshape(k, 128, ncol)
    return np.ascontiguousarray(blk.transpose(1, 0, 2)).reshape(128, k * ncol)


def _pack_weights(inp, L):
    wpk = np.zeros((L, 128, E_W), np.float32)
    for l in range(L):
        w_in = np.asarray(inp["w_in"][l])
        w_out = np.asarray(inp["w_out"][l])
        w1 = np.asarray(inp["w_ff1"][l])
        w2 = np.asarray(inp["w_ff2"][l])
        wg = np.asarray(inp["w_ple_gate"][l])
        wp = np.asarray(inp["w_ple_proj"][l])
        parts = []
        parts += [_chunk_block(w_in, c * 128) for c in range(3)]
        parts += [_chunk_block(w_in, 384, 384)]
        parts += [_chunk_block(w_in, 1920), _chunk_block(w_in, 2048)]
        for c in range(3):
            parts += [_chunk_block(w_in, 768 + c * 128), _chunk_block(w_in, 1536 + c * 128),
                      _chunk_block(w_in, 1152 + c * 128)]
        parts += [_chunk_block(w_out, m * 128) for m in range(8)]
        parts += [_chunk_block(w1, j * 128) for j in range(32)]
        parts += [_chunk_block(w2, m * 128) for m in range(8)]
        parts += [_chunk_block(wp, m * 128) for m in range(8)]
        parts += [_chunk_block(wg, m * 128) for m in range(8)]
        row = np.concatenate(parts, axis=1)
        assert row.shape == (128, E_W), row.shape
        wpk[l] = row
    return wpk


def _pack_small(inp, L, first_seg):
    spk = np.zeros((L, 128, S_SMALL), np.float32)
    for l in range(L):
        s = spk[l]
        s[:, C_GMIX:C_GMIX + 8] = np.asarray(inp["norm_mix_g"][l]).reshape(8, 128).T
        s[:, C_GFF:C_GFF + 8] = np.asarray(inp["norm_ff_g"][l]).reshape(8, 128).T
        s[:, C_GPLE:C_GPLE + 8] = np.asarray(inp["norm_ple_g"][l]).reshape(8, 128).T
        s[:, C_GFIN:C_GFIN + 8] = np.asarray(inp["final_g"]).reshape(8, 128).T
        cw = np.asarray(inp["conv_w"][l])
        for c in range(3):
            for k in range(3):
                s[:, C_CW + c * 3 + k] = cw[k, c * 128:(c + 1) * 128]
        s[:, C_PSC:C_PSC + 2] = np.asarray(inp["pool_scale"][l]).reshape(2, 128).T
        wins = np.array([[2, 4], [8, 16]], np.float32)
        for cc in range(2):
            for hf in range(2):
                win = wins[cc, hf]
                s[hf * 64:(hf + 1) * 64, C_IW + cc] = 1.0 / win
                for i in range(16):
                    v = win / min(i + 1.0, win) if first_seg else 1.0
                    s[hf * 64:(hf + 1) * 64, C_CORR + cc * 16 + i] = v
        s[:, C_LNG:C_LNG + 384] = np.asarray(inp["sgu_ln_g"][l])[None, :]
        s[:, C_LNB:C_LNB + 384] = np.asarray(inp["sgu_ln_b"][l])[None, :]
    return spk


def _pack_big(inp, L):
    bpk = np.zeros((L, 128, S_BIG), np.float32)
    for l in range(L):
        w = np.asarray(inp["sgu_w"][l])
        bpk[l, :, 0:768] = np.ascontiguousarray(w.transpose(2, 0, 1)).reshape(128, 768)
        bs = np.asarray(inp["sgu_b"][l]).reshape(768)
        bpk[l, 0, 768:1536] = bs
        bpk[l, 32, 768:1536] = bs
        wpool = np.asarray(inp["pool_w"][l])
        for cc in range(2):
            blk = np.zeros((128, 128), np.float32)
            blk[0:64, 0:64] = wpool[2 * cc]
            blk[64:128, 64:128] = wpool[2 * cc + 1]
            bpk[l, :, 1536 + cc * 128:1536 + (cc + 1) * 128] = blk
    return bpk


class Buf:
    __slots__ = ("name", "w", "r")

    def __init__(self, name):
        self.name = name
        self.w = None
        self.r = {}


class Eng:
    def __init__(self, nc, name, h, self_sync):
        self.name = name
        self.h = h
        self.sem = nc.alloc_semaphore("s_" + name)
        self.cnt = 0
        self.seen = {}
        self.self_sync = self_sync
        self.pend_r = []
        self.pend_w = []


class Slot:
    def __init__(self, nc, name):
        self.name = name
        self.sem = nc.alloc_semaphore("d_" + name)
        self.cnt = 0


class TR:
    def __init__(self, nc):
        self.nc = nc
        self.pe = Eng(nc, "pe", nc.tensor, False)
        self.act = Eng(nc, "act", nc.scalar, True)
        self.dve = Eng(nc, "dve", nc.vector, True)
        self.pool = Eng(nc, "pool", nc.gpsimd, True)
        self.sp = Eng(nc, "sp", nc.sync, False)

    @staticmethod
    def _deps(reads, writes):
        need = {}
        for b in reads:
            if b.w is not None:
                o, c = b.w
                if need.get(o, 0) < c:
                    need[o] = c
        for b in writes:
            if b.w is not None:
                o, c = b.w
                if need.get(o, 0) < c:
                    need[o] = c
            for o, c in b.r.items():
                if need.get(o, 0) < c:
                    need[o] = c
        return need

    @staticmethod
    def _wait(eng, need):
        for o, c in need.items():
            if o is eng and not eng.self_sync:
                continue
            if eng.seen.get(o, 0) >= c:
                continue
            eng.h.wait_ge(o.sem, c)
            eng.seen[o] = c

    def op(self, eng, fn, reads=(), writes=()):
        self._wait(eng, self._deps(reads, writes))
        ins = fn()
        eng.cnt += 1
        ins.then_inc(eng.sem, 1)
        for b in reads:
            b.r[eng] = eng.cnt
        for b in writes:
            b.w = (eng, eng.cnt)
            b.r = {}
        return ins

    def mm(self, fn, reads=(), writes=(), inc=False):
        eng = self.pe
        self._wait(eng, self._deps(reads, writes))
        ins = fn()
        eng.pend_r.extend(reads)
        eng.pend_w.extend(writes)
        if inc:
            eng.cnt += 1
            ins.then_inc(eng.sem, 1)
            for b in eng.pend_r:
                b.r[eng] = eng.cnt
            for b in eng.pend_w:
                b.w = (eng, eng.cnt)
                b.r = {}
            eng.pend_r = []
            eng.pend_w = []
        return ins

    def dma(self, q, slot, fn, reads=(), writes=()):
        self._wait(q, self._deps(reads, writes))
        ins = fn()
        slot.cnt += 16
        ins.then_inc(slot.sem, 16)
        for b in reads:
            b.r[slot] = slot.cnt
        for b in writes:
            b.w = (slot, slot.cnt)
            b.r = {}
        return ins


def build_nc(L, NCH, tiles, nbuf_w=3):
    assert sum(tiles) == NCH and max(tiles) <= 4 and len(set(tiles[i] for i in range(0, len(tiles) - len(tiles) % 2))) <= 1
    NTOK = NCH * 128
    NMAIN = (NCH - HALO) * 128
    nc = bass.Bass("TRN2", target_bir_lowering=False)
    tr = TR(nc)
    PE, ACT, DVE, POOL, SP = tr.pe, tr.act, tr.dve, tr.pool, tr.sp

    xT = nc.dram_tensor("xT", [KC, 128, NTOK], F32, kind="ExternalInput").ap()
    pT = nc.dram_tensor("pT", [L, 2, 128, NTOK], F32, kind="ExternalInput").ap()
    wpk = nc.dram_tensor("wpk", [L, 128, E_W], F32, kind="ExternalInput").ap()
    spk = nc.dram_tensor("spk", [L, 128, S_SMALL], F32, kind="ExternalInput").ap()
    bpk = nc.dram_tensor("bpk", [L, 128, S_BIG], F32, kind="ExternalInput").ap()
    outT = nc.dram_tensor("outT", [KC, 128, NMAIN], F32, kind="ExternalOutput").ap()
    wbf = nc.dram_tensor("wbf", [L, 128, E_W], BF16, kind="Internal").ap()

    def sb(name, shape, dt):
        return nc.alloc_sbuf_tensor(name, shape, dt).ap()

    NXB = 4
    TM = max(tiles) * 128
    NHS = 2
    x_sb = [sb("x%d" % i, [128, KC, TM], F32) for i in range(NXB)]
    x_buf = [[Buf("x%d_%d" % (i, k)) for k in range(KC)] for i in range(NXB)]

    class Half:
        pass

    halves = []
    for hs in range(NHS):
        H = Half()
        H.hs = hs
        H.h_sb = sb("h%d" % hs, [128, KC, TM], BF16)
        H.h_buf = [Buf("h%d_%d" % (hs, k)) for k in range(KC)]
        H.hid_sb = sb("hid%d" % hs, [128, 32, TM], BF16)
        H.hid_buf = [Buf("hid%d_%d" % (hs, k)) for k in range(32)]
        H.ycat_sb = H.hid_sb[:, 0:KC, :]
        H.ycat_buf = H.hid_buf[0:KC]
        H.u_sb = sb("u%d" % hs, [128, 3, TM], F32)
        H.u_buf = [Buf("u%d_%d" % (hs, k)) for k in range(3)]
        H.p_sb = sb("pbf%d" % hs, [128, 2, TM], BF16)
        H.p_buf = Buf("pbf%d" % hs)
        H.vn_sb = [sb("vn%d_%d" % (hs, i), [128, 384], BF16) for i in range(max(tiles))]
        H.vn_buf = [Buf("vn%d_%d" % (hs, i)) for i in range(max(tiles))]
        H.sq_sb = [sb("sq%d_%d" % (hs, i), [128, TM], BF16) for i in range(KC)]
        H.sq_buf = [Buf("sq%d_%d" % (hs, i)) for i in range(KC)]
        H.pl_sb = [sb("pl%d_%d" % (hs, i), [128, TM], BF16) for i in range(2)]
        H.pl_buf = [Buf("pl%d_%d" % (hs, i)) for i in range(2)]
        H.sl_p = Slot(nc, "p%d" % hs)
        halves.append(H)
    cur = Half()
    wple_sb = sb("wple", [128, 2048], BF16)
    wple_buf = Buf("wple")
    w_sb = [sb("w%d" % i, [128, WSLOT], BF16) for i in range(nbuf_w)]
    w_buf = [Buf("w%d" % i) for i in range(nbuf_w)]
    small_sb = sb("small", [128, L, S_SMALL], F32)
    small_buf = Buf("small")
    wt_sb = sb("wt", [128, L, 768], BF16)
    bshl_sb = sb("bshl", [128, L, 768], BF16)
    wpb_sb = sb("wpb", [128, L, 256], BF16)
    wt_buf = Buf("wt")
    bshl_buf = Buf("bshl")
    wpb_buf = Buf("wpb")
    stage_sb = x_sb[NXB - 1].rearrange("p k t -> p (k t)")[:, 0:S_BIG]
    assert KC * TM >= S_BIG
    stage_buf = Buf("stage")
    ones_sb = sb("ones", [128, 128], BF16)
    sel_sb = sb("sel", [128, 64], BF16)
    mhalf_sb = sb("mhalf", [128, 8], F32)
    ms_sb = [sb("ms%d" % i, [128, TM], F32) for i in range(2)]
    ms_buf = [Buf("ms%d" % i) for i in range(2)]
    v_sb = [sb("v%d" % i, [128, 384], F32) for i in range(4)]
    v_buf = [Buf("v%d" % i) for i in range(4)]
    vsq_sbs = [sb("vsq%d" % i, [128, 384], F32) for i in range(4)]
    vsq_bufs = [Buf("vsq%d" % i) for i in range(4)]
    st_sb = [sb("st%d" % i, [128, 5, NHEAD], F32) for i in range(4)]
    st_buf = [[Buf("st%d_%d" % (i, j)) for j in range(5)] for i in range(4)]
    gcs_sb = [sb("gcs%d" % i, [128, TM], F32) for i in range(4)]
    gcs_buf = [Buf("gcs%d" % i) for i in range(4)]
    zbs_sb = [sb("zbs%d" % i, [128, TM], F32) for i in range(4)]
    zbs_buf = [Buf("zbs%d" % i) for i in range(4)]
    mxs_sb = [sb("mxs%d" % i, [128, 384], F32) for i in range(2)]
    mxs_buf = [Buf("mxs%d" % i) for i in range(2)]
    stbf_sb = sb("stbf", [128, 768], BF16)
    stbf_buf = Buf("stbf")
    gbs_sb = [sb("gbs%d" % i, [128, TM], F32) for i in range(4)]
    gbs_buf = [Buf("gbs%d" % i) for i in range(4)]
    hb_sb = [sb("hb%d" % i, [128, TM + 2], F32) for i in range(2)]
    hb_buf = [Buf("hb%d" % i) for i in range(2)]
    a0_sb = [sb("a0_%d" % i, [128, TM], F32) for i in range(2)]
    a0_buf = [Buf("a0_%d" % i) for i in range(2)]
    ccar_sb = sb("ccar", [128, L, 3, 2], F32)
    ccar_buf = [[Buf("ccar%d_%d" % (l, c)) for c in range(3)] for l in range(L)]
    zc_sb = [sb("zc%d" % i, [128, TM + 16], F32) for i in range(2 * NHS)]
    zc_buf = [Buf("zc%d" % i) for i in range(2 * NHS)]
    S_sb = [sb("S%d" % i, [128, TM + 16], F32) for i in range(4)]
    S_buf = [Buf("S%d" % i) for i in range(4)]
    zcar_sb = sb("zcar", [128, L, 2, 16], F32)
    zcar_buf = [[Buf("zcar%d_%d" % (l, c)) for c in range(2)] for l in range(L)]
    rl_sb = [sb("rl%d" % i, [128, TM], F32) for i in range(3)]
    rl_buf = [Buf("rl%d" % i) for i in range(3)]
    th_sb, th_buf = rl_sb, rl_buf
    t1_sb, t1_buf = gcs_sb, gcs_buf

    ps_sb = [nc.alloc_psum_tensor("ps%d" % i, [128, 512], F32).ap() for i in range(8)]
    ps_buf = [Buf("ps%d" % i) for i in range(8)]
    ps_next = [0]

    def ps_alloc():
        i = ps_next[0]
        ps_next[0] = (i + 1) % 8
        assert not PE.pend_w, "ps_alloc inside an open accumulation group"
        assert ps_buf[i].w is None or ps_buf[i].r, "psum bank %d still live" % i
        return ps_sb[i], ps_buf[i]

    sl_small = Slot(nc, "small")
    sl_stage = Slot(nc, "stage")
    sl_wple = Slot(nc, "wple")
    sl_wplewb = Slot(nc, "wplewb")
    sl_x = [Slot(nc, "x%d" % i) for i in range(NXB)]
    sl_o = [Slot(nc, "o%d" % i) for i in range(NXB)]
    sl_w = [Slot(nc, "w%d" % i) for i in range(nbuf_w)]
    sl_wb = [Slot(nc, "wb%d" % i) for i in range(nbuf_w)]
    wbf_buf = [[Buf("wbf%d_%d" % (l, j)) for j in range(NPIECE)] for l in range(L)]

    cbufs = [Buf("c_ones"), Buf("c_sel"), Buf("c_mhalf")]
    tr.op(POOL, lambda: nc.gpsimd.memset(ones_sb, 1.0 / 1024.0), writes=[cbufs[0]])
    tr.op(POOL, lambda: nc.gpsimd.memset(sel_sb, 0.0), writes=[cbufs[1]])
    tr.op(POOL, lambda: nc.gpsimd.memset(sel_sb[0:1, :], 1.0), writes=[cbufs[1]])
    tr.op(POOL, lambda: nc.gpsimd.memset(sel_sb[32:33, :], 1.0), writes=[cbufs[1]])
    tr.op(POOL, lambda: nc.gpsimd.memset(mhalf_sb, -0.5), writes=[cbufs[2]])
    allc = []
    for l in range(L):
        allc += ccar_buf[l] + zcar_buf[l]
    tr.op(POOL, lambda: nc.gpsimd.memset(ccar_sb, 0.0), writes=[b for l in range(L) for b in ccar_buf[l]])
    tr.op(POOL, lambda: nc.gpsimd.memset(zcar_sb, 0.0), writes=[b for l in range(L) for b in zcar_buf[l]])
    tr.dma(SP, sl_small, lambda: nc.sync.dma_start(out=small_sb, in_=spk.rearrange("l p s -> p l s")),
           writes=[small_buf])
    for l in range(L):
        tr.dma(SP, sl_stage, lambda l=l: nc.sync.dma_start(out=stage_sb, in_=bpk[l]), writes=[stage_buf])
        tr.op(POOL, lambda l=l: nc.gpsimd.affine_select(
            out=wt_sb[:, l, :].rearrange("p (h t) -> p h t", h=NHEAD),
            in_=stage_sb[:, 0:768].rearrange("p (h t) -> p h t", h=NHEAD),
            pattern=[[0, NHEAD], [1, 128]], compare_op=ALU.is_ge, fill=0.0, base=0,
            channel_multiplier=-1), reads=[stage_buf], writes=[wt_buf])
        tr.op(DVE, lambda: nc.vector.tensor_copy(out=stbf_sb, in_=stage_sb[:, 768:1536]),
              reads=[stage_buf], writes=[stbf_buf])
        tr.op(DVE, lambda l=l: nc.vector.tensor_copy(out=bshl_sb[0:32, l, :], in_=stbf_sb[0:32, :]),
              reads=[stbf_buf], writes=[bshl_buf])
        tr.op(DVE, lambda l=l: nc.vector.tensor_tensor(out=bshl_sb[32:64, l, :], in0=stage_sb[32:64, 768:1536],
                                                       in1=stbf_sb[32:64, :], op=ALU.subtract),
              reads=[stage_buf, stbf_buf], writes=[bshl_buf])
        tr.op(DVE, lambda l=l: nc.vector.tensor_copy(out=bshl_sb[64:128, l, :], in_=stbf_sb[64:128, :]),
              reads=[stbf_buf], writes=[bshl_buf])
        tr.op(DVE, lambda l=l: nc.vector.tensor_copy(out=wpb_sb[:, l, :], in_=stage_sb[:, 1536:1792]),
              reads=[stage_buf], writes=[wpb_buf])

    for b_ in x_buf[NXB - 1]:
        b_.w = stage_buf.w
        b_.r = dict(stage_buf.r)

    npairs = (len(tiles) + NHS - 1) // NHS
    seq = [(pi, l, j) for pi in range(npairs) for l in range(L) for j in RING]
    issued = [0]

    def issue_load(gi):
        pi, l, j = seq[gi]
        s = gi % nbuf_w
        _, off, ln = PIECES[j]
        if pi == 0:
            tr.dma(POOL, sl_w[s], lambda: nc.gpsimd.dma_start(
                out=w_sb[s][:, 0:ln].rearrange("p (a b) -> p a b", b=1024),
                in_=wpk[l, :, off:off + ln].rearrange("p (a b) -> p a b", b=1024)),
                writes=[w_buf[s]])
            if npairs > 1:
                tr.dma(SP, sl_wb[s], lambda: nc.sync.dma_start(out=wbf[l, :, off:off + ln], in_=w_sb[s][:, 0:ln]),
                       reads=[w_buf[s]], writes=[wbf_buf[l][j]])
        else:
            tr.dma(SP, sl_w[s], lambda: nc.sync.dma_start(out=w_sb[s][:, 0:ln], in_=wbf[l, :, off:off + ln]),
                   reads=[wbf_buf[l][j]], writes=[w_buf[s]])

    def get_piece(pi, l, rj):
        gi = (pi * L + l) * NRING + rj
        while issued[0] < len(seq) and issued[0] <= gi + nbuf_w - 1:
            issue_load(issued[0])
            issued[0] += 1
        s = gi % nbuf_w
        return w_sb[s], w_buf[s]

    def load_ple(pi, l):
        _, off, ln = PIECES[PLE_IDX]
        if pi == 0:
            tr.dma(POOL, sl_wple, lambda: nc.gpsimd.dma_start(
                out=wple_sb.rearrange("p (a b) -> p a b", b=1024),
                in_=wpk[l, :, off:off + ln].rearrange("p (a b) -> p a b", b=1024)), writes=[wple_buf])
            if npairs > 1:
                tr.dma(SP, sl_wplewb, lambda: nc.sync.dma_start(out=wbf[l, :, off:off + ln], in_=wple_sb),
                       reads=[wple_buf], writes=[wbf_buf[l][PLE_IDX]])
        else:
            tr.dma(SP, sl_wple, lambda: nc.sync.dma_start(out=wple_sb, in_=wbf[l, :, off:off + ln]),
                   reads=[wbf_buf[l][PLE_IDX]], writes=[wple_buf])


    def norm_sq(xi, T, k):
        tr.op(ACT, lambda: nc.scalar.activation(out=cur.sq_sb[k][:, :T], in_=x_sb[xi][:, k, :T], func=AF.Square),
              reads=[x_buf[xi][k]], writes=[cur.sq_buf[k]])

    def norm(xi, T, l, gcol, skip_sq=False):
        xs, xb = x_sb[xi], x_buf[xi]
        pst, psb = ps_alloc()
        if not skip_sq:
            for k in range(KC):
                norm_sq(xi, T, k)
        for k in range(KC):
            tr.mm(lambda k=k: nc.tensor.matmul(pst[:, :T], lhsT=ones_sb, rhs=cur.sq_sb[k][:, :T],
                                               start=(k == 0), stop=(k == KC - 1)),
                  reads=[cur.sq_buf[k], cbufs[0]], writes=[psb], inc=True)
        r = norm.par
        norm.par ^= 1
        tr.op(ACT, lambda: nc.scalar.activation(out=ms_sb[r][:, :T], in_=pst[:, :T], func=AF.Sqrt,
                                                bias=eps_sb[:, 0:1], scale=1.0),
              reads=[psb, cbufs[2]], writes=[ms_buf[r]])
        tr.op(DVE, lambda: nc.vector.reciprocal(out=ms_sb[r][:, :T], in_=ms_sb[r][:, :T]),
              reads=[], writes=[ms_buf[r]])
        return r

    norm.par = 0

    def norm_apply_h(xi, T, l, gcol, r):
        xs, xb = x_sb[xi], x_buf[xi]
        for k in range(KC):
            tr.op(DVE, lambda k=k: nc.vector.scalar_tensor_tensor(
                out=cur.h_sb[:, k, :T], in0=xs[:, k, :T], scalar=small_sb[:, l, gcol + k:gcol + k + 1],
                in1=ms_sb[r][:, :T], op0=ALU.mult, op1=ALU.mult),
                reads=[xb[k], ms_buf[r], small_buf], writes=[cur.h_buf[k]])

    def group(out_ap, out_buf, pairs, reads_each):
        n = len(pairs)
        for i, (lt, rh) in enumerate(pairs):
            tr.mm(lambda lt=lt, rh=rh, i=i: nc.tensor.matmul(out_ap, lhsT=lt, rhs=rh, start=(i == 0), stop=(i == n - 1)),
                  reads=reads_each[i], writes=[out_buf], inc=(i == n - 1))

    eps_sb = sb("eps", [128, 2], F32)
    tr.op(POOL, lambda: nc.gpsimd.memset(eps_sb[:, 0:1], RMS_EPS), writes=[cbufs[2]])
    tr.op(POOL, lambda: nc.gpsimd.memset(eps_sb[:, 1:2], LN_EPS), writes=[cbufs[2]])

    lnpar = [0]
    cvpar = [0]

    def ln_chain(psv, psvb, l, c):
        r = lnpar[0]
        lnpar[0] = (r + 1) % 4
        vs, vb = v_sb[r], v_buf[r]
        vsq_sb, vsq_buf = vsq_sbs[r], vsq_bufs[r]
        st, stb = st_sb[r], st_buf[r]
        tr.op(ACT, lambda: nc.scalar.activation(out=vs, in_=psv[:, 0:384], func=AF.Gelu), reads=[psvb], writes=[vb])
        v3 = vs.rearrange("p (h d) -> p h d", h=NHEAD)
        tr.op(DVE, lambda: nc.vector.tensor_reduce(out=st[:, 0, :], in_=v3, axis=AX.X, op=ALU.add),
              reads=[vb], writes=[stb[0]])
        tr.op(ACT, lambda: nc.scalar.activation(out=vsq_sb, in_=vs, func=AF.Square), reads=[vb], writes=[vsq_buf])
        tr.op(DVE, lambda: nc.vector.tensor_reduce(out=st[:, 1, :], in_=vsq_sb.rearrange("p (h d) -> p h d", h=NHEAD),
                                                   axis=AX.X, op=ALU.add), reads=[vsq_buf], writes=[stb[1]])
        tr.op(DVE, lambda: nc.vector.tensor_scalar(out=st[:, 2, :], in0=st[:, 0, :], scalar1=1.0 / 64.0, scalar2=None,
                                                   op0=ALU.mult), reads=[stb[0]], writes=[stb[2]])
        tr.op(DVE, lambda: nc.vector.tensor_tensor(out=st[:, 3, :], in0=st[:, 2, :], in1=st[:, 2, :], op=ALU.mult),
              reads=[stb[2]], writes=[stb[3]])
        tr.op(DVE, lambda: nc.vector.tensor_scalar(out=st[:, 4, :], in0=st[:, 1, :], scalar1=1.0 / 64.0, scalar2=LN_EPS,
                                                   op0=ALU.mult, op1=ALU.add), reads=[stb[1]], writes=[stb[4]])
        tr.op(DVE, lambda: nc.vector.tensor_tensor(out=st[:, 4, :], in0=st[:, 4, :], in1=st[:, 3, :], op=ALU.subtract),
              reads=[stb[3]], writes=[stb[4]])
        tr.op(POOL, lambda: nc.gpsimd.tensor_tensor(out=st[:, 4, :], in0=st[:, 4, :], in1=mhalf_sb[:, 0:NHEAD], op=ALU.pow),
              reads=[cbufs[2]], writes=[stb[4]])
        mean_bc = st[:, 2, :].unsqueeze(2).broadcast_to([128, NHEAD, 64])
        rstd_bc = st[:, 4, :].unsqueeze(2).broadcast_to([128, NHEAD, 64])
        vsq3 = vsq_sb.rearrange("p (h d) -> p h d", h=NHEAD)
        lng3 = small_sb[:, l, C_LNG:C_LNG + 384].rearrange("p (h d) -> p h d", h=NHEAD)
        tr.op(DVE, lambda: nc.vector.tensor_tensor(out=v3, in0=v3, in1=mean_bc, op=ALU.subtract),
              reads=[stb[2]], writes=[vb])
        tr.op(DVE, lambda: nc.vector.tensor_tensor(out=vsq3, in0=lng3, in1=rstd_bc, op=ALU.mult),
              reads=[stb[4], small_buf], writes=[vsq_buf])
        tr.op(DVE, lambda: nc.vector.tensor_tensor(out=vs, in0=vs, in1=vsq_sb, op=ALU.mult),
              reads=[vsq_buf], writes=[vb])
        tr.op(DVE, lambda: nc.vector.tensor_tensor(out=cur.vn_sb[c], in0=vs, in1=small_sb[:, l, C_LNB:C_LNB + 384], op=ALU.add),
              reads=[vb, small_buf], writes=[cur.vn_buf[c]])
        return c

    def mix_chunk(c, r, l):
        pm, pmb = ps_alloc()
        for j in range(3):
            for e in range(2):
                hd = 2 * j + e
                o = pm[e * 64:(e + 1) * 64, j * 128:(j + 1) * 128]
                tr.mm(lambda o=o, hd=hd: nc.tensor.matmul(o, lhsT=cur.vn_sb[r][:, hd * 64:(hd + 1) * 64],
                                                          rhs=wt_sb[:, l, hd * 128:(hd + 1) * 128],
                                                          start=True, stop=False, skip_group_check=True),
                      reads=[cur.vn_buf[r], wt_buf], writes=[pmb])
                last = (j == 2 and e == 1)
                tr.mm(lambda o=o, hd=hd: nc.tensor.matmul(o, lhsT=sel_sb, rhs=bshl_sb[:, l, hd * 128:(hd + 1) * 128],
                                                          start=False, stop=True, skip_group_check=True),
                      reads=[bshl_buf, cbufs[1]], writes=[pmb], inc=last)
        mr = mixpar[0]
        mixpar[0] ^= 1
        tr.op(ACT, lambda: nc.scalar.activation(out=mxs_sb[mr], in_=pm[:, 0:384], func=AF.Copy), reads=[pmb], writes=[mxs_buf[mr]])
        tr.op(DVE, lambda: nc.vector.tensor_tensor(
            out=cur.ycat_sb[:, 0:3, c * 128:(c + 1) * 128],
            in0=mxs_sb[mr].rearrange("p (j t) -> p j t", j=3),
            in1=cur.u_sb[:, 0:3, c * 128:(c + 1) * 128], op=ALU.mult),
            reads=[mxs_buf[mr]] + cur.u_buf, writes=cur.ycat_buf[0:3])

    mixpar = [0]

    def load_x(ti, t0):
        xi_ = ti % NXB
        T_ = tiles[ti] * 128
        tr.dma(SP, sl_x[xi_], lambda: nc.sync.dma_start(
            out=x_sb[xi_][:, :, :T_], in_=xT[:, :, t0:t0 + T_].rearrange("k p t -> p k t")), writes=x_buf[xi_])

    def tile_gen(ti, pi, tok0):
        nch = tiles[ti]
        T = nch * 128
        xi = ti % NXB
        xs, xb = x_sb[xi], x_buf[xi]
        r = norm(xi, T, 0, C_GMIX)
        norm_apply_h(xi, T, 0, C_GMIX, r)
        for l in range(L):
            yield
            if cur.hs == 0:
                load_ple(pi, l)
            tr.dma(POOL, cur.sl_p, lambda: nc.gpsimd.dma_start(
                out=cur.p_sb[:, :, :T], in_=pT[l, :, :, tok0:tok0 + T].rearrange("k p t -> p k t")), writes=[cur.p_buf])
            wp, wb = get_piece(pi, l, 0)
            w4 = wp[:, 0:3072].rearrange("p (m k j) -> p m k j", m=3, k=KC)
            for m in range(3):
                pu, pub = ps_alloc()
                group(pu[:, :T], pub, [(w4[:, m, k, :], cur.h_sb[:, k, :T]) for k in range(KC)],
                      [[wb, cur.h_buf[k]] for k in range(KC)])
                tr.op(ACT, lambda m=m, pu=pu: nc.scalar.activation(out=cur.u_sb[:, m, :T], in_=pu[:, :T], func=AF.Gelu),
                      reads=[pub], writes=[cur.u_buf[m]])
            yield
            wp, wb = get_piece(pi, l, 1)
            wv = wp[:, 0:3072].rearrange("p (k j) -> p k j", k=KC)
            vn_idx = []
            pend_mix = []
            for c in range(nch):
                pv, pvb = ps_alloc()
                group(pv[:, 0:384], pvb, [(cur.h_sb[:, k, c * 128:(c + 1) * 128], wv[:, k, :]) for k in range(KC)],
                      [[wb, cur.h_buf[k]] for k in range(KC)])
                pend_mix.append((c, pv, pvb))
            ln_done = {}

            def do_ln(c):
                cc_, pv_, pvb_ = pend_mix[c]
                ln_done[c] = ln_chain(pv_, pvb_, l, c)

            do_ln(0)
            if nch > 1:
                do_ln(1)
            yield
            wp, wb = get_piece(pi, l, 2)
            w4 = wp[:, 0:2048].rearrange("p (m k j) -> p m k j", m=2, k=KC)
            has_fix = (tok0 <= HALO * 128 < tok0 + T)
            q0 = 16 + HALO * 128 - tok0
            for cc in range(2):
                pz, pzb = ps_alloc()
                group(pz[:, :T], pzb, [(w4[:, cc, k, :], cur.h_sb[:, k, :T]) for k in range(KC)],
                      [[wb, cur.h_buf[k]] for k in range(KC)])
                zs, zb = zc_sb[cur.hs * 2 + cc], zc_buf[cur.hs * 2 + cc]
                tr.op(ACT, lambda: nc.scalar.activation(out=zs[:, 16:16 + T], in_=pz[:, :T], func=AF.Copy),
                      reads=[pzb], writes=[zb])
                tr.op(DVE, lambda cc=cc: nc.vector.tensor_copy(out=zs[:, 0:16], in_=zcar_sb[:, l, cc, :]),
                      reads=[zcar_buf[l][cc]], writes=[zb])
                tr.op(DVE, lambda cc=cc: nc.vector.tensor_copy(out=zcar_sb[:, l, cc, :], in_=zs[:, T:T + 16]),
                      reads=[zb], writes=[zcar_buf[l][cc]])
                W_ = 16 + T
                tr.op(POOL, lambda: nc.gpsimd.tensor_tensor(out=S_sb[0][:, 1:W_], in0=zs[:, 1:W_], in1=zs[:, 0:W_ - 1], op=ALU.add),
                      reads=[zb], writes=[S_buf[0]])
                if cc == 0:
                    tr.op(POOL, lambda: nc.gpsimd.tensor_tensor(out=S_sb[1][64:128, 3:W_], in0=S_sb[0][64:128, 3:W_],
                                                                in1=S_sb[0][64:128, 1:W_ - 2], op=ALU.add),
                          reads=[S_buf[0]], writes=[S_buf[1]])
                    srcs = [(0, slice(0, 64)), (1, slice(64, 128))]
                else:
                    tr.op(POOL, lambda: nc.gpsimd.tensor_tensor(out=S_sb[1][:, 3:W_], in0=S_sb[0][:, 3:W_],
                                                                in1=S_sb[0][:, 1:W_ - 2], op=ALU.add),
                          reads=[S_buf[0]], writes=[S_buf[1]])
                    tr.op(POOL, lambda: nc.gpsimd.tensor_tensor(out=S_sb[2][:, 7:W_], in0=S_sb[1][:, 7:W_],
                                                                in1=S_sb[1][:, 3:W_ - 4], op=ALU.add),
                          reads=[S_buf[1]], writes=[S_buf[2]])
                    tr.op(POOL, lambda: nc.gpsimd.tensor_tensor(out=S_sb[3][64:128, 15:W_], in0=S_sb[2][64:128, 15:W_],
                                                                in1=S_sb[2][64:128, 7:W_ - 8], op=ALU.add),
                          reads=[S_buf[2]], writes=[S_buf[3]])
                    srcs = [(2, slice(0, 64)), (3, slice(64, 128))]
                for si, psl in srcs:
                    if has_fix:
                        tr.op(POOL, lambda si=si, psl=psl, cc=cc: nc.gpsimd.tensor_tensor(
                            out=S_sb[si][psl, q0:q0 + 16], in0=S_sb[si][psl, q0:q0 + 16],
                            in1=small_sb[psl, l, C_CORR + cc * 16:C_CORR + (cc + 1) * 16], op=ALU.mult),
                            reads=[small_buf], writes=[S_buf[si]])
                    tr.op(POOL, lambda si=si, psl=psl, cc=cc: nc.gpsimd.tensor_scalar(
                        out=S_sb[si][psl, 16:16 + T], in0=S_sb[si][psl, 16:16 + T], scalar1=small_sb[psl, l, C_IW + cc:C_IW + cc + 1],
                        scalar2=0.0, op0=ALU.mult, op1=ALU.add),
                        reads=[small_buf], writes=[S_buf[si]])
                    tr.op(POOL, lambda si=si, psl=psl, cc=cc: nc.gpsimd.tensor_tensor(
                        out=cur.pl_sb[cc][psl, :T], in0=S_sb[si][psl, 16:16 + T], in1=zs[psl, 16:16 + T], op=ALU.subtract),
                        reads=[S_buf[si], zb], writes=[cur.pl_buf[cc]])
            for c in range(3):
                yield
                wp, wb = get_piece(pi, l, 3 + c)
                w4 = wp[:, 0:3072].rearrange("p (m k j) -> p m k j", m=3, k=KC)
                pss = []
                for m in range(3):
                    pz, pzb = ps_alloc()
                    group(pz[:, :T], pzb, [(w4[:, m, k, :], cur.h_sb[:, k, :T]) for k in range(KC)],
                          [[wb, cur.h_buf[k]] for k in range(KC)])
                    pss.append((pz, pzb))
                (pz, pzb), (pgc, pgcb), (pgb, pgbb) = pss
                rr = cvpar[0]
                cvpar[0] = (rr + 1) % 4
                cw = lambda k, c=c: small_sb[:, l, C_CW + c * 3 + k:C_CW + c * 3 + k + 1]
                tr.op(ACT, lambda: nc.scalar.activation(out=zbs_sb[rr][:, :T], in_=pz[:, :T], func=AF.Copy),
                      reads=[pzb], writes=[zbs_buf[rr]])
                tr.op(ACT, lambda: nc.scalar.activation(out=gcs_sb[rr][:, :T], in_=pgc[:, :T], func=AF.Copy),
                      reads=[pgcb], writes=[gcs_buf[rr]])
                tr.op(ACT, lambda: nc.scalar.activation(out=gbs_sb[rr][:, :T], in_=pgb[:, :T], func=AF.Copy),
                      reads=[pgbb], writes=[gbs_buf[rr]])
                tr.op(DVE, lambda: nc.vector.tensor_tensor(out=hb_sb[rr % 2][:, 2:2 + T], in0=zbs_sb[rr][:, :T], in1=gcs_sb[rr][:, :T],
                                                           op=ALU.mult), reads=[zbs_buf[rr], gcs_buf[rr]], writes=[hb_buf[rr % 2]])
                tr.op(DVE, lambda c=c: nc.vector.tensor_copy(out=hb_sb[rr % 2][:, 0:2], in_=ccar_sb[:, l, c, :]),
                      reads=[ccar_buf[l][c]], writes=[hb_buf[rr % 2]])
                tr.op(DVE, lambda: nc.vector.tensor_scalar(out=a0_sb[rr % 2][:, :T], in0=hb_sb[rr % 2][:, 2:2 + T], scalar1=cw(2), scalar2=None,
                                                           op0=ALU.mult), reads=[hb_buf[rr % 2], small_buf], writes=[a0_buf[rr % 2]])
                tr.op(DVE, lambda c=c: nc.vector.tensor_copy(out=ccar_sb[:, l, c, :], in_=hb_sb[rr % 2][:, T:T + 2]),
                      reads=[hb_buf[rr % 2]], writes=[ccar_buf[l][c]])
                tr.op(DVE, lambda: nc.vector.scalar_tensor_tensor(out=a0_sb[rr % 2][:, :T], in0=hb_sb[rr % 2][:, 1:1 + T], scalar=cw(1),
                                                                  in1=a0_sb[rr % 2][:, :T], op0=ALU.mult, op1=ALU.add),
                      reads=[hb_buf[rr % 2], small_buf], writes=[a0_buf[rr % 2]])
                tr.op(DVE, lambda: nc.vector.scalar_tensor_tensor(out=a0_sb[rr % 2][:, :T], in0=hb_sb[rr % 2][:, 0:T], scalar=cw(0),
                                                                  in1=a0_sb[rr % 2][:, :T], op0=ALU.mult, op1=ALU.add),
                      reads=[hb_buf[rr % 2], small_buf], writes=[a0_buf[rr % 2]])
                tr.op(DVE, lambda c=c: nc.vector.tensor_tensor(out=cur.ycat_sb[:, 3 + c, :T], in0=a0_sb[rr % 2][:, :T], in1=gbs_sb[rr][:, :T],
                                                               op=ALU.mult), reads=[a0_buf[rr % 2], gbs_buf[rr]], writes=[cur.ycat_buf[3 + c]])
                if c == 0:
                    for c2 in range(2, nch):
                        do_ln(c2)
                if c >= 1 and c - 1 < nch:
                    mix_chunk(c - 1, ln_done[c - 1], l)
                if c == 2:
                    for c2 in range(2, min(nch, 3)):
                        mix_chunk(c2, ln_done[c2], l)
            if nch > 3:
                mix_chunk(3, ln_done[3], l)
            korder = [3, 4, 5, 0, 1, 2, 6, 7]
            for i in range(2):
                yield
                if i == 0:
                    for cc in range(2):
                        pp, ppb = ps_alloc()
                        tr.mm(lambda cc=cc, pp=pp: nc.tensor.matmul(pp[:, :T], lhsT=wpb_sb[:, l, cc * 128:(cc + 1) * 128],
                                                                    rhs=cur.pl_sb[cc][:, :T], start=True, stop=True),
                              reads=[wpb_buf, cur.pl_buf[cc]], writes=[ppb], inc=True)
                        tr.op(ACT, lambda cc=cc, pp=pp: nc.scalar.activation(out=cur.ycat_sb[:, 6 + cc, :T], in_=pp[:, :T], func=AF.Copy,
                                                                             scale=small_sb[:, l, C_PSC + cc:C_PSC + cc + 1]),
                              reads=[ppb, small_buf], writes=[cur.ycat_buf[6 + cc]])
                wp, wb = get_piece(pi, l, 6 + i)
                w4 = wp.rearrange("p (m k j) -> p m k j", m=4, k=KC)
                for mi in range(4):
                    m = i * 4 + mi
                    po, pob = ps_alloc()
                    group(po[:, :T], pob, [(w4[:, mi, k, :], cur.ycat_sb[:, k, :T]) for k in korder],
                          [[wb, cur.ycat_buf[k]] for k in korder])
                    tr.op(DVE, lambda m=m, po=po: nc.vector.tensor_tensor(out=xs[:, m, :T], in0=xs[:, m, :T], in1=po[:, :T], op=ALU.add),
                          reads=[pob], writes=[xb[m]])
                    norm_sq(xi, T, m)
            r = norm(xi, T, l, C_GFF, skip_sq=True)
            norm_apply_h(xi, T, l, C_GFF, r)
            for i in range(8):
                yield
                wp, wb = get_piece(pi, l, 8 + i)
                w4 = wp.rearrange("p (m k j) -> p m k j", m=4, k=KC)
                for mi in range(4):
                    j = i * 4 + mi
                    pf, pfb = ps_alloc()
                    group(pf[:, :T], pfb, [(w4[:, mi, k, :], cur.h_sb[:, k, :T]) for k in range(KC)],
                          [[wb, cur.h_buf[k]] for k in range(KC)])
                    rr = j % 3
                    tr.op(ACT, lambda pf=pf, rr=rr: nc.scalar.activation(out=rl_sb[rr][:, :T], in_=pf[:, :T], func=AF.Relu),
                          reads=[pfb], writes=[rl_buf[rr]])
                    tr.op(POOL, lambda j=j, rr=rr: nc.gpsimd.tensor_tensor(out=cur.hid_sb[:, j, :T], in0=rl_sb[rr][:, :T], in1=rl_sb[rr][:, :T],
                                                                        op=ALU.mult), reads=[rl_buf[rr]], writes=[cur.hid_buf[j]])
            for m in range(8):
                yield
                wp, wb = get_piece(pi, l, 16 + m)
                w3 = wp.rearrange("p (k j) -> p k j", k=32)
                po, pob = ps_alloc()
                group(po[:, :T], pob, [(w3[:, k, :], cur.hid_sb[:, k, :T]) for k in range(32)],
                      [[wb, cur.hid_buf[k]] for k in range(32)])
                tr.op(DVE, lambda m=m, po=po: nc.vector.tensor_tensor(out=xs[:, m, :T], in0=xs[:, m, :T], in1=po[:, :T], op=ALU.add),
                      reads=[pob], writes=[xb[m]])
                norm_sq(xi, T, m)
            r = norm(xi, T, l, C_GPLE, skip_sq=True)
            norm_apply_h(xi, T, l, C_GPLE, r)
            wpl, wplb = wple_sb, wple_buf
            wpl4 = wpl[:, 0:2048].rearrange("p (m k j) -> p m k j", m=8, k=2)
            for i in range(2):
                yield
                wp, wb = get_piece(pi, l, 24 + i)
                w4 = wp.rearrange("p (m k j) -> p m k j", m=4, k=KC)
                for mi in range(4):
                    m = i * 4 + mi
                    pg, pgb_ = ps_alloc()
                    group(pg[:, :T], pgb_, [(w4[:, mi, k, :], cur.h_sb[:, k, :T]) for k in range(KC)],
                          [[wb, cur.h_buf[k]] for k in range(KC)])
                    pq, pqb = ps_alloc()
                    group(pq[:, :T], pqb, [(wpl4[:, m, k, :], cur.p_sb[:, k, :T]) for k in range(2)],
                          [[wplb, cur.p_buf] for k in range(2)])
                    rr = m % 2
                    tr.op(ACT, lambda pg=pg, rr=rr: nc.scalar.activation(out=th_sb[rr][:, :T], in_=pg[:, :T], func=AF.Tanh, scale=0.5),
                          reads=[pgb_], writes=[th_buf[rr]])
                    tr.op(DVE, lambda pq=pq, rr=rr: nc.vector.scalar_tensor_tensor(out=t1_sb[rr][:, :T], in0=th_sb[rr][:, :T], scalar=1.0,
                                                                                   in1=pq[:, :T], op0=ALU.add, op1=ALU.mult),
                          reads=[th_buf[rr], pqb], writes=[t1_buf[rr]])
                    tr.op(DVE, lambda m=m, rr=rr: nc.vector.scalar_tensor_tensor(out=xs[:, m, :T], in0=t1_sb[rr][:, :T], scalar=0.5,
                                                                                  in1=xs[:, m, :T], op0=ALU.mult, op1=ALU.add),
                          reads=[t1_buf[rr]], writes=[xb[m]])
            if l + 1 < L:
                r = norm(xi, T, l + 1, C_GMIX)
                norm_apply_h(xi, T, l + 1, C_GMIX, r)
        r = norm(xi, T, L - 1, C_GFIN)
        for k in range(KC):
            tr.op(DVE, lambda k=k: nc.vector.scalar_tensor_tensor(
                out=xs[:, k, :T], in0=xs[:, k, :T], scalar=small_sb[:, L - 1, C_GFIN + k:C_GFIN + k + 1],
                in1=ms_sb[r][:, :T], op0=ALU.mult, op1=ALU.mult),
                reads=[ms_buf[r], small_buf], writes=[xb[k]])
        lo = max(tok0, HALO * 128)
        hi = tok0 + T
        if hi > lo:
            tr.dma(SP, sl_o[xi], lambda: nc.sync.dma_start(
                out=outT[:, :, lo - HALO * 128:hi - HALO * 128].rearrange("k p t -> p k t"),
                in_=xs[:, :, lo - tok0:hi - tok0]), reads=xb)

    tok_starts = [0]
    for n_ in tiles:
        tok_starts.append(tok_starts[-1] + n_ * 128)
    pairs = [list(range(i, min(i + NHS, len(tiles)))) for i in range(0, len(tiles), NHS)]
    for t_ in pairs[0]:
        load_x(t_, tok_starts[t_])
    for pi, pr in enumerate(pairs):
        if pi + 1 < len(pairs):
            for t_ in pairs[pi + 1]:
                load_x(t_, tok_starts[t_])
        gens = [(tile_gen(t_, pi, tok_starts[t_]), halves[k_]) for k_, t_ in enumerate(pr)]
        alive = list(gens)
        while alive:
            for g_ in list(alive):
                cur.__dict__.update(g_[1].__dict__)
                try:
                    next(g_[0])
                except StopIteration:
                    alive.remove(g_)
    for s in sl_o:
        if s.cnt:
            nc.sync.wait_ge(s.sem, s.cnt)
    for s in sl_wb + [sl_wplewb]:
        if s.cnt:
            nc.sync.wait_ge(s.sem, s.cnt)
    return nc


def _default_tiles(nch):
    n = -(-nch // 2)
    base = nch // n
    rem = nch - base * n
    return [base + 1] * rem + [base] * (n - rem)


def kernel(x, p, norm_mix_g, w_in, sgu_w, sgu_b, sgu_ln_g, sgu_ln_b, conv_w, pool_w, pool_scale, w_out,
           norm_ff_g, w_ff1, w_ff2, norm_ple_g, w_ple_gate, w_ple_proj, final_g, _cfg=None):
    cfg = dict(CFG)
    if _cfg:
        cfg.update(_cfg)
    inp = dict(norm_mix_g=norm_mix_g, w_in=w_in, sgu_w=sgu_w, sgu_b=sgu_b, sgu_ln_g=sgu_ln_g, sgu_ln_b=sgu_ln_b,
               conv_w=conv_w, pool_w=pool_w, pool_scale=pool_scale, w_out=w_out, norm_ff_g=norm_ff_g, w_ff1=w_ff1,
               w_ff2=w_ff2, norm_ple_g=norm_ple_g, w_ple_gate=w_ple_gate, w_ple_proj=w_ple_proj, final_g=final_g)
    x = np.asarray(x, np.float32)
    p = np.asarray(p, np.float32)
    L = cfg["L"]
    B, S, _ = x.shape
    nseg = NCORES // B
    SEGC = S // 128 // nseg
    NCH = SEGC + HALO
    tiles = cfg["TILES"] or _default_tiles(NCH)
    wpk = _pack_weights(inp, L)
    bpk = _pack_big(inp, L)
    spk_first = _pack_small(inp, L, True)
    spk_rest = _pack_small(inp, L, False)
    in_maps = []
    for c in range(NCORES):
        b, sg = divmod(c, nseg)
        t0 = sg * SEGC * 128
        xs = np.zeros((NCH * 128, D), np.float32)
        ps = np.zeros((L, NCH * 128, DPLE), np.float32)
        if sg == 0:
            xs[HALO * 128:] = x[b, t0:t0 + SEGC * 128]
            ps[:, HALO * 128:] = p[:L, b, t0:t0 + SEGC * 128]
        else:
            xs[:] = x[b, t0 - HALO * 128:t0 + SEGC * 128]
            ps[:] = p[:L, b, t0 - HALO * 128:t0 + SEGC * 128]
        xTc = np.ascontiguousarray(xs.T).reshape(KC, 128, NCH * 128)
        pTc = np.ascontiguousarray(ps.transpose(0, 2, 1)).reshape(L, 2, 128, NCH * 128)
        in_maps.append({"xT": xTc, "pT": pTc, "wpk": wpk, "spk": spk_first if sg == 0 else spk_rest, "bpk": bpk})
    nc = build_nc(L, NCH, tiles, cfg["NBUF_W"])
    res = run_bass_kernel_spmd(nc, in_maps, core_ids=list(range(NCORES)))
    out = np.zeros((B, S, D), np.float32)
    for c in range(NCORES):
        b, sg = divmod(c, nseg)
        t0 = sg * SEGC * 128
        oT = np.asarray(res.results[c]["outT"]).reshape(D, SEGC * 128)
        out[b, t0:t0 + SEGC * 128] = oT.T
    return out
```

```python
import numpy as np
import concourse.bass as bass
import concourse.mybir as mybir
from concourse.bass_utils import run_bass_kernel_spmd

F32 = mybir.dt.float32
BF16 = mybir.dt.bfloat16
AF = mybir.ActivationFunctionType
ALU = mybir.AluOpType
AX = mybir.AxisListType

D = 1024
KC = 8
DFF = 4096
DPLE = 256
NHEAD = 6
RMS_EPS = 1e-6
LN_EPS = 1e-5
HALO = 2
NCORES = 8

CFG = dict(L=4, SEGC=32, TILES=None, NBUF_W=4)

PIECES = []


def _mk_pieces():
    off = 0

    def add(name, ln):
        nonlocal off
        PIECES.append((name, off, ln))
        off += ln
    add("U", 3072)
    add("V", 3072)
    add("POOL", 2048)
    for c in range(3):
        add("CONV%d" % c, 3072)
    for i in range(2):
        add("OUT%d" % i, 4096)
    for i in range(8):
        add("FF1_%d" % i, 4096)
    for i in range(8):
        add("FF2_%d" % i, 4096)
    add("PLE", 2048)
    for i in range(2):
        add("GATE%d" % i, 4096)
    return off


E_W = _mk_pieces()
NPIECE = len(PIECES)
WSLOT = 4096
PLE_IDX = [i for i, pc in enumerate(PIECES) if pc[0] == "PLE"][0]
RING = [i for i in range(NPIECE) if i != PLE_IDX]
NRING = len(RING)

C_GMIX, C_GFF, C_GPLE, C_GFIN, C_CW, C_PSC, C_IW, C_CORR, C_LNG, C_LNB = 0, 8, 16, 24, 32, 41, 43, 45, 77, 461
S_SMALL = 845
S_BIG = 1792


def _chunk_block(W, col0, ncol=128):
    k = W.shape[0] // 128
    blk = W[:, col0:col0 + ncol].reshape(k, 128, ncol)
    return np.ascontiguousarray(blk.transpose(1, 0, 2)).reshape(128, k * ncol)


def _pack_weights(inp, L):
    wpk = np.zeros((L, 128, E_W), np.float32)
    for l in range(L):
        w_in = np.asarray(inp["w_in"][l])
        w_out = np.asarray(inp["w_out"][l])
        w1 = np.asarray(inp["w_ff1"][l])
        w2 = np.asarray(inp["w_ff2"][l])
        wg = np.asarray(inp["w_ple_gate"][l])
        wp = np.asarray(inp["w_ple_proj"][l])
        parts = []
        parts += [_chunk_block(w_in, c * 128) for c in range(3)]
        parts += [_chunk_block(w_in, 384, 384)]
        parts += [_chunk_block(w_in, 1920), _chunk_block(w_in, 2048)]
        for c in range(3):
            parts += [_chunk_block(w_in, 768 + c * 128), _chunk_block(w_in, 1536 + c * 128),
                      _chunk_block(w_in, 1152 + c * 128)]
        parts += [_chunk_block(w_out, m * 128) for m in range(8)]
        parts += [_chunk_block(w1, j * 128) for j in range(32)]
        parts += [_chunk_block(w2, m * 128) for m in range(8)]
        parts += [_chunk_block(wp, m * 128) for m in range(8)]
        parts += [_chunk_block(wg, m * 128) for m in range(8)]
        row = np.concatenate(parts, axis=1)
        assert row.shape == (128, E_W), row.shape
        wpk[l] = row
    return wpk


def _pack_small(inp, L, first_seg):
    spk = np.zeros((L, 128, S_SMALL), np.float32)
    for l in range(L):
        s = spk[l]
        s[:, C_GMIX:C_GMIX + 8] = np.asarray(inp["norm_mix_g"][l]).reshape(8, 128).T
        s[:, C_GFF:C_GFF + 8] = np.asarray(inp["norm_ff_g"][l]).reshape(8, 128).T
        s[:, C_GPLE:C_GPLE + 8] = np.asarray(inp["norm_ple_g"][l]).reshape(8, 128).T
        s[:, C_GFIN:C_GFIN + 8] = np.asarray(inp["final_g"]).reshape(8, 128).T
        cw = np.asarray(inp["conv_w"][l])
        for c in range(3):
            for k in range(3):
                s[:, C_CW + c * 3 + k] = cw[k, c * 128:(c + 1) * 128]
        s[:, C_PSC:C_PSC + 2] = np.asarray(inp["pool_scale"][l]).reshape(2, 128).T
        wins = np.array([[2, 4], [8, 16]], np.float32)
        for cc in range(2):
            for hf in range(2):
                win = wins[cc, hf]
                s[hf * 64:(hf + 1) * 64, C_IW + cc] = 1.0 / win
                for i in range(16):
                    v = win / min(i + 1.0, win) if first_seg else 1.0
                    s[hf * 64:(hf + 1) * 64, C_CORR + cc * 16 + i] = v
        s[:, C_LNG:C_LNG + 384] = np.asarray(inp["sgu_ln_g"][l])[None, :]
        s[:, C_LNB:C_LNB + 384] = np.asarray(inp["sgu_ln_b"][l])[None, :]
    return spk


def _pack_big(inp, L):
    bpk = np.zeros((L, 128, S_BIG), np.float32)
    for l in range(L):
        w = np.asarray(inp["sgu_w"][l])
        bpk[l, :, 0:768] = np.ascontiguousarray(w.transpose(2, 0, 1)).reshape(128, 768)
        bs = np.asarray(inp["sgu_b"][l]).reshape(768)
        bpk[l, 0, 768:1536] = bs
        bpk[l, 32, 768:1536] = bs
        wpool = np.asarray(inp["pool_w"][l])
        for cc in range(2):
            blk = np.zeros((128, 128), np.float32)
            blk[0:64, 0:64] = wpool[2 * cc]
            blk[64:128, 64:128] = wpool[2 * cc + 1]
            bpk[l, :, 1536 + cc * 128:1536 + (cc + 1) * 128] = blk
    return bpk


class Buf:
    __slots__ = ("name", "w", "r")

    def __init__(self, name):
        self.name = name
        self.w = None
        self.r = {}


class Eng:
    def __init__(self, nc, name, h, self_sync):
        self.name = name
        self.h = h
        self.sem = nc.alloc_semaphore("s_" + name)
        self.cnt = 0
        self.seen = {}
        self.self_sync = self_sync
        self.pend_r = []
        self.pend_w = []


class Slot:
    def __init__(self, nc, name):
        self.name = name
        self.sem = nc.alloc_semaphore("d_" + name)
        self.cnt = 0


class TR:
    def __init__(self, nc):
        self.nc = nc
        self.pe = Eng(nc, "pe", nc.tensor, False)
        self.act = Eng(nc, "act", nc.scalar, True)
        self.dve = Eng(nc, "dve", nc.vector, True)
        self.pool = Eng(nc, "pool", nc.gpsimd, True)
        self.sp = Eng(nc, "sp", nc.sync, False)

    @staticmethod
    def _deps(reads, writes):
        need = {}
        for b in reads:
            if b.w is not None:
                o, c = b.w
                if need.get(o, 0) < c:
                    need[o] = c
        for b in writes:
            if b.w is not None:
                o, c = b.w
                if need.get(o, 0) < c:
                    need[o] = c
            for o, c in b.r.items():
                if need.get(o, 0) < c:
                    need[o] = c
        return need

    @staticmethod
    def _wait(eng, need):
        for o, c in need.items():
            if o is eng and not eng.self_sync:
                continue
            if eng.seen.get(o, 0) >= c:
                continue
            eng.h.wait_ge(o.sem, c)
            eng.seen[o] = c

    def op(self, eng, fn, reads=(), writes=()):
        self._wait(eng, self._deps(reads, writes))
        ins = fn()
        eng.cnt += 1
        ins.then_inc(eng.sem, 1)
        for b in reads:
            b.r[eng] = eng.cnt
        for b in writes:
            b.w = (eng, eng.cnt)
            b.r = {}
        return ins

    def mm(self, fn, reads=(), writes=(), inc=False):
        eng = self.pe
        self._wait(eng, self._deps(reads, writes))
        ins = fn()
        eng.pend_r.extend(reads)
        eng.pend_w.extend(writes)
        if inc:
            eng.cnt += 1
            ins.then_inc(eng.sem, 1)
            for b in eng.pend_r:
                b.r[eng] = eng.cnt
            for b in eng.pend_w:
                b.w = (eng, eng.cnt)
                b.r = {}
            eng.pend_r = []
            eng.pend_w = []
        return ins

    def dma(self, q, slot, fn, reads=(), writes=()):
        self._wait(q, self._deps(reads, writes))
        ins = fn()
        slot.cnt += 16
        ins.then_inc(slot.sem, 16)
        for b in reads:
            b.r[slot] = slot.cnt
        for b in writes:
            b.w = (slot, slot.cnt)
            b.r = {}
        return ins


def build_nc(L, NCH, tiles, nbuf_w=3):
    assert sum(tiles) == NCH and max(tiles) <= 4 and len(set(tiles[i] for i in range(0, len(tiles) - len(tiles) % 2))) <= 1
    NTOK = NCH * 128
    NMAIN = (NCH - HALO) * 128
    nc = bass.Bass("TRN2", target_bir_lowering=False)
    tr = TR(nc)
    PE, ACT, DVE, POOL, SP = tr.pe, tr.act, tr.dve, tr.pool, tr.sp

    xT = nc.dram_tensor("xT", [KC, 128, NTOK], F32, kind="ExternalInput").ap()
    pT = nc.dram_tensor("pT", [L, 2, 128, NTOK], F32, kind="ExternalInput").ap()
    wpk = nc.dram_tensor("wpk", [L, 128, E_W], F32, kind="ExternalInput").ap()
    spk = nc.dram_tensor("spk", [L, 128, S_SMALL], F32, kind="ExternalInput").ap()
    bpk = nc.dram_tensor("bpk", [L, 128, S_BIG], F32, kind="ExternalInput").ap()
    outT = nc.dram_tensor("outT", [KC, 128, NMAIN], F32, kind="ExternalOutput").ap()
    wbf = nc.dram_tensor("wbf", [L, 128, E_W], BF16, kind="Internal").ap()

    def sb(name, shape, dt):
        return nc.alloc_sbuf_tensor(name, shape, dt).ap()

    NXB = 4
    TM = max(tiles) * 128
    NHS = 2
    x_sb = [sb("x%d" % i, [128, KC, TM], F32) for i in range(NXB)]
    x_buf = [[Buf("x%d_%d" % (i, k)) for k in range(KC)] for i in range(NXB)]

    class Half:
        pass

    halves = []
    for hs in range(NHS):
        H = Half()
        H.hs = hs
        H.h_sb = sb("h%d" % hs, [128, KC, TM], BF16)
        H.h_buf = [Buf("h%d_%d" % (hs, k)) for k in range(KC)]
        H.hid_sb = sb("hid%d" % hs, [128, 32, TM], BF16)
        H.hid_buf = [Buf("hid%d_%d" % (hs, k)) for k in range(32)]
        H.ycat_sb = H.hid_sb[:, 0:KC, :]
        H.ycat_buf = H.hid_buf[0:KC]
        H.u_sb = sb("u%d" % hs, [128, 3, TM], F32)
        H.u_buf = [Buf("u%d_%d" % (hs, k)) for k in range(3)]
        H.p_sb = sb("pbf%d" % hs, [128, 2, TM], BF16)
        H.p_buf = Buf("pbf%d" % hs)
        H.vn_sb = [sb("vn%d_%d" % (hs, i), [128, 384], BF16) for i in range(max(tiles))]
        H.vn_buf = [Buf("vn%d_%d" % (hs, i)) for i in range(max(tiles))]
        H.sq_sb = [sb("sq%d_%d" % (hs, i), [128, TM], BF16) for i in range(KC)]
        H.sq_buf = [Buf("sq%d_%d" % (hs, i)) for i in range(KC)]
        H.pl_sb = [sb("pl%d_%d" % (hs, i), [128, TM], BF16) for i in range(2)]
        H.pl_buf = [Buf("pl%d_%d" % (hs, i)) for i in range(2)]
        H.sl_p = Slot(nc, "p%d" % hs)
        halves.append(H)
    cur = Half()
    wple_sb = sb("wple", [128, 2048], BF16)
    wple_buf = Buf("wple")
    w_sb = [sb("w%d" % i, [128, WSLOT], BF16) for i in range(nbuf_w)]
    w_buf = [Buf("w%d" % i) for i in range(nbuf_w)]
    small_sb = sb("small", [128, L, S_SMALL], F32)
    small_buf = Buf("small")
    wt_sb = sb("wt", [128, L, 768], BF16)
    bshl_sb = sb("bshl", [128, L, 768], BF16)
    wpb_sb = sb("wpb", [128, L, 256], BF16)
    wt_buf = Buf("wt")
    bshl_buf = Buf("bshl")
    wpb_buf = Buf("wpb")
    stage_sb = x_sb[NXB - 1].rearrange("p k t -> p (k t)")[:, 0:S_BIG]
    assert KC * TM >= S_BIG
    stage_buf = Buf("stage")
    ones_sb = sb("ones", [128, 128], BF16)
    sel_sb = sb("sel", [128, 64], BF16)
    mhalf_sb = sb("mhalf", [128, 8], F32)
    ms_sb = [sb("ms%d" % i, [128, TM], F32) for i in range(2)]
    ms_buf = [Buf("ms%d" % i) for i in range(2)]
    v_sb = [sb("v%d" % i, [128, 384], F32) for i in range(4)]
    v_buf = [Buf("v%d" % i) for i in range(4)]
    vsq_sbs = [sb("vsq%d" % i, [128, 384], F32) for i in range(4)]
    vsq_bufs = [Buf("vsq%d" % i) for i in range(4)]
    st_sb = [sb("st%d" % i, [128, 5, NHEAD], F32) for i in range(4)]
    st_buf = [[Buf("st%d_%d" % (i, j)) for j in range(5)] for i in range(4)]
    gcs_sb = [sb("gcs%d" % i, [128, TM], F32) for i in range(4)]
    gcs_buf = [Buf("gcs%d" % i) for i in range(4)]
    zbs_sb = [sb("zbs%d" % i, [128, TM], F32) for i in range(4)]
    zbs_buf = [Buf("zbs%d" % i) for i in range(4)]
    mxs_sb = [sb("mxs%d" % i, [128, 384], F32) for i in range(2)]
    mxs_buf = [Buf("mxs%d" % i) for i in range(2)]
    stbf_sb = sb("stbf", [128, 768], BF16)
    stbf_buf = Buf("stbf")
    gbs_sb = [sb("gbs%d" % i, [128, TM], F32) for i in range(4)]
    gbs_buf = [Buf("gbs%d" % i) for i in range(4)]
    hb_sb = [sb("hb%d" % i, [128, TM + 2], F32) for i in range(2)]
    hb_buf = [Buf("hb%d" % i) for i in range(2)]
    a0_sb = [sb("a0_%d" % i, [128, TM], F32) for i in range(2)]
    a0_buf = [Buf("a0_%d" % i) for i in range(2)]
    ccar_sb = sb("ccar", [128, L, 3, 2], F32)
    ccar_buf = [[Buf("ccar%d_%d" % (l, c)) for c in range(3)] for l in range(L)]
    zc_sb = [sb("zc%d" % i, [128, TM + 16], F32) for i in range(2 * NHS)]
    zc_buf = [Buf("zc%d" % i) for i in range(2 * NHS)]
    S_sb = [sb("S%d" % i, [128, TM + 16], F32) for i in range(4)]
    S_buf = [Buf("S%d" % i) for i in range(4)]
    zcar_sb = sb("zcar", [128, L, 2, 16], F32)
    zcar_buf = [[Buf("zcar%d_%d" % (l, c)) for c in range(2)] for l in range(L)]
    rl_sb = [sb("rl%d" % i, [128, TM], F32) for i in range(3)]
    rl_buf = [Buf("rl%d" % i) for i in range(3)]
    th_sb, th_buf = rl_sb, rl_buf
    t1_sb, t1_buf = gcs_sb, gcs_buf

    ps_sb = [nc.alloc_psum_tensor("ps%d" % i, [128, 512], F32).ap() for i in range(8)]
    ps_buf = [Buf("ps%d" % i) for i in range(8)]
    ps_next = [0]

    def ps_alloc():
        i = ps_next[0]
        ps_next[0] = (i + 1) % 8
        assert not PE.pend_w, "ps_alloc inside an open accumulation group"
        assert ps_buf[i].w is None or ps_buf[i].r, "psum bank %d still live" % i
        return ps_sb[i], ps_buf[i]

    sl_small = Slot(nc, "small")
    sl_stage = Slot(nc, "stage")
    sl_wple = Slot(nc, "wple")
    sl_wplewb = Slot(nc, "wplewb")
    sl_x = [Slot(nc, "x%d" % i) for i in range(NXB)]
    sl_o = [Slot(nc, "o%d" % i) for i in range(NXB)]
    sl_w = [Slot(nc, "w%d" % i) for i in range(nbuf_w)]
    sl_wb = [Slot(nc, "wb%d" % i) for i in range(nbuf_w)]
    wbf_buf = [[Buf("wbf%d_%d" % (l, j)) for j in range(NPIECE)] for l in range(L)]

    cbufs = [Buf("c_ones"), Buf("c_sel"), Buf("c_mhalf")]
    tr.op(POOL, lambda: nc.gpsimd.memset(ones_sb, 1.0 / 1024.0), writes=[cbufs[0]])
    tr.op(POOL, lambda: nc.gpsimd.memset(sel_sb, 0.0), writes=[cbufs[1]])
    tr.op(POOL, lambda: nc.gpsimd.memset(sel_sb[0:1, :], 1.0), writes=[cbufs[1]])
    tr.op(POOL, lambda: nc.gpsimd.memset(sel_sb[32:33, :], 1.0), writes=[cbufs[1]])
    tr.op(POOL, lambda: nc.gpsimd.memset(mhalf_sb, -0.5), writes=[cbufs[2]])
    allc = []
    for l in range(L):
        allc += ccar_buf[l] + zcar_buf[l]
    tr.op(POOL, lambda: nc.gpsimd.memset(ccar_sb, 0.0), writes=[b for l in range(L) for b in ccar_buf[l]])
    tr.op(POOL, lambda: nc.gpsimd.memset(zcar_sb, 0.0), writes=[b for l in range(L) for b in zcar_buf[l]])
    tr.dma(SP, sl_small, lambda: nc.sync.dma_start(out=small_sb, in_=spk.rearrange("l p s -> p l s")),
           writes=[small_buf])
    for l in range(L):
        tr.dma(SP, sl_stage, lambda l=l: nc.sync.dma_start(out=stage_sb, in_=bpk[l]), writes=[stage_buf])
        tr.op(POOL, lambda l=l: nc.gpsimd.affine_select(
            out=wt_sb[:, l, :].rearrange("p (h t) -> p h t", h=NHEAD),
            in_=stage_sb[:, 0:768].rearrange("p (h t) -> p h t", h=NHEAD),
            pattern=[[0, NHEAD], [1, 128]], compare_op=ALU.is_ge, fill=0.0, base=0,
            channel_multiplier=-1), reads=[stage_buf], writes=[wt_buf])
        tr.op(DVE, lambda: nc.vector.tensor_copy(out=stbf_sb, in_=stage_sb[:, 768:1536]),
              reads=[stage_buf], writes=[stbf_buf])
        tr.op(DVE, lambda l=l: nc.vector.tensor_copy(out=bshl_sb[0:32, l, :], in_=stbf_sb[0:32, :]),
              reads=[stbf_buf], writes=[bshl_buf])
        tr.op(DVE, lambda l=l: nc.vector.tensor_tensor(out=bshl_sb[32:64, l, :], in0=stage_sb[32:64, 768:1536],
                                                       in1=stbf_sb[32:64, :], op=ALU.subtract),
              reads=[stage_buf, stbf_buf], writes=[bshl_buf])
        tr.op(DVE, lambda l=l: nc.vector.tensor_copy(out=bshl_sb[64:128, l, :], in_=stbf_sb[64:128, :]),
              reads=[stbf_buf], writes=[bshl_buf])
        tr.op(DVE, lambda l=l: nc.vector.tensor_copy(out=wpb_sb[:, l, :], in_=stage_sb[:, 1536:1792]),
              reads=[stage_buf], writes=[wpb_buf])

    for b_ in x_buf[NXB - 1]:
        b_.w = stage_buf.w
        b_.r = dict(stage_buf.r)

    npairs = (len(tiles) + NHS - 1) // NHS
    seq = [(pi, l, j) for pi in range(npairs) for l in range(L) for j in RING]
    issued = [0]

    def issue_load(gi):
        pi, l, j = seq[gi]
        s = gi % nbuf_w
        _, off, ln = PIECES[j]
        if pi == 0:
            tr.dma(POOL, sl_w[s], lambda: nc.gpsimd.dma_start(
                out=w_sb[s][:, 0:ln].rearrange("p (a b) -> p a b", b=1024),
                in_=wpk[l, :, off:off + ln].rearrange("p (a b) -> p a b", b=1024)),
                writes=[w_buf[s]])
            if npairs > 1:
                tr.dma(SP, sl_wb[s], lambda: nc.sync.dma_start(out=wbf[l, :, off:off + ln], in_=w_sb[s][:, 0:ln]),
                       reads=[w_buf[s]], writes=[wbf_buf[l][j]])
        else:
            tr.dma(SP, sl_w[s], lambda: nc.sync.dma_start(out=w_sb[s][:, 0:ln], in_=wbf[l, :, off:off + ln]),
                   reads=[wbf_buf[l][j]], writes=[w_buf[s]])

    def get_piece(pi, l, rj):
        gi = (pi * L + l) * NRING + rj
        while issued[0] < len(seq) and issued[0] <= gi + nbuf_w - 1:
            issue_load(issued[0])
            issued[0] += 1
        s = gi % nbuf_w
        return w_sb[s], w_buf[s]

    def load_ple(pi, l):
        _, off, ln = PIECES[PLE_IDX]
        if pi == 0:
            tr.dma(POOL, sl_wple, lambda: nc.gpsimd.dma_start(
                out=wple_sb.rearrange("p (a b) -> p a b", b=1024),
                in_=wpk[l, :, off:off + ln].rearrange("p (a b) -> p a b", b=1024)), writes=[wple_buf])
            if npairs > 1:
                tr.dma(SP, sl_wplewb, lambda: nc.sync.dma_start(out=wbf[l, :, off:off + ln], in_=wple_sb),
                       reads=[wple_buf], writes=[wbf_buf[l][PLE_IDX]])
        else:
            tr.dma(SP, sl_wple, lambda: nc.sync.dma_start(out=wple_sb, in_=wbf[l, :, off:off + ln]),
                   reads=[wbf_buf[l][PLE_IDX]], writes=[wple_buf])


    def norm_sq(xi, T, k):
        tr.op(ACT, lambda: nc.scalar.activation(out=cur.sq_sb[k][:, :T], in_=x_sb[xi][:, k, :T], func=AF.Square),
              reads=[x_buf[xi][k]], writes=[cur.sq_buf[k]])

    def norm(xi, T, l, gcol, skip_sq=False):
        xs, xb = x_sb[xi], x_buf[xi]
        pst, psb = ps_alloc()
        if not skip_sq:
            for k in range(KC):
                norm_sq(xi, T, k)
        for k in range(KC):
            tr.mm(lambda k=k: nc.tensor.matmul(pst[:, :T], lhsT=ones_sb, rhs=cur.sq_sb[k][:, :T],
                                               start=(k == 0), stop=(k == KC - 1)),
                  reads=[cur.sq_buf[k], cbufs[0]], writes=[psb], inc=True)
        r = norm.par
        norm.par ^= 1
        tr.op(ACT, lambda: nc.scalar.activation(out=ms_sb[r][:, :T], in_=pst[:, :T], func=AF.Sqrt,
                                                bias=eps_sb[:, 0:1], scale=1.0),
              reads=[psb, cbufs[2]], writes=[ms_buf[r]])
        tr.op(DVE, lambda: nc.vector.reciprocal(out=ms_sb[r][:, :T], in_=ms_sb[r][:, :T]),
              reads=[], writes=[ms_buf[r]])
        return r

    norm.par = 0

    def norm_apply_h(xi, T, l, gcol, r):
        xs, xb = x_sb[xi], x_buf[xi]
        for k in range(KC):
            tr.op(DVE, lambda k=k: nc.vector.scalar_tensor_tensor(
                out=cur.h_sb[:, k, :T], in0=xs[:, k, :T], scalar=small_sb[:, l, gcol + k:gcol + k + 1],
                in1=ms_sb[r][:, :T], op0=ALU.mult, op1=ALU.mult),
                reads=[xb[k], ms_buf[r], small_buf], writes=[cur.h_buf[k]])

    def group(out_ap, out_buf, pairs, reads_each):
        n = len(pairs)
        for i, (lt, rh) in enumerate(pairs):
            tr.mm(lambda lt=lt, rh=rh, i=i: nc.tensor.matmul(out_ap, lhsT=lt, rhs=rh, start=(i == 0), stop=(i == n - 1)),
                  reads=reads_each[i], writes=[out_buf], inc=(i == n - 1))

    eps_sb = sb("eps", [128, 2], F32)
    tr.op(POOL, lambda: nc.gpsimd.memset(eps_sb[:, 0:1], RMS_EPS), writes=[cbufs[2]])
    tr.op(POOL, lambda: nc.gpsimd.memset(eps_sb[:, 1:2], LN_EPS), writes=[cbufs[2]])

    lnpar = [0]
    cvpar = [0]

    def ln_chain_gen(psv, psvb, l, c):
        r = lnpar[0]
        lnpar[0] = (r + 1) % 4
        vs, vb = v_sb[r], v_buf[r]
        vsq_sb, vsq_buf = vsq_sbs[r], vsq_bufs[r]
        st, stb = st_sb[r], st_buf[r]
        tr.op(ACT, lambda: nc.scalar.activation(out=vs, in_=psv[:, 0:384], func=AF.Gelu), reads=[psvb], writes=[vb])
        v3 = vs.rearrange("p (h d) -> p h d", h=NHEAD)
        yield
        tr.op(DVE, lambda: nc.vector.tensor_reduce(out=st[:, 0, :], in_=v3, axis=AX.X, op=ALU.add),
              reads=[vb], writes=[stb[0]])
        yield
        tr.op(ACT, lambda: nc.scalar.activation(out=vsq_sb, in_=vs, func=AF.Square), reads=[vb], writes=[vsq_buf])
        yield
        tr.op(DVE, lambda: nc.vector.tensor_reduce(out=st[:, 1, :], in_=vsq_sb.rearrange("p (h d) -> p h d", h=NHEAD),
                                                   axis=AX.X, op=ALU.add), reads=[vsq_buf], writes=[stb[1]])
        yield
        tr.op(DVE, lambda: nc.vector.tensor_scalar(out=st[:, 2, :], in0=st[:, 0, :], scalar1=1.0 / 64.0, scalar2=None,
                                                   op0=ALU.mult), reads=[stb[0]], writes=[stb[2]])
        yield
        tr.op(DVE, lambda: nc.vector.tensor_tensor(out=st[:, 3, :], in0=st[:, 2, :], in1=st[:, 2, :], op=ALU.mult),
              reads=[stb[2]], writes=[stb[3]])
        yield
        tr.op(DVE, lambda: nc.vector.tensor_scalar(out=st[:, 4, :], in0=st[:, 1, :], scalar1=1.0 / 64.0, scalar2=LN_EPS,
                                                   op0=ALU.mult, op1=ALU.add), reads=[stb[1]], writes=[stb[4]])
        yield
        tr.op(DVE, lambda: nc.vector.tensor_tensor(out=st[:, 4, :], in0=st[:, 4, :], in1=st[:, 3, :], op=ALU.subtract),
              reads=[stb[3]], writes=[stb[4]])
        yield
        tr.op(POOL, lambda: nc.gpsimd.tensor_tensor(out=st[:, 4, :], in0=st[:, 4, :], in1=mhalf_sb[:, 0:NHEAD], op=ALU.pow),
              reads=[cbufs[2]], writes=[stb[4]])
        mean_bc = st[:, 2, :].unsqueeze(2).broadcast_to([128, NHEAD, 64])
        rstd_bc = st[:, 4, :].unsqueeze(2).broadcast_to([128, NHEAD, 64])
        vsq3 = vsq_sb.rearrange("p (h d) -> p h d", h=NHEAD)
        lng3 = small_sb[:, l, C_LNG:C_LNG + 384].rearrange("p (h d) -> p h d", h=NHEAD)
        yield
        tr.op(DVE, lambda: nc.vector.tensor_tensor(out=v3, in0=v3, in1=mean_bc, op=ALU.subtract),
              reads=[stb[2]], writes=[vb])
        yield
        tr.op(POOL, lambda: nc.gpsimd.tensor_tensor(out=vsq3, in0=lng3, in1=rstd_bc, op=ALU.mult),
              reads=[stb[4], small_buf], writes=[vsq_buf])
        yield
        tr.op(DVE, lambda: nc.vector.tensor_tensor(out=vs, in0=vs, in1=vsq_sb, op=ALU.mult),
              reads=[vsq_buf], writes=[vb])
        yield
        tr.op(DVE, lambda: nc.vector.tensor_tensor(out=cur.vn_sb[c], in0=vs, in1=small_sb[:, l, C_LNB:C_LNB + 384], op=ALU.add),
              reads=[vb, small_buf], writes=[cur.vn_buf[c]])
        return

    def ln_chains(items, l):
        gens = [ln_chain_gen(pv_, pvb_, l, c_) for (pv_, pvb_, c_) in items]
        alive = list(gens)
        while alive:
            for g_ in list(alive):
                try:
                    next(g_)
                except StopIteration:
                    alive.remove(g_)

    def mix_chunk(c, r, l):
        pm, pmb = ps_alloc()
        for j in range(3):
            for e in range(2):
                hd = 2 * j + e
                o = pm[e * 64:(e + 1) * 64, j * 128:(j + 1) * 128]
                tr.mm(lambda o=o, hd=hd: nc.tensor.matmul(o, lhsT=cur.vn_sb[r][:, hd * 64:(hd + 1) * 64],
                                                          rhs=wt_sb[:, l, hd * 128:(hd + 1) * 128],
                                                          start=True, stop=False, skip_group_check=True),
                      reads=[cur.vn_buf[r], wt_buf], writes=[pmb])
                last = (j == 2 and e == 1)
                tr.mm(lambda o=o, hd=hd: nc.tensor.matmul(o, lhsT=sel_sb, rhs=bshl_sb[:, l, hd * 128:(hd + 1) * 128],
                                                          start=False, stop=True, skip_group_check=True),
                      reads=[bshl_buf, cbufs[1]], writes=[pmb], inc=last)
        mr = mixpar[0]
        mixpar[0] ^= 1
        tr.op(ACT, lambda: nc.scalar.activation(out=mxs_sb[mr], in_=pm[:, 0:384], func=AF.Copy), reads=[pmb], writes=[mxs_buf[mr]])
        tr.op(DVE, lambda: nc.vector.tensor_tensor(
            out=cur.ycat_sb[:, 0:3, c * 128:(c + 1) * 128],
            in0=mxs_sb[mr].rearrange("p (j t) -> p j t", j=3),
            in1=cur.u_sb[:, 0:3, c * 128:(c + 1) * 128], op=ALU.mult),
            reads=[mxs_buf[mr]] + cur.u_buf, writes=cur.ycat_buf[0:3])

    mixpar = [0]

    def load_x(ti, t0):
        xi_ = ti % NXB
        T_ = tiles[ti] * 128
        tr.dma(SP, sl_x[xi_], lambda: nc.sync.dma_start(
            out=x_sb[xi_][:, :, :T_], in_=xT[:, :, t0:t0 + T_].rearrange("k p t -> p k t")), writes=x_buf[xi_])

    def tile_gen(ti, pi, tok0):
        nch = tiles[ti]
        T = nch * 128
        xi = ti % NXB
        xs, xb = x_sb[xi], x_buf[xi]
        r = norm(xi, T, 0, C_GMIX)
        norm_apply_h(xi, T, 0, C_GMIX, r)
        for l in range(L):
            yield
            if cur.hs == 0:
                load_ple(pi, l)
            tr.dma(POOL, cur.sl_p, lambda: nc.gpsimd.dma_start(
                out=cur.p_sb[:, :, :T], in_=pT[l, :, :, tok0:tok0 + T].rearrange("k p t -> p k t")), writes=[cur.p_buf])
            wp, wb = get_piece(pi, l, 0)
            w4 = wp[:, 0:3072].rearrange("p (m k j) -> p m k j", m=3, k=KC)
            for m in range(3):
                pu, pub = ps_alloc()
                group(pu[:, :T], pub, [(w4[:, m, k, :], cur.h_sb[:, k, :T]) for k in range(KC)],
                      [[wb, cur.h_buf[k]] for k in range(KC)])
                tr.op(ACT, lambda m=m, pu=pu: nc.scalar.activation(out=cur.u_sb[:, m, :T], in_=pu[:, :T], func=AF.Gelu),
                      reads=[pub], writes=[cur.u_buf[m]])
            yield
            wp, wb = get_piece(pi, l, 1)
            wv = wp[:, 0:3072].rearrange("p (k j) -> p k j", k=KC)
            vn_idx = []
            pend_mix = []
            for c in range(nch):
                pv, pvb = ps_alloc()
                group(pv[:, 0:384], pvb, [(cur.h_sb[:, k, c * 128:(c + 1) * 128], wv[:, k, :]) for k in range(KC)],
                      [[wb, cur.h_buf[k]] for k in range(KC)])
                pend_mix.append((c, pv, pvb))
            ln_done = {}

            def do_ln(c):
                cc_, pv_, pvb_ = pend_mix[c]
                ln_chains([(pv_, pvb_, c)], l)
                ln_done[c] = c

            ln_chains([(pend_mix[c_][1], pend_mix[c_][2], c_) for c_ in range(min(nch, 2))], l)
            for c_ in range(min(nch, 2)):
                ln_done[c_] = c_
            yield
            wp, wb = get_piece(pi, l, 2)
            w4 = wp[:, 0:2048].rearrange("p (m k j) -> p m k j", m=2, k=KC)
            has_fix = (tok0 <= HALO * 128 < tok0 + T)
            q0 = 16 + HALO * 128 - tok0
            for cc in range(2):
                pz, pzb = ps_alloc()
                group(pz[:, :T], pzb, [(w4[:, cc, k, :], cur.h_sb[:, k, :T]) for k in range(KC)],
                      [[wb, cur.h_buf[k]] for k in range(KC)])
                zs, zb = zc_sb[cur.hs * 2 + cc], zc_buf[cur.hs * 2 + cc]
                tr.op(ACT, lambda: nc.scalar.activation(out=zs[:, 16:16 + T], in_=pz[:, :T], func=AF.Copy),
                      reads=[pzb], writes=[zb])
                tr.op(DVE, lambda cc=cc: nc.vector.tensor_copy(out=zs[:, 0:16], in_=zcar_sb[:, l, cc, :]),
                      reads=[zcar_buf[l][cc]], writes=[zb])
                tr.op(DVE, lambda cc=cc: nc.vector.tensor_copy(out=zcar_sb[:, l, cc, :], in_=zs[:, T:T + 16]),
                      reads=[zb], writes=[zcar_buf[l][cc]])
                W_ = 16 + T
                tr.op(POOL, lambda: nc.gpsimd.tensor_tensor(out=S_sb[0][:, 1:W_], in0=zs[:, 1:W_], in1=zs[:, 0:W_ - 1], op=ALU.add),
                      reads=[zb], writes=[S_buf[0]])
                if cc == 0:
                    tr.op(POOL, lambda: nc.gpsimd.tensor_tensor(out=S_sb[1][64:128, 3:W_], in0=S_sb[0][64:128, 3:W_],
                                                                in1=S_sb[0][64:128, 1:W_ - 2], op=ALU.add),
                          reads=[S_buf[0]], writes=[S_buf[1]])
                    srcs = [(0, slice(0, 64)), (1, slice(64, 128))]
                else:
                    tr.op(POOL, lambda: nc.gpsimd.tensor_tensor(out=S_sb[1][:, 3:W_], in0=S_sb[0][:, 3:W_],
                                                                in1=S_sb[0][:, 1:W_ - 2], op=ALU.add),
                          reads=[S_buf[0]], writes=[S_buf[1]])
                    tr.op(POOL, lambda: nc.gpsimd.tensor_tensor(out=S_sb[2][:, 7:W_], in0=S_sb[1][:, 7:W_],
                                                                in1=S_sb[1][:, 3:W_ - 4], op=ALU.add),
                          reads=[S_buf[1]], writes=[S_buf[2]])
                    tr.op(POOL, lambda: nc.gpsimd.tensor_tensor(out=S_sb[3][64:128, 15:W_], in0=S_sb[2][64:128, 15:W_],
                                                                in1=S_sb[2][64:128, 7:W_ - 8], op=ALU.add),
                          reads=[S_buf[2]], writes=[S_buf[3]])
                    srcs = [(2, slice(0, 64)), (3, slice(64, 128))]
                for si, psl in srcs:
                    if has_fix:
                        tr.op(POOL, lambda si=si, psl=psl, cc=cc: nc.gpsimd.tensor_tensor(
                            out=S_sb[si][psl, q0:q0 + 16], in0=S_sb[si][psl, q0:q0 + 16],
                            in1=small_sb[psl, l, C_CORR + cc * 16:C_CORR + (cc + 1) * 16], op=ALU.mult),
                            reads=[small_buf], writes=[S_buf[si]])
                    tr.op(POOL, lambda si=si, psl=psl, cc=cc: nc.gpsimd.tensor_scalar(
                        out=S_sb[si][psl, 16:16 + T], in0=S_sb[si][psl, 16:16 + T], scalar1=small_sb[psl, l, C_IW + cc:C_IW + cc + 1],
                        scalar2=0.0, op0=ALU.mult, op1=ALU.add),
                        reads=[small_buf], writes=[S_buf[si]])
                    tr.op(POOL, lambda si=si, psl=psl, cc=cc: nc.gpsimd.tensor_tensor(
                        out=cur.pl_sb[cc][psl, :T], in0=S_sb[si][psl, 16:16 + T], in1=zs[psl, 16:16 + T], op=ALU.subtract),
                        reads=[S_buf[si], zb], writes=[cur.pl_buf[cc]])
            for c in range(3):
                yield
                wp, wb = get_piece(pi, l, 3 + c)
                w4 = wp[:, 0:3072].rearrange("p (m k j) -> p m k j", m=3, k=KC)
                pss = []
                for m in range(3):
                    pz, pzb = ps_alloc()
                    group(pz[:, :T], pzb, [(w4[:, m, k, :], cur.h_sb[:, k, :T]) for k in range(KC)],
                          [[wb, cur.h_buf[k]] for k in range(KC)])
                    pss.append((pz, pzb))
                (pz, pzb), (pgc, pgcb), (pgb, pgbb) = pss
                rr = cvpar[0]
                cvpar[0] = (rr + 1) % 4
                cw = lambda k, c=c: small_sb[:, l, C_CW + c * 3 + k:C_CW + c * 3 + k + 1]
                tr.op(ACT, lambda: nc.scalar.activation(out=zbs_sb[rr][:, :T], in_=pz[:, :T], func=AF.Copy),
                      reads=[pzb], writes=[zbs_buf[rr]])
                tr.op(ACT, lambda: nc.scalar.activation(out=gcs_sb[rr][:, :T], in_=pgc[:, :T], func=AF.Copy),
                      reads=[pgcb], writes=[gcs_buf[rr]])
                tr.op(ACT, lambda: nc.scalar.activation(out=gbs_sb[rr][:, :T], in_=pgb[:, :T], func=AF.Copy),
                      reads=[pgbb], writes=[gbs_buf[rr]])
                tr.op(DVE, lambda: nc.vector.tensor_tensor(out=hb_sb[rr % 2][:, 2:2 + T], in0=zbs_sb[rr][:, :T], in1=gcs_sb[rr][:, :T],
                                                           op=ALU.mult), reads=[zbs_buf[rr], gcs_buf[rr]], writes=[hb_buf[rr % 2]])
                tr.op(DVE, lambda c=c: nc.vector.tensor_copy(out=hb_sb[rr % 2][:, 0:2], in_=ccar_sb[:, l, c, :]),
                      reads=[ccar_buf[l][c]], writes=[hb_buf[rr % 2]])
                tr.op(DVE, lambda: nc.vector.tensor_scalar(out=a0_sb[rr % 2][:, :T], in0=hb_sb[rr % 2][:, 2:2 + T], scalar1=cw(2), scalar2=None,
                                                           op0=ALU.mult), reads=[hb_buf[rr % 2], small_buf], writes=[a0_buf[rr % 2]])
                tr.op(DVE, lambda c=c: nc.vector.tensor_copy(out=ccar_sb[:, l, c, :], in_=hb_sb[rr % 2][:, T:T + 2]),
                      reads=[hb_buf[rr % 2]], writes=[ccar_buf[l][c]])
                tr.op(DVE, lambda: nc.vector.scalar_tensor_tensor(out=a0_sb[rr % 2][:, :T], in0=hb_sb[rr % 2][:, 1:1 + T], scalar=cw(1),
                                                                  in1=a0_sb[rr % 2][:, :T], op0=ALU.mult, op1=ALU.add),
                      reads=[hb_buf[rr % 2], small_buf], writes=[a0_buf[rr % 2]])
                tr.op(DVE, lambda: nc.vector.scalar_tensor_tensor(out=a0_sb[rr % 2][:, :T], in0=hb_sb[rr % 2][:, 0:T], scalar=cw(0),
                                                                  in1=a0_sb[rr % 2][:, :T], op0=ALU.mult, op1=ALU.add),
                      reads=[hb_buf[rr % 2], small_buf], writes=[a0_buf[rr % 2]])
                tr.op(DVE, lambda c=c: nc.vector.tensor_tensor(out=cur.ycat_sb[:, 3 + c, :T], in0=a0_sb[rr % 2][:, :T], in1=gbs_sb[rr][:, :T],
                                                               op=ALU.mult), reads=[a0_buf[rr % 2], gbs_buf[rr]], writes=[cur.ycat_buf[3 + c]])
                if c == 0:
                    for c2 in range(2, nch):
                        do_ln(c2)
                if c >= 1 and c - 1 < nch:
                    mix_chunk(c - 1, ln_done[c - 1], l)
                if c == 2:
                    for c2 in range(2, min(nch, 3)):
                        mix_chunk(c2, ln_done[c2], l)
            if nch > 3:
                mix_chunk(3, ln_done[3], l)
            korder = [3, 4, 5, 0, 1, 2, 6, 7]
            for i in range(2):
                yield
                if i == 0:
                    for cc in range(2):
                        pp, ppb = ps_alloc()
                        tr.mm(lambda cc=cc, pp=pp: nc.tensor.matmul(pp[:, :T], lhsT=wpb_sb[:, l, cc * 128:(cc + 1) * 128],
                                                                    rhs=cur.pl_sb[cc][:, :T], start=True, stop=True),
                              reads=[wpb_buf, cur.pl_buf[cc]], writes=[ppb], inc=True)
                        tr.op(ACT, lambda cc=cc, pp=pp: nc.scalar.activation(out=cur.ycat_sb[:, 6 + cc, :T], in_=pp[:, :T], func=AF.Copy,
                                                                             scale=small_sb[:, l, C_PSC + cc:C_PSC + cc + 1]),
                              reads=[ppb, small_buf], writes=[cur.ycat_buf[6 + cc]])
                wp, wb = get_piece(pi, l, 6 + i)
                w4 = wp.rearrange("p (m k j) -> p m k j", m=4, k=KC)
                for mi in range(4):
                    m = i * 4 + mi
                    po, pob = ps_alloc()
                    group(po[:, :T], pob, [(w4[:, mi, k, :], cur.ycat_sb[:, k, :T]) for k in korder],
                          [[wb, cur.ycat_buf[k]] for k in korder])
                    tr.op(DVE, lambda m=m, po=po: nc.vector.tensor_tensor(out=xs[:, m, :T], in0=xs[:, m, :T], in1=po[:, :T], op=ALU.add),
                          reads=[pob], writes=[xb[m]])
                    norm_sq(xi, T, m)
            r = norm(xi, T, l, C_GFF, skip_sq=True)
            norm_apply_h(xi, T, l, C_GFF, r)
            for i in range(8):
                yield
                wp, wb = get_piece(pi, l, 8 + i)
                w4 = wp.rearrange("p (m k j) -> p m k j", m=4, k=KC)
                for mi in range(4):
                    j = i * 4 + mi
                    pf, pfb = ps_alloc()
                    group(pf[:, :T], pfb, [(w4[:, mi, k, :], cur.h_sb[:, k, :T]) for k in range(KC)],
                          [[wb, cur.h_buf[k]] for k in range(KC)])
                    rr = j % 3
                    tr.op(ACT, lambda pf=pf, rr=rr: nc.scalar.activation(out=rl_sb[rr][:, :T], in_=pf[:, :T], func=AF.Relu),
                          reads=[pfb], writes=[rl_buf[rr]])
                    tr.op(POOL, lambda j=j, rr=rr: nc.gpsimd.tensor_tensor(out=cur.hid_sb[:, j, :T], in0=rl_sb[rr][:, :T], in1=rl_sb[rr][:, :T],
                                                                        op=ALU.mult), reads=[rl_buf[rr]], writes=[cur.hid_buf[j]])
            for m in range(8):
                yield
                wp, wb = get_piece(pi, l, 16 + m)
                w3 = wp.rearrange("p (k j) -> p k j", k=32)
                po, pob = ps_alloc()
                group(po[:, :T], pob, [(w3[:, k, :], cur.hid_sb[:, k, :T]) for k in range(32)],
                      [[wb, cur.hid_buf[k]] for k in range(32)])
                tr.op(DVE, lambda m=m, po=po: nc.vector.tensor_tensor(out=xs[:, m, :T], in0=xs[:, m, :T], in1=po[:, :T], op=ALU.add),
                      reads=[pob], writes=[xb[m]])
                norm_sq(xi, T, m)
            r = norm(xi, T, l, C_GPLE, skip_sq=True)
            norm_apply_h(xi, T, l, C_GPLE, r)
            wpl, wplb = wple_sb, wple_buf
            wpl4 = wpl[:, 0:2048].rearrange("p (m k j) -> p m k j", m=8, k=2)
            for i in range(2):
                yield
                wp, wb = get_piece(pi, l, 24 + i)
                w4 = wp.rearrange("p (m k j) -> p m k j", m=4, k=KC)
                for mi in range(4):
                    m = i * 4 + mi
                    pg, pgb_ = ps_alloc()
                    group(pg[:, :T], pgb_, [(w4[:, mi, k, :], cur.h_sb[:, k, :T]) for k in range(KC)],
                          [[wb, cur.h_buf[k]] for k in range(KC)])
                    pq, pqb = ps_alloc()
                    group(pq[:, :T], pqb, [(wpl4[:, m, k, :], cur.p_sb[:, k, :T]) for k in range(2)],
                          [[wplb, cur.p_buf] for k in range(2)])
                    rr = m % 2
                    tr.op(ACT, lambda pg=pg, rr=rr: nc.scalar.activation(out=th_sb[rr][:, :T], in_=pg[:, :T], func=AF.Tanh, scale=0.5),
                          reads=[pgb_], writes=[th_buf[rr]])
                    tr.op(DVE, lambda pq=pq, rr=rr: nc.vector.scalar_tensor_tensor(out=t1_sb[rr][:, :T], in0=th_sb[rr][:, :T], scalar=1.0,
                                                                                   in1=pq[:, :T], op0=ALU.add, op1=ALU.mult),
                          reads=[th_buf[rr], pqb], writes=[t1_buf[rr]])
                    tr.op(DVE, lambda m=m, rr=rr: nc.vector.scalar_tensor_tensor(out=xs[:, m, :T], in0=t1_sb[rr][:, :T], scalar=0.5,
                                                                                  in1=xs[:, m, :T], op0=ALU.mult, op1=ALU.add),
                          reads=[t1_buf[rr]], writes=[xb[m]])
            if l + 1 < L:
                r = norm(xi, T, l + 1, C_GMIX)
                norm_apply_h(xi, T, l + 1, C_GMIX, r)
        r = norm(xi, T, L - 1, C_GFIN)
        for k in range(KC):
            tr.op(DVE, lambda k=k: nc.vector.scalar_tensor_tensor(
                out=xs[:, k, :T], in0=xs[:, k, :T], scalar=small_sb[:, L - 1, C_GFIN + k:C_GFIN + k + 1],
                in1=ms_sb[r][:, :T], op0=ALU.mult, op1=ALU.mult),
                reads=[ms_buf[r], small_buf], writes=[xb[k]])
        lo = max(tok0, HALO * 128)
        hi = tok0 + T
        if hi > lo:
            tr.dma(SP, sl_o[xi], lambda: nc.sync.dma_start(
                out=outT[:, :, lo - HALO * 128:hi - HALO * 128].rearrange("k p t -> p k t"),
                in_=xs[:, :, lo - tok0:hi - tok0]), reads=xb)

    tok_starts = [0]
    for n_ in tiles:
        tok_starts.append(tok_starts[-1] + n_ * 128)
    pairs = [list(range(i, min(i + NHS, len(tiles)))) for i in range(0, len(tiles), NHS)]
    for t_ in pairs[0]:
        load_x(t_, tok_starts[t_])
    for pi, pr in enumerate(pairs):
        if pi + 1 < len(pairs):
            for t_ in pairs[pi + 1]:
                load_x(t_, tok_starts[t_])
        gens = [(tile_gen(t_, pi, tok_starts[t_]), halves[k_]) for k_, t_ in enumerate(pr)]
        alive = list(gens)
        while alive:
            for g_ in list(alive):
                cur.__dict__.update(g_[1].__dict__)
                try:
                    next(g_[0])
                except StopIteration:
                    alive.remove(g_)
    for s in sl_o:
        if s.cnt:
            nc.sync.wait_ge(s.sem, s.cnt)
    for s in sl_wb + [sl_wplewb]:
        if s.cnt:
            nc.sync.wait_ge(s.sem, s.cnt)
    return nc


def _default_tiles(nch):
    n = -(-nch // 2)
    base = nch // n
    rem = nch - base * n
    return [base + 1] * rem + [base] * (n - rem)


def kernel(x, p, norm_mix_g, w_in, sgu_w, sgu_b, sgu_ln_g, sgu_ln_b, conv_w, pool_w, pool_scale, w_out,
           norm_ff_g, w_ff1, w_ff2, norm_ple_g, w_ple_gate, w_ple_proj, final_g, _cfg=None):
    cfg = dict(CFG)
    if _cfg:
        cfg.update(_cfg)
    inp = dict(norm_mix_g=norm_mix_g, w_in=w_in, sgu_w=sgu_w, sgu_b=sgu_b, sgu_ln_g=sgu_ln_g, sgu_ln_b=sgu_ln_b,
               conv_w=conv_w, pool_w=pool_w, pool_scale=pool_scale, w_out=w_out, norm_ff_g=norm_ff_g, w_ff1=w_ff1,
               w_ff2=w_ff2, norm_ple_g=norm_ple_g, w_ple_gate=w_ple_gate, w_ple_proj=w_ple_proj, final_g=final_g)
    x = np.asarray(x, np.float32)
    p = np.asarray(p, np.float32)
    L = cfg["L"]
    B, S, _ = x.shape
    nseg = NCORES // B
    SEGC = S // 128 // nseg
    NCH = SEGC + HALO
    tiles = cfg["TILES"] or _default_tiles(NCH)
    wpk = _pack_weights(inp, L)
    bpk = _pack_big(inp, L)
    spk_first = _pack_small(inp, L, True)
    spk_rest = _pack_small(inp, L, False)
    in_maps = []
    for c in range(NCORES):
        b, sg = divmod(c, nseg)
        t0 = sg * SEGC * 128
        xs = np.zeros((NCH * 128, D), np.float32)
        ps = np.zeros((L, NCH * 128, DPLE), np.float32)
        if sg == 0:
            xs[HALO * 128:] = x[b, t0:t0 + SEGC * 128]
            ps[:, HALO * 128:] = p[:L, b, t0:t0 + SEGC * 128]
        else:
            xs[:] = x[b, t0 - HALO * 128:t0 + SEGC * 128]
            ps[:] = p[:L, b, t0 - HALO * 128:t0 + SEGC * 128]
        xTc = np.ascontiguousarray(xs.T).reshape(KC, 128, NCH * 128)
        pTc = np.ascontiguousarray(ps.transpose(0, 2, 1)).reshape(L, 2, 128, NCH * 128)
        in_maps.append({"xT": xTc, "pT": pTc, "wpk": wpk, "spk": spk_first if sg == 0 else spk_rest, "bpk": bpk})
    nc = build_nc(L, NCH, tiles, cfg["NBUF_W"])
    res = run_bass_kernel_spmd(nc, in_maps, core_ids=list(range(NCORES)))
    out = np.zeros((B, S, D), np.float32)
    for c in range(NCORES):
        b, sg = divmod(c, nseg)
        t0 = sg * SEGC * 128
        oT = np.asarray(res.results[c]["outT"]).reshape(D, SEGC * 128)
        out[b, t0:t0 + SEGC * 128] = oT.T
    return out
```
